# Optimizing a Trainium2 kernel written in Bass

```python
import math
import jax, jax.numpy as jnp
from jax import lax
import numpy as np

D_MODEL = 1024
BATCH = 8
SEQ = 4096
DEPTH = 1

N_MEM = 256
RMS_EPS = 1e-6
NEG_INF = -1e30

HG_EXPAND = 128
HG_HEADS = D_MODEL // HG_EXPAND
HG_DK = HG_EXPAND
HG_DV = D_MODEL // HG_HEADS
HG_WIDTH = HG_HEADS * HG_DV
HG_CHUNK = 64
HG_SCALE = HG_DK ** -0.5

DA_CONFIGS = ((128, 1), (512, 4), (2048, 16))
DA_HEADS_PER_GROUP = 4
DA_HEADS = DA_HEADS_PER_GROUP * len(DA_CONFIGS)
DA_HEAD_DIM = D_MODEL // 8
DA_QKV_WIDTH = DA_HEADS * DA_HEAD_DIM
DA_WIDTH = DA_HEADS_PER_GROUP * DA_HEAD_DIM
DA_SCALE = DA_HEAD_DIM ** -0.5

MEM_HEADS = 4
MEM_HEAD_DIM = D_MODEL // 8
MEM_WIDTH = MEM_HEADS * MEM_HEAD_DIM
MEM_SCALE = MEM_HEAD_DIM ** -0.5

D_FF = ((8 * D_MODEL // 3 + 255) // 256) * 256

IN_SPLITS = (HG_WIDTH,) * 5 + (DA_QKV_WIDTH,) * 3 + (MEM_WIDTH,) + (D_MODEL,) * 3
IN_COLS = sum(IN_SPLITS)
IN_SPLIT_POINTS = tuple(int(p) for p in np.cumsum(IN_SPLITS)[:-1])

kernel_name = "hybrid_hgrn2_dilated_memory_block"


def rmsnorm(x, gain):
    xf = x.astype(jnp.float32)
    y = xf * lax.rsqrt(jnp.mean(xf * xf, axis=-1, keepdims=True) + RMS_EPS)
    return (y * gain.astype(jnp.float32)).astype(x.dtype)


def alibi_slopes(n):
    return (2.0 ** (-8.0 * np.arange(1, n + 1) / n)).astype(np.float32)


def gla_chunkwise(q, k, v, log_f):
    B, H, L, dk = q.shape
    dv = v.shape[-1]
    C = HG_CHUNK
    n = L // C
    q, k, log_f = [t.reshape(B, H, n, C, dk) for t in (q, k, log_f)]
    v = v.reshape(B, H, n, C, dv)
    b = jnp.cumsum(log_f.astype(jnp.float32), axis=3)
    b_last = b[:, :, :, -1:, :]
    q_in = q * jnp.exp(b)
    k_in = k * jnp.exp(-b)
    k_st = k * jnp.exp(b_last - b)
    causal = jnp.tril(jnp.ones((C, C), dtype=bool))
    a = jnp.where(causal, jnp.einsum('bhnti,bhnsi->bhnts', q_in, k_in), 0.0)
    o_intra = jnp.einsum('bhnts,bhnsj->bhntj', a, v)
    chunk_state = jnp.einsum('bhnsi,bhnsj->bhnij', k_st, v)
    decay = jnp.exp(b_last[:, :, :, 0, :])

    def step(S, inp):
        d, s = inp
        return d[..., None] * S + s, S

    S0 = jnp.zeros((B, H, dk, dv), jnp.float32)
    _, S_in = lax.scan(step, S0, (jnp.moveaxis(decay, 2, 0), jnp.moveaxis(chunk_state, 2, 0)))
    S_in = jnp.moveaxis(S_in, 0, 2)
    o_inter = jnp.einsum('bhnti,bhnij->bhntj', q_in, S_in)
    return (o_intra + o_inter).reshape(B, H, L, dv)


def hgrn2_mixer(q, f_fw, f_bw, inp, gate, lb_fw, lb_bw, norm_gain):
    B, L, _ = q.shape

    def to_heads(t):
        return t.reshape(B, L, HG_HEADS, -1).transpose(0, 2, 1, 3)

    qh = to_heads(jax.nn.silu(q)) * HG_SCALE
    vh = to_heads(inp)

    def direction(f_logit, lb, flip):
        forget = lb + (1.0 - lb) * jax.nn.sigmoid(f_logit.astype(jnp.float32))
        kh, logf = to_heads(1.0 - forget), to_heads(jnp.log(forget))
        if flip:
            rev = lambda t: jnp.flip(t, axis=2)
            return rev(gla_chunkwise(rev(qh), rev(kh), rev(vh), rev(logf)))
        return gla_chunkwise(qh, kh, vh, logf)

    o = direction(f_fw, lb_fw, False) + direction(f_bw, lb_bw, True)
    o = o.transpose(0, 2, 1, 3)
    o = rmsnorm(o, norm_gain) * jax.nn.silu(gate.reshape(B, L, HG_HEADS, HG_DV).astype(jnp.float32))
    return o.reshape(B, L, HG_WIDTH).astype(q.dtype)


def dilated_group(q, k, v, dilation, radius, slopes):
    B, L, Hg, dh = q.shape
    d, P = dilation, radius
    Ld = L // d
    nb = -(-Ld // P)
    Lp = nb * P

    def residues(t):
        return t.reshape(B, Ld, d, Hg, dh).transpose(0, 3, 2, 1, 4)

    qr, kr, vr = residues(q), residues(k), residues(v)
    qb = jnp.pad(qr, ((0, 0),) * 3 + ((0, Lp - Ld), (0, 0))).reshape(B, Hg, d, nb, P, dh)
    kv_pad = ((0, 0),) * 3 + ((P, Lp - Ld + P), (0, 0))

    def key_blocks(t):
        tb = jnp.pad(t, kv_pad).reshape(B, Hg, d, nb + 2, P, dh)
        return jnp.concatenate([tb[:, :, :, :-2], tb[:, :, :, 1:-1], tb[:, :, :, 2:]], axis=4)

    kb, vb = key_blocks(kr), key_blocks(vr)
    qi = jnp.arange(P)[:, None]
    kj = jnp.arange(3 * P)[None, :]
    rel = kj - P - qi
    s_pos = jnp.arange(nb)[:, None, None] * P + kj[None] - P
    valid = (jnp.abs(rel) <= radius)[None] & (s_pos >= 0) & (s_pos < Ld)
    dist = (d * jnp.abs(rel)).astype(jnp.float32)
    slopes = jnp.asarray(slopes, jnp.float32)

    scores = jnp.einsum('bhrnid,bhrnjd->bhrnij', qb, kb).astype(jnp.float32) * DA_SCALE
    scores = scores - slopes[:, None, None, None, None] * dist
    scores = jnp.where(valid, scores, NEG_INF)
    lse = jax.nn.logsumexp(scores, axis=-1)
    p = jnp.exp(scores - lse[..., None])
    o = jnp.einsum('bhrnij,bhrnjd->bhrnid', p.astype(vb.dtype), vb)
    o = o.reshape(B, Hg, d, Lp, dh)[:, :, :, :Ld]
    o = o.transpose(0, 3, 2, 1, 4).reshape(B, L, Hg, dh)
    lse = lse.reshape(B, Hg, d, Lp)[:, :, :, :Ld].transpose(0, 3, 2, 1).reshape(B, L, Hg)
    return o, lse


def dilated_mixer(q, k, v, q_gain, k_gain):
    B, L, _ = q.shape
    q = rmsnorm(q.reshape(B, L, DA_HEADS, DA_HEAD_DIM), q_gain)
    k = rmsnorm(k.reshape(B, L, DA_HEADS, DA_HEAD_DIM), k_gain)
    v = v.reshape(B, L, DA_HEADS, DA_HEAD_DIM)
    slopes = alibi_slopes(DA_HEADS)
    outs, lses = [], []
    for g, (window, dilation) in enumerate(DA_CONFIGS):
        lo, hi = g * DA_HEADS_PER_GROUP, (g + 1) * DA_HEADS_PER_GROUP
        o, lse = dilated_group(q[:, :, lo:hi], k[:, :, lo:hi], v[:, :, lo:hi],
                               dilation, window // (2 * dilation), slopes[lo:hi])
        outs.append(o)
        lses.append(lse)
    w = jax.nn.softmax(jnp.stack(lses, axis=0), axis=0)
    o = jnp.sum(w[..., None] * jnp.stack(outs, axis=0).astype(jnp.float32), axis=0)
    return o.reshape(B, L, DA_WIDTH).astype(q.dtype)


def memory_mixer(q, mem_n, w_kv, q_gain, k_gain):
    B, L, _ = q.shape
    M = mem_n.shape[1]
    k, v = jnp.split(mem_n @ w_kv, 2, axis=-1)
    qh = rmsnorm(q.reshape(B, L, MEM_HEADS, MEM_HEAD_DIM), q_gain)
    kh = rmsnorm(k.reshape(B, M, MEM_HEADS, MEM_HEAD_DIM), k_gain)
    vh = v.reshape(B, M, MEM_HEADS, MEM_HEAD_DIM)
    s = jnp.einsum('blhd,bmhd->bhlm', qh, kh).astype(jnp.float32) * MEM_SCALE
    p = jax.nn.softmax(s, axis=-1)
    o = jnp.einsum('bhlm,bmhd->blhd', p.astype(vh.dtype), vh)
    return o.reshape(B, L, MEM_WIDTH)


def setup_inputs(seed: int = 0) -> dict:
    key = jax.random.key(seed)
    ks = jax.random.split(key, 22)

    def w(k, shape, fan_in):
        return jax.random.normal(k, shape, jnp.float32) * fan_in ** -0.5

    def gain(k, shape):
        return 1.0 + 0.02 * jax.random.normal(k, shape, jnp.float32)

    return {
        "x": jax.random.normal(ks[0], (BATCH, SEQ, D_MODEL), jnp.float32),
        "mem": jax.random.normal(ks[1], (BATCH, N_MEM, D_MODEL), jnp.float32),
        "norm_mix_gain": gain(ks[2], (DEPTH, D_MODEL)),
        "norm_mem_gain": gain(ks[3], (DEPTH, D_MODEL)),
        "w_in": w(ks[4], (DEPTH, D_MODEL, IN_COLS), D_MODEL),
        "lb_logits_fw": 0.1 * jax.random.normal(ks[5], (DEPTH + 1, HG_WIDTH), jnp.float32),
        "lb_logits_bw": 0.1 * jax.random.normal(ks[6], (DEPTH + 1, HG_WIDTH), jnp.float32),
        "hg_norm_gain": gain(ks[7], (DEPTH, HG_DV)),
        "da_q_gain": gain(ks[8], (DEPTH, DA_HEAD_DIM)),
        "da_k_gain": gain(ks[9], (DEPTH, DA_HEAD_DIM)),
        "w_mem_kv": w(ks[10], (DEPTH, D_MODEL, 2 * MEM_WIDTH), D_MODEL),
        "mem_q_gain": gain(ks[11], (DEPTH, MEM_HEAD_DIM)),
        "mem_k_gain": gain(ks[12], (DEPTH, MEM_HEAD_DIM)),
        "w_proj_hg": w(ks[13], (DEPTH, HG_WIDTH, D_MODEL), HG_WIDTH),
        "w_proj_da": w(ks[14], (DEPTH, DA_WIDTH, D_MODEL), DA_WIDTH),
        "w_proj_mem": w(ks[15], (DEPTH, MEM_WIDTH, D_MODEL), MEM_WIDTH),
        "w_out": w(ks[16], (DEPTH, D_MODEL, D_MODEL), D_MODEL),
        "norm_ffn_gain": gain(ks[17], (DEPTH, D_MODEL)),
        "w_ffn_in": w(ks[18], (DEPTH, D_MODEL, 2 * D_FF), D_MODEL),
        "w_ffn_out": w(ks[19], (DEPTH, D_FF, D_MODEL), D_FF),
    }


def reference(x, mem, norm_mix_gain, norm_mem_gain, w_in, lb_logits_fw, lb_logits_bw,
              hg_norm_gain, da_q_gain, da_k_gain, w_mem_kv, mem_q_gain, mem_k_gain,
              w_proj_hg, w_proj_da, w_proj_mem, w_out, norm_ffn_gain, w_ffn_in, w_ffn_out):
    lb_fw_table = jnp.cumsum(jax.nn.softmax(lb_logits_fw.astype(jnp.float32), axis=0), axis=0)
    lb_bw_table = jnp.cumsum(jax.nn.softmax(lb_logits_bw.astype(jnp.float32), axis=0), axis=0)
    for l in range(DEPTH):
        h = rmsnorm(x, norm_mix_gain[l])
        proj = h @ w_in[l]
        (hg_q, hg_f_fw, hg_f_bw, hg_i, hg_g, da_q, da_k, da_v, mem_q,
         gate_hg, gate_da, gate_mem) = jnp.split(proj, IN_SPLIT_POINTS, axis=-1)

        o_hg = hgrn2_mixer(hg_q, hg_f_fw, hg_f_bw, hg_i, hg_g,
                           lb_fw_table[l], lb_bw_table[l], hg_norm_gain[l])
        o_da = dilated_mixer(da_q, da_k, da_v, da_q_gain[l], da_k_gain[l])
        mem_n = rmsnorm(mem, norm_mem_gain[l])
        o_mem = memory_mixer(mem_q, mem_n, w_mem_kv[l], mem_q_gain[l], mem_k_gain[l])

        merged = (jax.nn.sigmoid(gate_hg) * (o_hg @ w_proj_hg[l])
                  + jax.nn.sigmoid(gate_da) * (o_da @ w_proj_da[l])
                  + jax.nn.sigmoid(gate_mem) * (o_mem @ w_proj_mem[l]))
        x = x + merged @ w_out[l]

        h = rmsnorm(x, norm_ffn_gain[l])
        a, b = jnp.split(h @ w_ffn_in[l], 2, axis=-1)
        x = x + (jax.nn.silu(a) * b) @ w_ffn_out[l]
    return x
```

```python
import numpy as np
from contextlib import ExitStack
import concourse.bass as bass
import concourse.mybir as mybir
from concourse.bass_utils import run_bass_kernel_spmd

F32 = mybir.dt.float32
BF16 = mybir.dt.bfloat16
AF = mybir.ActivationFunctionType
ALU = mybir.AluOpType
AX = mybir.AxisListType

P = 128
L = 4096
D = 1024
KC = 8
NT = L // P
NMEM = 256
DFF = 2816
EPS = 1e-6
IN_COLS = 13312
OFF_HG = 0
OFF_DA = 5120
OFF_MEMQ = 9728
OFF_GATE = 10240
DA_CFG = ((128, 1), (512, 4), (2048, 16))
SLOPES = (2.0 ** (-8.0 * np.arange(1, 13) / 12)).astype(np.float64)


class Res:
    __slots__ = ("name", "w", "rd")

    def __init__(self, name):
        self.name = name
        self.w = None
        self.rd = {}


class Sched:
    EPOCH = 30000

    def __init__(self, nc, es):
        self.nc = nc
        self.es = es
        self.h = {"pe": nc.tensor, "act": nc.scalar, "dve": nc.vector, "pool": nc.gpsimd, "sp": nc.sync}
        self.E = {n: dict(sems=[], count=0, waited={}) for n in self.h}
        self.dsem = {}
        self.nsem = 0
        self.nops = 0

    def _newsem(self, name):
        self.nsem += 1
        return self.es.enter_context(self.nc.semaphore(name))

    def _next_tag(self, e):
        E = self.E[e]
        c = E["count"] + 1
        ep = (c - 1) // self.EPOCH
        while len(E["sems"]) <= ep:
            E["sems"].append(self._newsem(f"s_{e}_{len(E['sems'])}"))
        return (E["sems"][ep], c - ep * self.EPOCH, e, (e, ep))

    def _waits(self, e, reads, writes, is_dma):
        need = {}

        def add(tag, kind):
            sem, val, pe, key = tag
            if pe == e and not is_dma:
                if e == "pe":
                    return
                if kind != "raw":
                    return
            if key not in need or need[key][1] < val:
                need[key] = (sem, val)

        for r in reads:
            if r.w is not None:
                add(r.w, "raw")
        for w in writes:
            if w.w is not None:
                add(w.w, "waw")
            for t in w.rd.values():
                add(t, "war")
        E = self.E[e]
        for key, (sem, val) in need.items():
            if E["waited"].get(key, 0) >= val:
                continue
            E["waited"][key] = val
            self.h[e].wait_ge(sem, val)

    def _commit(self, tag, reads, writes):
        for w in writes:
            w.w = tag
            w.rd = {}
        for r in reads:
            k = tag[3]
            if k not in r.rd or r.rd[k][1] < tag[1]:
                r.rd[k] = tag

    def op(self, e, fn, R=(), W=(), signal=True):
        self._waits(e, R, W, False)
        tag = self._next_tag(e)
        ins = fn(self.h[e])
        if signal:
            ins.then_inc(tag[0], 1)
            self.E[e]["count"] += 1
        self._commit(tag, R, W)
        self.nops += 1
        return ins

    def dma(self, q, out, in_, R, W, key):
        self._waits(q, R, W, True)
        if key not in self.dsem:
            self.dsem[key] = [self._newsem(f"d_{len(self.dsem)}"), 0]
        ds = self.dsem[key]
        ds[1] += 16
        tag = (ds[0], ds[1], "dma", ("dma", key))
        self.h[q].dma_start(out=out, in_=in_).then_inc(ds[0], 16)
        self._commit(tag, R, W)
        self.nops += 1

    def barrier(self):
        tags = []
        for e, E in self.E.items():
            if E["count"] > 0:
                c = E["count"]
                ep = (c - 1) // self.EPOCH
                tags.append((E["sems"][ep], c - ep * self.EPOCH, (e, ep)))
        for key, ds in self.dsem.items():
            if ds[1] > 0:
                tags.append((ds[0], ds[1], ("dma", key)))
        for e, E in self.E.items():
            for sem, val, key in tags:
                if key[0] == e and e == "pe":
                    continue
                if E["waited"].get(key, 0) >= val:
                    continue
                E["waited"][key] = val
                self.h[e].wait_ge(sem, val)


def build(debug=False, phases=("mem", "da", "hg", "tail"), debug_slots=None, debug_heads=None):
    nc = bass.Bass("TRN2", target_bir_lowering=False)
    es = ExitStack()
    S = Sched(nc, es)

    def din(name, shape, dt=F32):
        return nc.dram_tensor(name, list(shape), dt, kind="ExternalInput").ap()

    x_d = din("x", [L, D])
    mem_d = din("mem", [NMEM, D])
    w_in_d = din("w_in", [D, IN_COLS])
    w_kv_d = din("w_mem_kv", [D, D])
    w_phg_d = din("w_proj_hg", [1024, D])
    w_pda_d = din("w_proj_da", [512, D])
    w_pmem_d = din("w_proj_mem", [512, D])
    w_out_d = din("w_out", [D, D])
    w_fin_d = din("w_ffn_in", [D, 2 * DFF])
    w_fout_d = din("w_ffn_out", [DFF, D])
    gains_d = din("gainsT", [P, 3, KC])
    hgains_d = din("hgains", [P, 5])
    lbl_d = din("lbl", [P, 4, 8])
    cmask_d = din("cmask", [P, 2, P])
    damask_d = din("damask", [P, 12, 2 * P])
    rmask_d = din("rmask", [P, 1024])
    ident_d = din("ident", [P, P])
    out_d = nc.dram_tensor("out", [L, D], F32, kind="ExternalOutput").ap()
    mix_kind = "ExternalOutput" if debug else "Internal"
    mixT_d = nc.dram_tensor("mixT", [2048, L], BF16, kind=mix_kind).ap()
    h2T_d = nc.dram_tensor("h2T", [D, L], BF16, kind="Internal").ap()
    gate_d = nc.dram_tensor("gateT", [3072, L], BF16, kind="Internal").ap()

    def sb(st, name, shape, dt, side=None):
        if side is None:
            return st.enter_context(nc.sbuf_tensor(name, list(shape), dt))
        return st.enter_context(nc.sbuf_tensor(name, list(shape), dt, side=side))

    hT_r = [Res(f"hT{n}") for n in range(NT)]
    ident = sb(es, "identb", [P, P], BF16)
    ones = sb(es, "onesb", [P, P], BF16)
    cmask = sb(es, "cmaskb", [P, 2, P], BF16)
    rmask = sb(es, "rmaskf", [P, 1024], F32)
    gainsT = sb(es, "gainsT_sb", [P, 3, KC], F32)
    hgains = sb(es, "hgains_sb", [P, 5], F32)
    hgs = sb(es, "hgs", [P, 5], F32)
    lbl = sb(es, "lbl_sb", [P, 4, 8], F32)
    lbv = sb(es, "lbv", [P, 2, 8], F32)
    oml = sb(es, "oml", [P, 2, 8], F32)
    noml = sb(es, "noml", [P, 2, 8], F32)
    const_r = Res("const")
    psum = [es.enter_context(nc.psum_tensor(f"ps{i}", [P, 512], F32)) for i in range(8)]
    ps_r = [Res(f"ps{i}") for i in range(8)]
    ps_i = [0]

    ps_c = [0] * 8

    def PS(chain=None, nch=2):
        if chain is None:
            i = ps_i[0] % 8
            ps_i[0] += 1
        else:
            w = 8 // nch
            i = w * chain + ps_c[chain] % w
            ps_c[chain] += 1
        return psum[i], ps_r[i]

    def run_chains(gens, stagger=0):
        gens = [(i_, g_) for i_, g_ in enumerate(gens)]
        rnd = 0
        while gens:
            for i_, g_ in list(gens):
                if rnd < i_ * stagger:
                    continue
                try:
                    next(g_)
                except StopIteration:
                    gens.remove((i_, g_))
            rnd += 1

    SQ128 = float(np.sqrt(128.0))
    HG_STAG, P0_STAG = 0, 1
    NWST, WCAP = 3, 1024
    wst = [sb(es, f"wst{i}", [P, WCAP], F32) for i in range(NWST)]
    hstack = ExitStack()
    hT = sb(hstack, "hT", [P, KC, L], BF16, side="right")

    with ExitStack() as ph:
        cst = sb(ph, "cstage", [P, 2 * P], F32)
        cst_r = Res("cstage")

        def load_const(dram, dst, n, cast):
            flat_d = dram if len(dram.shape) == 2 else (
                dram.rearrange("p a b -> p (a b)"))
            if cast:
                S.dma("sp", cst[:, 0:n], flat_d, [], [cst_r], "cstage")
                dflat = dst[:] if len(dst.shape) == 2 else dst[:].rearrange("p a b -> p (a b)")
                S.op("dve", lambda e: e.tensor_copy(out=dflat, in_=cst[:, 0:n]), [cst_r], [const_r])
            else:
                dflat = dst[:] if len(dst.shape) == 2 else dst[:].rearrange("p a b -> p (a b)")
                S.dma("sp", dflat, flat_d, [], [const_r], "const_" + dst.name)

        load_const(ident_d, ident, P, True)
        load_const(cmask_d, cmask, 2 * P, True)
        S.dma("sp", rmask[:], rmask_d[:, 0:1024], [], [const_r], "const_rmask")
        load_const(gains_d, gainsT, 3 * KC, False)
        load_const(hgains_d, hgains, 5, False)
        load_const(lbl_d, lbl, 32, False)
        S.op("dve", lambda e: e.memset(ones[:], 1.0), [], [const_r])
        S.op("dve", lambda e: e.tensor_scalar(out=hgs[:], in0=hgains[:], scalar1=1.0 / SQ128, scalar2=None,
                                              op0=ALU.mult), [const_r], [const_r])
        for d_ in range(2):
            S.op("dve", lambda e, d_=d_: e.tensor_sub(out=lbv[:, d_, :], in0=lbl[:, 2 * d_, :],
                                                      in1=lbl[:, 2 * d_ + 1, :]), [const_r], [const_r])
        S.op("act", lambda e: e.activation(out=lbv[:], in_=lbv[:], func=AF.Sigmoid), [const_r], [const_r])
        S.op("dve", lambda e: e.tensor_scalar(out=oml[:], in0=lbv[:], scalar1=-1.0, scalar2=1.0,
                                              op0=ALU.mult, op1=ALU.add), [const_r], [const_r])
        S.op("dve", lambda e: e.tensor_scalar(out=noml[:], in0=oml[:], scalar1=-1.0, scalar2=None,
                                              op0=ALU.mult), [const_r], [const_r])
        S.barrier()

    wst_r = [Res(f"wst{i}") for i in range(NWST)]
    wst_i = [0]

    def wload(dst, dst_r, w2d, col0, ncols, kc0, nkc, gain_idx=None, dst_kc0=0, dst_col0=0, eng="dve"):
        assert nkc * ncols <= WCAP
        i = wst_i[0] % len(wst)
        wst_i[0] += 1
        st = wst[i][:, 0:nkc * ncols].rearrange("p (c n) -> p c n", c=nkc)
        src = w2d[kc0 * P:(kc0 + nkc) * P, col0:col0 + ncols].rearrange("(c p) n -> p c n", p=P)
        S.dma("sp", st, src, [], [wst_r[i]], f"wst{i}")
        o = dst[:, dst_kc0:dst_kc0 + nkc, dst_col0:dst_col0 + ncols]
        if gain_idx is None:
            S.op(eng, lambda e: e.tensor_copy(out=o, in_=st), [wst_r[i]], [dst_r])
        else:
            g = gainsT[:, gain_idx, kc0:kc0 + nkc].unsqueeze(2).to_broadcast([P, nkc, ncols])
            S.op(eng, lambda e: e.tensor_tensor(out=o, in0=st, in1=g, op=ALU.mult),
                 [wst_r[i], const_r], [dst_r])

    def wload_big(dst, dst_r, w2d, col0, ncols, nkc_total, gain_idx=None, dst_kc0=0, dst_col0=0, eng="dve"):
        for cc in range(0, ncols, 1024):
            nc_ = min(1024, ncols - cc)
            step = max(1, WCAP // nc_)
            for k0 in range(0, nkc_total, step):
                wload(dst, dst_r, w2d, col0 + cc, nc_, k0, min(step, nkc_total - k0), gain_idx,
                      dst_kc0=dst_kc0 + k0, dst_col0=dst_col0 + cc, eng=eng)

    def norm_transpose(ph, tag, src_rows, ntiles, dstT, dst_res, dma_key, NCH=2):
        xin = [sb(ph, f"{tag}_xin{i}", [P, D], F32) for i in range(NCH)]
        xin_r = [Res(f"{tag}_xin{i}") for i in range(NCH)]
        junk = [sb(ph, f"{tag}_junk{i}", [P, D], BF16) for i in range(NCH)]
        junk_r = [Res(f"junk{i}") for i in range(NCH)]
        hb = [sb(ph, f"{tag}_hb{i}", [P, D], BF16) for i in range(NCH)]
        hb_r = [Res(f"{tag}_hb{i}") for i in range(NCH)]
        ss = sb(ph, f"{tag}_ss", [P, 2 * ntiles], F32)
        ss_r = [Res(f"{tag}_ss{i}") for i in range(ntiles)]

        def chain(i):
            for n in range(i, ntiles, NCH):
                S.dma("sp", xin[i][:], src_rows(n), [], [xin_r[i]], f"{dma_key}{i}")
                yield
                S.op("act", lambda e: e.activation(out=junk[i][:], in_=xin[i][:], func=AF.Square,
                                                   accum_out=ss[:, 2 * n:2 * n + 1]), [xin_r[i]], [junk_r[i], ss_r[n]])
                yield
                S.op("act", lambda e: e.activation(out=ss[:, 2 * n + 1:2 * n + 2], in_=ss[:, 2 * n:2 * n + 1],
                                                   func=AF.Sqrt, bias=float(EPS), scale=1.0 / D), [ss_r[n]], [ss_r[n]])
                yield
                S.op("dve", lambda e: e.reciprocal(out=ss[:, 2 * n + 1:2 * n + 2], in_=ss[:, 2 * n + 1:2 * n + 2]),
                     [ss_r[n]], [ss_r[n]])
                yield
                S.op("dve", lambda e: e.tensor_scalar(out=hb[i][:], in0=xin[i][:], scalar1=ss[:, 2 * n + 1:2 * n + 2],
                                                      scalar2=None, op0=ALU.mult), [xin_r[i], ss_r[n]], [hb_r[i]])
                yield
                pt, pr = PS(i, NCH)
                ptb = pt[:].bitcast(BF16)
                for c in range(KC):
                    S.op("pe", lambda e: e.transpose(out=ptb[:, c * P:(c + 1) * P], in_=hb[i][:, c * P:(c + 1) * P],
                                                     identity=ident[:]), [hb_r[i], const_r], [pr], signal=(c == KC - 1))
                yield
                S.op("dve" if i % 2 == 0 else "act", (lambda e: e.tensor_copy(
                    out=dstT[:, :, n * P:(n + 1) * P], in_=ptb.rearrange("p (c t) -> p c t", c=KC))) if i % 2 == 0 else (
                    lambda e: e.activation(out=dstT[:, :, n * P:(n + 1) * P],
                                           in_=ptb.rearrange("p (c t) -> p c t", c=KC), func=AF.Copy)),
                     [pr], [dst_res[n]])
                yield

        run_chains([chain(c_) for c_ in range(NCH)], stagger=P0_STAG)

    memw = ExitStack()
    mem_pre = {}
    if "mem" in phases:
        mem_pre["wkv"] = (sb(memw, "wkv", [P, KC, D], BF16), Res("wkv"))
        mem_pre["wq"] = (sb(memw, "wq_mem", [P, KC, 512], BF16), Res("wq_mem"))
    with ExitStack() as ph:
        norm_transpose(ph, "p0", lambda n: x_d[n * P:(n + 1) * P, :], NT, hT, hT_r, "p0x", NCH=4)
        if "mem" in phases:
            wload_big(mem_pre["wkv"][0], mem_pre["wkv"][1], w_kv_d, 0, D, KC, gain_idx=1)
            wload_big(mem_pre["wq"][0], mem_pre["wq"][1], w_in_d, OFF_MEMQ, 512, KC, gain_idx=0)
        S.barrier()

    def hT_res(t0, nt):
        return hT_r[t0 // P:(t0 + nt + P - 1) // P]

    def proj_fm(ps_ap, pr, wb, wb_r, j0, ncol, t0, nt, tstep=1):
        for c in range(KC):
            rhs = hT[:, c, t0:t0 + (nt - 1) * tstep + 1:tstep] if tstep > 1 else hT[:, c, t0:t0 + nt]
            S.op("pe", lambda e, c=c, rhs=rhs: e.matmul(ps_ap, lhsT=wb[:, c, j0:j0 + ncol], rhs=rhs,
                                                        start=(c == 0), stop=(c == KC - 1)),
                 [wb_r] + (hT_r if tstep > 1 else hT_res(t0, nt)), [pr], signal=(c == KC - 1))

    qk_cnt = [0]

    def qk_norm_g(bufs, src_ps, src_r, n, gain_col, extra_scale, dst_ap, dst_r, chain, nch=2):
        sqb, sqb_r, rs, rs_r = bufs
        S.op("act", lambda e: e.activation(out=sqb[:, 0:n], in_=src_ps, func=AF.Square), [src_r], [sqb_r])
        yield
        p2, p2r = PS(chain, nch)
        S.op("pe", lambda e: e.matmul(p2[:, 0:n], lhsT=ones[:], rhs=sqb[:, 0:n], start=True, stop=True),
             [sqb_r, const_r], [p2r])
        yield
        S.op("act", lambda e: e.activation(out=rs[:, 0:n], in_=p2[:, 0:n], func=AF.Ln, bias=float(EPS),
                                           scale=1.0 / P), [p2r], [rs_r])
        yield
        S.op("act", lambda e: e.activation(out=rs[:, 0:n], in_=rs[:, 0:n], func=AF.Exp, scale=-0.5), [rs_r], [rs_r])
        yield
        gcol = (hgs if extra_scale == "qscale" else hgains)[:, gain_col:gain_col + 1]
        S.op("dve", lambda e: e.scalar_tensor_tensor(out=dst_ap, in0=src_ps, scalar=gcol, in1=rs[:, 0:n],
                                                     op0=ALU.mult, op1=ALU.mult),
             [src_r, rs_r, const_r], [dst_r])
        yield

    def qk_norm(ph_bufs, src_ps, src_r, n, gain_col, extra_scale, dst_ap, dst_r):
        sqb, sqb_r, rs, rs_r = ph_bufs[qk_cnt[0] % len(ph_bufs)]
        qk_cnt[0] += 1
        S.op("act", lambda e: e.activation(out=sqb[:, 0:n], in_=src_ps, func=AF.Square), [src_r], [sqb_r])
        p2, p2r = PS()
        S.op("pe", lambda e: e.matmul(p2[:, 0:n], lhsT=ones[:], rhs=sqb[:, 0:n], start=True, stop=True),
             [sqb_r, const_r], [p2r])
        S.op("act", lambda e: e.activation(out=rs[:, 0:n], in_=p2[:, 0:n], func=AF.Ln, bias=float(EPS),
                                           scale=1.0 / P), [p2r], [rs_r])
        S.op("act", lambda e: e.activation(out=rs[:, 0:n], in_=rs[:, 0:n], func=AF.Exp, scale=-0.5), [rs_r], [rs_r])
        gcol = (hgs if extra_scale == "qscale" else hgains)[:, gain_col:gain_col + 1]
        S.op("dve", lambda e: e.scalar_tensor_tensor(out=dst_ap, in0=src_ps, scalar=gcol, in1=rs[:, 0:n],
                                                     op0=ALU.mult, op1=ALU.mult),
             [src_r, rs_r, const_r], [dst_r])

    def phase_mem():
        with ExitStack() as ph:
            mnT = sb(ph, "mnT", [P, KC, NMEM], BF16)
            mnT_r = [Res("mnT0"), Res("mnT1")]
            norm_transpose(ph, "pm", lambda n: mem_d[n * P:(n + 1) * P, :], 2, mnT, mnT_r, "pmx")
            wkv, wkv_r = mem_pre["wkv"]
            wq, wq_r = mem_pre["wq"]
            khT = sb(ph, "khT_mem", [P, 4, NMEM], BF16)
            khT_r = Res("khT_mem")
            vm = sb(ph, "v_mem", [P, 2, 512], BF16)
            vm_r = Res("v_mem")
            nb = [(sb(ph, f"sqb_mem{i}", [P, 512], BF16), Res(f"sqb{i}"), sb(ph, f"rs_mem{i}", [P, 512], F32),
                   Res(f"rs{i}")) for i in range(4)]
            for hd in range(4):
                pk, pkr = PS()
                for c in range(KC):
                    S.op("pe", lambda e, c=c, hd=hd: e.matmul(pk[:, 0:NMEM], lhsT=wkv[:, c, hd * P:(hd + 1) * P],
                                                               rhs=mnT[:, c, :], start=(c == 0), stop=(c == KC - 1)),
                         [wkv_r] + mnT_r, [pkr], signal=(c == KC - 1))
                qk_norm(nb, pk[:, 0:NMEM], pkr, NMEM, 4, None, khT[:, hd, :], khT_r)
            for mt in range(2):
                pv, pvr = PS()
                for c in range(KC):
                    S.op("pe", lambda e, c=c, mt=mt: e.matmul(pv[:, :], lhsT=mnT[:, c, mt * P:(mt + 1) * P],
                                                               rhs=wkv[:, c, 512:1024], start=(c == 0),
                                                               stop=(c == KC - 1)),
                         [wkv_r] + mnT_r, [pvr], signal=(c == KC - 1))
                S.op("act", lambda e, mt=mt: e.activation(out=vm[:, mt, :], in_=pv[:, :], func=AF.Copy), [pvr], [vm_r])
            qh = [[sb(ph, f"qh_mem{c}_{i}", [P, 512], BF16) for i in range(2)] for c in range(4)]
            qh_r = [[Res(f"qh{c}_{i}") for i in range(2)] for c in range(4)]
            pT = [[sb(ph, f"pT_mem{c}_{i}", [P, 2, 512], BF16) for i in range(2)] for c in range(4)]
            pT_r = [[Res(f"pT{c}_{i}") for i in range(2)] for c in range(4)]
            rz = [sb(ph, f"rz_mem{c}", [P, 512], F32) for c in range(4)]
            rz_r = [Res(f"rz_mem{c}") for c in range(4)]
            ost = [[sb(ph, f"ost_mem{c}_{i}", [P, 512], BF16) for i in range(2)] for c in range(4)]
            ost_r = [[Res(f"ost{c}_{i}") for i in range(2)] for c in range(4)]

            def mem_chain(c):
                it = 0
                for blk in range(L // 512):
                    t0 = blk * 512
                    ob = blk % 2
                    for h2 in range(1):
                        hd = c
                        b = it % 2
                        it += 1
                        pq, pqr = PS(c, 4)
                        proj_fm(pq[:, :], pqr, wq, wq_r, hd * P, P, t0, 512)
                        yield
                        yield from qk_norm_g(nb[c], pq[:, :], pqr, 512, 3, "qscale", qh[c][b][:], qh_r[c][b], c, 4)
                        for mt in range(2):
                            p_s, p_sr = PS(c, 4)
                            S.op("pe", lambda e: e.matmul(p_s[:, :], lhsT=khT[:, hd, mt * P:(mt + 1) * P],
                                                          rhs=qh[c][b][:], start=True, stop=True),
                                 [khT_r, qh_r[c][b]], [p_sr])
                            yield
                            S.op("act", lambda e: e.activation(out=pT[c][b][:, mt, :], in_=p_s[:, :], func=AF.Exp),
                                 [p_sr], [pT_r[c][b]])
                            yield
                        pu, pur = PS(c, 4)
                        pz, pzr = PS(c, 4)
                        for mt in range(2):
                            S.op("pe", lambda e: e.matmul(pu[:, :], lhsT=vm[:, mt, hd * P:(hd + 1) * P],
                                                          rhs=pT[c][b][:, mt, :], start=(mt == 0), stop=(mt == 1)),
                                 [vm_r, pT_r[c][b]], [pur], signal=(mt == 1))
                        for mt in range(2):
                            S.op("pe", lambda e: e.matmul(pz[:, :], lhsT=ones[:], rhs=pT[c][b][:, mt, :],
                                                          start=(mt == 0), stop=(mt == 1)),
                                 [const_r, pT_r[c][b]], [pzr], signal=(mt == 1))
                        yield
                        S.op("act", lambda e: e.activation(out=rz[c][:], in_=pz[:, :], func=AF.Ln), [pzr], [rz_r[c]])
                        yield
                        S.op("act", lambda e: e.activation(out=rz[c][:], in_=rz[c][:], func=AF.Exp, scale=-1.0),
                             [rz_r[c]], [rz_r[c]])
                        yield
                        S.op("dve", lambda e: e.tensor_tensor(out=ost[c][ob][:], in0=pu[:, :], in1=rz[c][:],
                                                              op=ALU.mult), [pur, rz_r[c]], [ost_r[c][ob]])
                        yield
                    dst = mixT_d[1536 + c * P:1536 + (c + 1) * P, t0:t0 + 512]
                    S.dma("pool", dst, ost[c][ob][:], [ost_r[c][ob]], [mix_r[12 + c]], f"ost_mem{c}_{ob}")
                    yield

            run_chains([mem_chain(c_) for c_ in range(4)], stagger=3)
            S.barrier()

    mix_r = [Res(f"mix{i}") for i in range(16)]

    def phase_da(slots=range(4), pre_barrier=None):
        with ExitStack() as ph:
            damask = sb(ph, "damaskb", [P, 12, 2 * P], BF16)
            dst_ = sb(ph, "dastage", [P, 24 * P], F32)
            dst_r = Res("dastage")
            damask_r = Res("damask")
            S.dma("sp", dst_[:], damask_d.rearrange("p a b -> p (a b)"), [], [dst_r], "dastage")
            S.op("dve", lambda e: e.tensor_copy(out=damask[:].rearrange("p a b -> p (a b)"), in_=dst_[:]),
                 [dst_r], [damask_r])
            wqkv = [[sb(ph, f"w_da{k}_{i}", [P, KC, P], BF16) for k in range(3)] for i in range(2)]
            wqkv_r = [[Res(f"w_da{k}_{i}") for k in range(3)] for i in range(2)]
            qhT = sb(ph, "qhT_da", [P, L], BF16)
            khT = sb(ph, "khT_da", [P, L], BF16)
            qk_r = [Res("qhT_da"), Res("khT_da")]
            vtm = sb(ph, "vtm_da", [P, NT, P], BF16)
            vtm_r = Res("vtm_da")
            uz = sb(ph, "uz_da", [P, 2, L], F32)
            uz_r = Res("uz_da")
            nb_ = [(sb(ph, f"sqb_da{i}", [P, 512], BF16), Res(f"sqb_da{i}"), sb(ph, f"rs_da{i}", [P, 512], F32),
                    Res(f"rs_da{i}")) for i in range(3)]
            pex = [sb(ph, f"pex_da{i}", [P, 2, 2, P], BF16) for i in range(2)]
            pex_r = [Res("pex0"), Res("pex1")]
            pm = [sb(ph, f"pm_da{i}", [P, 2, 2, P], BF16) for i in range(2)]
            pm_r = [Res("pm0"), Res("pm1")]
            rz = sb(ph, "rz_da", [P, 512], F32)
            rz_r = Res("rz_da")
            ost = [sb(ph, f"ost_da{i}", [P, 512], BF16) for i in range(2)]
            ost_r = [Res("ost_da0"), Res("ost_da1")]
            hcount = 0
            for slot in slots:
                for g in range(3):
                    head = g * 4 + slot
                    d = DA_CFG[g][1]
                    Ld = L // d
                    nb = Ld // P
                    wi = hcount % 2
                    hcount += 1
                    if hcount == 1:
                        for k in range(3):
                            wload(wqkv[wi][k], wqkv_r[wi][k], w_in_d, OFF_DA + k * 1536 + head * P, P, 0, KC, gain_idx=0)
                    nxt = hcount
                    slots_l = list(slots)
                    if nxt < 3 * len(slots_l):
                        nhead = (nxt % 3) * 4 + slots_l[nxt // 3]
                        for k in range(3):
                            wload(wqkv[nxt % 2][k], wqkv_r[nxt % 2][k], w_in_d, OFF_DA + k * 1536 + nhead * P, P, 0, KC,
                                  gain_idx=0)
                    wq, wk, wv = wqkv[wi]
                    wq_r, wk_r, wv_r = wqkv_r[wi]
                    items = []
                    for blk in range(L // 512):
                        items.append((blk * 512, wq, wq_r, 1, "qscale", qhT, qk_r[0]))
                        items.append((blk * 512, wk, wk_r, 2, None, khT, qk_r[1]))
                    live = {}

                    def qkA(j):
                        t0, w_, w_r_, gcol_i, esc, dstT_, dres = items[j]
                        sqb, sqb_r, rs, rs_r = nb_[j % len(nb_)]
                        pq, pqr = PS()
                        proj_fm(pq[:, :], pqr, w_, w_r_, 0, P, t0, 512)
                        S.op("act", lambda e: e.activation(out=sqb[:, :], in_=pq[:, :], func=AF.Square), [pqr], [sqb_r])
                        live[j] = (pq, pqr)

                    def qkB(j):
                        t0, w_, w_r_, gcol_i, esc, dstT_, dres = items[j]
                        sqb, sqb_r, rs, rs_r = nb_[j % len(nb_)]
                        pq, pqr = live.pop(j)
                        p2, p2r = PS()
                        S.op("pe", lambda e: e.matmul(p2[:, :], lhsT=ones[:], rhs=sqb[:, :], start=True, stop=True),
                             [sqb_r, const_r], [p2r])
                        S.op("act", lambda e: e.activation(out=rs[:, :], in_=p2[:, :], func=AF.Ln, bias=float(EPS),
                                                           scale=1.0 / P), [p2r], [rs_r])
                        S.op("act", lambda e: e.activation(out=rs[:, :], in_=rs[:, :], func=AF.Exp, scale=-0.5),
                             [rs_r], [rs_r])
                        gcol = (hgs if esc == "qscale" else hgains)[:, gcol_i:gcol_i + 1]
                        S.op("dve", lambda e: e.scalar_tensor_tensor(out=dstT_[:, t0:t0 + 512], in0=pq[:, :], scalar=gcol,
                                                                     in1=rs[:, :], op0=ALU.mult, op1=ALU.mult),
                             [pqr, rs_r, const_r], [dres])

                    qkA(0)
                    for j in range(len(items)):
                        if j + 1 < len(items):
                            qkA(j + 1)
                        qkB(j)
                    for bi0 in range(0, NT, 4):
                        pv, pvr = PS()
                        for j in range(4):
                            bi = bi0 + j
                            r, kb = bi // nb, bi % nb
                            tk0 = r + d * kb * P
                            for c in range(KC):
                                lhs = hT[:, c, tk0:tk0 + (P - 1) * d + 1:d]
                                S.op("pe", lambda e, c=c, j=j, lhs=lhs, pv=pv: e.matmul(
                                    pv[:, j * P:(j + 1) * P], lhsT=lhs, rhs=wv[:, c, :], start=(c == 0),
                                    stop=(c == KC - 1)), [wv_r] + hT_r, [pvr], signal=(c == KC - 1 and j == 3))
                        S.op("act", lambda e, bi0=bi0, pv=pv: e.activation(
                            out=vtm[:, bi0:bi0 + 4, :], in_=pv[:, :].rearrange("p (j v) -> p j v", j=4), func=AF.Copy),
                            [pvr], [vtm_r])
                    tiles = []
                    for r in range(d):
                        tiles.append((r, 0, 1))
                        i = 1
                        while i < nb:
                            if i + 1 < nb:
                                tiles.append((r, i, 2))
                                i += 2
                            else:
                                tiles.append((r, i, 1))
                                i += 1
                        tiles.append((r, nb, 1))
                    staged = {}

                    def stage1(idx):
                        r, i, nt_ = tiles[idx]
                        sl = idx % 2
                        p_s, p_sr = PS()
                        if nt_ == 2:
                            tq0 = r + d * (P * i - 64)
                            for tau in range(2):
                                qsl = qhT[:, tq0 + tau * P * d:tq0 + tau * P * d + (P - 1) * d + 1:d]
                                for bb in range(2):
                                    kb = i + tau - 1 + bb
                                    tk0 = r + d * kb * P
                                    ksl = khT[:, tk0:tk0 + (P - 1) * d + 1:d]
                                    S.op("pe", lambda e: e.matmul(
                                        p_s[:, tau * 2 * P + bb * P:tau * 2 * P + (bb + 1) * P], lhsT=ksl, rhs=qsl,
                                        start=True, stop=True), qk_r, [p_sr], signal=(tau == 1 and bb == 1))
                            S.op("act", lambda e: e.activation(out=pex[sl][:].rearrange("p t b a -> p (t b a)"),
                                                               in_=p_s[:, :], func=AF.Exp), [p_sr], [pex_r[sl]])
                            mk = damask[:, head, :].unsqueeze(1).to_broadcast([P, 2, 2 * P])
                            S.op("dve", lambda e: e.tensor_tensor(
                                out=pm[sl][:].rearrange("p t b a -> p t (b a)"),
                                in0=pex[sl][:].rearrange("p t b a -> p t (b a)"), in1=mk, op=ALU.mult),
                                [pex_r[sl], damask_r], [pm_r[sl]])
                            staged[idx] = (r, i, 2, None, tq0, 0, 2, sl)
                            return
                        a0 = 64 if i == 0 else 0
                        a1 = 64 if i == nb else P
                        nq = a1 - a0
                        tq0 = r + d * (P * i - 64 + a0)
                        qsl = qhT[:, tq0:tq0 + (nq - 1) * d + 1:d]
                        b0, b1 = (1 if i == 0 else 0), (1 if i == nb else 2)
                        for bb in range(b0, b1):
                            kb = i - 1 + bb
                            tk0 = r + d * kb * P
                            ksl = khT[:, tk0:tk0 + (P - 1) * d + 1:d]
                            S.op("pe", lambda e, bb=bb, ksl=ksl: e.matmul(
                                p_s[:, bb * P:bb * P + nq], lhsT=ksl, rhs=qsl, start=True, stop=True),
                                qk_r, [p_sr], signal=(bb == b1 - 1))
                        psv = p_s[:, 0:2 * P].rearrange("p (b a) -> p b a", b=2)[:, b0:b1, 0:nq]
                        S.op("act", lambda e: e.activation(out=pex[sl][:, 0, b0:b1, 0:nq], in_=psv, func=AF.Exp),
                             [p_sr], [pex_r[sl]])
                        mk = damask[:, head, :].rearrange("p (b a) -> p b a", b=2)[:, b0:b1, a0:a1]
                        S.op("dve", lambda e: e.tensor_tensor(out=pm[sl][:, 0, b0:b1, 0:nq],
                                                              in0=pex[sl][:, 0, b0:b1, 0:nq],
                                                              in1=mk, op=ALU.mult), [pex_r[sl], damask_r], [pm_r[sl]])
                        staged[idx] = (r, i, 1, nq, tq0, b0, b1, sl)

                    def stage2(idx):
                        r, i, nt_, nq, tq0, b0, b1, sl = staged.pop(idx)
                        pu, pur = PS()
                        if nt_ == 2:
                            for tau in range(2):
                                for bb in range(2):
                                    kb = i + tau - 1 + bb
                                    S.op("pe", lambda e: e.matmul(
                                        pu[:, tau * P:(tau + 1) * P], lhsT=vtm[:, r * nb + kb, :], rhs=pm[sl][:, tau, bb, :],
                                        start=(bb == 0), stop=(bb == 1)), [vtm_r, pm_r[sl]], [pur], signal=False)
                            for tau in range(2):
                                for bb in range(2):
                                    S.op("pe", lambda e: e.matmul(
                                        pu[:, 2 * P + tau * P:2 * P + (tau + 1) * P], lhsT=ones[:], rhs=pm[sl][:, tau, bb, :],
                                        start=(bb == 0), stop=(bb == 1)), [const_r, pm_r[sl]], [pur],
                                        signal=(tau == 1 and bb == 1))
                            puv = pu[:, :].rearrange("p (z a) -> p z a", z=2)
                            uzv = uz[:, :, tq0:tq0 + (2 * P - 1) * d + 1:d]
                        else:
                            for bb in range(b0, b1):
                                kb = i - 1 + bb
                                S.op("pe", lambda e, bb=bb, kb=kb: e.matmul(
                                    pu[:, 0:nq], lhsT=vtm[:, r * nb + kb, :], rhs=pm[sl][:, 0, bb, 0:nq],
                                    start=(bb == b0), stop=(bb == b1 - 1)), [vtm_r, pm_r[sl]], [pur], signal=False)
                            for bb in range(b0, b1):
                                S.op("pe", lambda e, bb=bb: e.matmul(
                                    pu[:, P:P + nq], lhsT=ones[:], rhs=pm[sl][:, 0, bb, 0:nq],
                                    start=(bb == b0), stop=(bb == b1 - 1)), [const_r, pm_r[sl]], [pur],
                                    signal=(bb == b1 - 1))
                            puv = pu[:, 0:2 * P].rearrange("p (b a) -> p b a", b=2)[:, :, 0:nq]
                            uzv = uz[:, :, tq0:tq0 + (nq - 1) * d + 1:d]
                        if g == 0:
                            S.op("act", lambda e: e.activation(out=uzv, in_=puv, func=AF.Copy), [pur], [uz_r])
                        else:
                            S.op("dve", lambda e: e.tensor_tensor(out=uzv, in0=puv, in1=uzv, op=ALU.add),
                                 [pur, uz_r], [uz_r])

                    for idx in range(len(tiles)):
                        stage1(idx)
                        if idx >= 1:
                            stage2(idx - 1)
                    stage2(len(tiles) - 1)
                for blk in range(L // 512):
                    t0 = blk * 512
                    ob = blk % 2
                    S.op("act", lambda e, t0=t0: e.activation(out=rz[:], in_=uz[:, 1, t0:t0 + 512], func=AF.Ln),
                         [uz_r], [rz_r])
                    S.op("act", lambda e: e.activation(out=rz[:], in_=rz[:], func=AF.Exp, scale=-1.0), [rz_r], [rz_r])
                    S.op("dve", lambda e, t0=t0, ob=ob: e.tensor_tensor(out=ost[ob][:], in0=uz[:, 0, t0:t0 + 512],
                                                                        in1=rz[:], op=ALU.mult),
                         [uz_r, rz_r], [ost_r[ob]])
                    S.dma("pool", mixT_d[1024 + slot * P:1024 + (slot + 1) * P, t0:t0 + 512], ost[ob][:],
                          [ost_r[ob]], [mix_r[8 + slot]], f"ost_da{ob}")
            if pre_barrier is not None:
                pre_barrier()
            S.barrier()

    HG_SCALE = float(128 ** -0.5)

    def phase_hg(heads=range(8), wts=None, wts_r=None, first_loaded=False):
        with ExitStack() as ph:
            sqT = sb(ph, "sqT_hg", [P, L], BF16)
            sqT_r = Res("sqT")
            gT = sb(ph, "gT_hg", [P, L], BF16)
            gT_r = Res("gT")
            vtm = sb(ph, "vtm_hg", [P, NT, P], BF16)
            vtm_r = Res("vtm_hg")
            qin = [sb(ph, f"qin_hg{i}", [P, L], BF16) for i in range(2)]
            kin = [sb(ph, f"kin_hg{i}", [P, L], BF16) for i in range(2)]
            qin_r = [Res("qin0"), Res("qin1")]
            kin_r = [Res("kin0"), Res("kin1")]
            SEG = 1024
            T1s = [sb(ph, f"T1_hg{i}", [P, 1 + SEG], F32) for i in range(2)]
            T1s_r = [Res("T1_0"), Res("T1_1")]
            T2s = [sb(ph, f"T2_hg{i}", [P, SEG], F32) for i in range(2)]
            T2s_r = [Res("T2_0"), Res("T2_1")]
            K1s = [sb(ph, f"K1_hg{i}", [P, SEG], BF16) for i in range(2)]
            K1s_r = [Res("K1_0"), Res("K1_1")]
            E1s = [sb(ph, f"E1_hg{i}", [P, SEG], BF16) for i in range(2)]
            E1s_r = [Res("E1_0"), Res("E1_1")]
            E2s, E2s_r = E1s, E1s_r
            st = [sb(ph, f"st_hg{i}", [P, 6, NT], F32) for i in range(2)]
            st_r = [Res("st0"), Res("st1")]
            oacc = sb(ph, "oacc_hg", [P, L], BF16)
            oacc_r = [Res(f"oacc{n}") for n in range(NT)]
            Sst = [[sb(ph, f"S_hg{i}_{j}", [P, P], F32) for j in range(2)] for i in range(2)]
            Sst_r = [[Res(f"S{i}_{j}") for j in range(2)] for i in range(2)]
            Sbf = [[sb(ph, f"Sbf_hg{i}_{j}", [P, P], BF16) for j in range(2)] for i in range(2)]
            Sbf_r = [[Res(f"Sbf{i}_{j}") for j in range(2)] for i in range(2)]
            AT2 = [sb(ph, f"AT2_hg{j}", [P, 2, P], BF16) for j in range(3)]
            AT_r2 = [Res(f"AT2_{j}") for j in range(3)]
            ktm2 = [sb(ph, f"ktm2_hg{j}", [P, 2, P], BF16) for j in range(3)]
            ktm_r2 = [Res(f"ktm2_{j}") for j in range(3)]
            sqbs = [sb(ph, f"sqb_hg{i}", [P, 512], BF16) for i in range(2)]
            sqbs_r = [Res("sqb_hg0"), Res("sqb_hg1")]
            sgs, sgs_r = sqbs, sqbs_r
            rss = [sb(ph, f"rs_hg{i}", [P, 512], F32) for i in range(2)]
            rss_r = [Res("rs_hg0"), Res("rs_hg1")]
            tmpns = [sb(ph, f"tmpn_hg{i}", [P, 512], BF16) for i in range(2)]
            tmpns_r = [Res("tmpn0"), Res("tmpn1")]
            ost = [sb(ph, f"ost_hg{i}", [P, 512], BF16) for i in range(2)]
            ost_r = [Res("ost_hg0"), Res("ost_hg1")]
            for i_ in range(2):
                S.op("dve", lambda e, i_=i_: e.memset(T1s[i_][:, 0:1], 0.0), [], [T1s_r[i_]])
            segc = [0]
            heads = list(heads)

            def loadw(hi_):
                for k in range(5):
                    wload(wts[hi_ % 2][k], wts_r[hi_ % 2][k], w_in_d, OFF_HG + k * 1024 + heads[hi_] * P, P, 0, KC,
                          gain_idx=0)

            if not first_loaded:
                loadw(0)
            for hi, h in enumerate(heads):
                wi = hi % 2
                if hi + 1 < len(heads):
                    loadw(hi + 1)
                W_, W_r = wts[wi], wts_r[wi]
                for blk in range(L // 512):
                    t0 = blk * 512
                    pq, pqr = PS()
                    proj_fm(pq[:, :], pqr, W_[0], W_r[0], 0, P, t0, 512)
                    sgi = blk % 2
                    S.op("act", lambda e, pq=pq, sgi=sgi: e.activation(out=sgs[sgi][:], in_=pq[:, :], func=AF.Sigmoid),
                         [pqr], [sgs_r[sgi]])
                    S.op("dve", lambda e, t0=t0, pq=pq, sgi=sgi: e.tensor_tensor(
                        out=sqT[:, t0:t0 + 512], in0=pq[:, :], in1=sgs[sgi][:], op=ALU.mult),
                        [pqr, sgs_r[sgi]], [sqT_r])

                def gv_chain():
                    for n0 in range(0, NT, 4):
                        pv, pvr = PS(2, 4)
                        for j in range(4):
                            for c in range(KC):
                                S.op("pe", lambda e: e.matmul(
                                    pv[:, j * P:(j + 1) * P], lhsT=hT[:, c, (n0 + j) * P:(n0 + j + 1) * P],
                                    rhs=W_[3][:, c, :], start=(c == 0), stop=(c == KC - 1)),
                                    [W_r[3]] + hT_r[n0:n0 + 4], [pvr], signal=(c == KC - 1 and j == 3))
                        yield
                        S.op("dve", lambda e: e.tensor_copy(
                            out=vtm[:, n0:n0 + 4, :], in_=pv[:, :].rearrange("p (j v) -> p j v", j=4)),
                            [pvr], [vtm_r])
                        yield
                        t0 = (n0 // 4) * 512
                        pg, pgr = PS(3, 4)
                        proj_fm(pg[:, :], pgr, W_[4], W_r[4], 0, P, t0, 512)
                        yield
                        sgi = (n0 // 4) % 2
                        S.op("act", lambda e: e.activation(out=sgs[sgi][:], in_=pg[:, :], func=AF.Sigmoid),
                             [pgr], [sgs_r[sgi]])
                        yield
                        S.op("dve", lambda e: e.tensor_tensor(out=gT[:, t0:t0 + 512], in0=pg[:, :], in1=sgs[sgi][:],
                                                              op=ALU.mult), [pgr, sgs_r[sgi]], [gT_r])
                        yield

                def pre_chain(dr):
                    sgn = 1.0 if dr == 0 else -1.0
                    lbc = lbv[:, dr, h:h + 1]
                    omc = oml[:, dr, h:h + 1]
                    nomc = noml[:, dr, h:h + 1]
                    T1, T1_r, T2, T2_r = T1s[dr], T1s_r[dr], T2s[dr], T2s_r[dr]
                    K1, K1_r, E1, E1_r, E2, E2_r = K1s[dr], K1s_r[dr], E1s[dr], E1s_r[dr], E2s[dr], E2s_r[dr]
                    T1d = T1[:, 1:1 + SEG]
                    NTS = SEG // P
                    for seg in range(L // SEG):
                        s0 = seg * SEG
                        for b2 in range(SEG // 512):
                            pf, pfr = PS(dr, 4)
                            proj_fm(pf[:, :], pfr, W_[1 + dr], W_r[1 + dr], 0, P, s0 + b2 * 512, 512)
                            yield
                            S.op("act", lambda e: e.activation(out=T1[:, 1 + b2 * 512:1 + (b2 + 1) * 512], in_=pf[:, :],
                                                               func=AF.Sigmoid), [pfr], [T1_r])
                            yield
                        S.op("dve", lambda e: e.tensor_scalar(out=K1[:], in0=T1d, scalar1=nomc, scalar2=omc,
                                                              op0=ALU.mult, op1=ALU.add), [T1_r, const_r], [K1_r])
                        S.op("dve", lambda e: e.tensor_scalar(out=T1d, in0=T1d, scalar1=omc, scalar2=lbc,
                                                              op0=ALU.mult, op1=ALU.add), [T1_r, const_r], [T1_r])
                        yield
                        S.op("act", lambda e: e.activation(out=T1d, in_=T1d, func=AF.Ln), [T1_r], [T1_r])
                        yield
                        if dr == 0:
                            S.op("dve", lambda e: e.tensor_tensor_scan(out=T2[:], data0=rmask[:, 0:SEG], data1=T1d,
                                                                       initial=0.0, op0=ALU.mult, op1=ALU.add),
                                 [T1_r, const_r], [T2_r])
                        else:
                            S.op("dve", lambda e: e.tensor_tensor_scan(out=T2[:], data0=T1[:, 0:SEG],
                                                                       data1=rmask[:, 0:SEG],
                                                                       initial=0.0, op0=ALU.add, op1=ALU.mult),
                                 [T1_r, const_r], [T2_r])
                        yield
                        T2v = T2[:].rearrange("p (n t) -> p n t", t=P)
                        T1v = T1d.rearrange("p (n t) -> p n t", t=P)
                        ns = slice(seg * NTS, (seg + 1) * NTS)
                        mid_i = 63 if dr == 0 else 64
                        S.op("dve", lambda e: e.tensor_copy(out=st[dr][:, 0, ns], in_=T2v[:, :, mid_i]),
                             [T2_r], [st_r[dr]])
                        if dr == 0:
                            S.op("dve", lambda e: e.tensor_copy(out=st[dr][:, 1, ns], in_=T2v[:, :, P - 1]),
                                 [T2_r], [st_r[dr]])
                        else:
                            S.op("dve", lambda e: e.tensor_tensor(out=st[dr][:, 1, ns], in0=T2v[:, :, P - 1],
                                                                  in1=T1v[:, :, P - 1], op=ALU.add),
                                 [T2_r, T1_r], [st_r[dr]])
                        yield
                        S.op("dve", lambda e: e.tensor_tensor(
                            out=T2v, in0=T2v, in1=st[dr][:, 0, ns].unsqueeze(2).to_broadcast([P, NTS, P]),
                            op=ALU.subtract), [T2_r, st_r[dr]], [T2_r])
                        yield
                        S.op("act", lambda e: e.activation(out=E1[:], in_=T2[:], func=AF.Exp, scale=sgn), [T2_r], [E1_r])
                        yield
                        S.op("dve", lambda e: e.scalar_tensor_tensor(
                            out=qin[dr][:, s0:s0 + SEG], in0=sqT[:, s0:s0 + SEG], scalar=HG_SCALE, in1=E1[:],
                            op0=ALU.mult, op1=ALU.mult), [sqT_r, E1_r], [qin_r[dr]])
                        yield
                        S.op("act", lambda e: e.activation(out=E2[:], in_=T2[:], func=AF.Exp, scale=-sgn), [T2_r], [E2_r])
                        yield
                        S.op("dve", lambda e: e.tensor_tensor(out=kin[dr][:, s0:s0 + SEG], in0=K1[:], in1=E2[:],
                                                              op=ALU.mult), [K1_r, E2_r], [kin_r[dr]])
                        yield
                    S.op("dve", lambda e: e.tensor_sub(out=st[dr][:, 2, :], in0=st[dr][:, 1, :], in1=st[dr][:, 0, :]),
                         [st_r[dr]], [st_r[dr]])
                    src_ws, src_wcs = (0, 2) if dr == 0 else (2, 0)
                    S.op("act", lambda e: e.activation(out=st[dr][:, 3, :], in_=st[dr][:, src_ws, :], func=AF.Exp),
                         [st_r[dr]], [st_r[dr]])
                    S.op("act", lambda e: e.activation(out=st[dr][:, 4, :], in_=st[dr][:, src_wcs, :], func=AF.Exp),
                         [st_r[dr]], [st_r[dr]])
                    S.op("act", lambda e: e.activation(out=st[dr][:, 5, :], in_=st[dr][:, 1, :], func=AF.Exp),
                         [st_r[dr]], [st_r[dr]])
                    yield

                run_chains([pre_chain(0), pre_chain(1), gv_chain()], stagger=HG_STAG)
                touched = set()

                pMs = {}
                oacc3 = oacc[:].rearrange("p (n t) -> p n t", t=P)

                def tile_of(step, dr):
                    return step if dr == 0 else NT - 1 - step

                def stageA1(step):
                    sl = step % 3
                    last = step == NT - 1
                    pA, pAr = PS()
                    for dr in range(2):
                        n = tile_of(step, dr)
                        ts_ = slice(n * P, (n + 1) * P)
                        S.op("pe", lambda e: e.matmul(pA[:, dr * P:(dr + 1) * P], lhsT=kin[dr][:, ts_], rhs=qin[dr][:, ts_],
                                                      start=True, stop=True), [kin_r[dr], qin_r[dr]], [pAr],
                             signal=(dr == 1))
                    S.op("dve", lambda e: e.tensor_tensor(out=AT2[sl][:], in0=pA[:, 0:2 * P].rearrange("p (d t) -> p d t", d=2),
                                                          in1=cmask[:], op=ALU.mult), [pAr, const_r], [AT_r2[sl]])
                    pT, pTr = PS()
                    if not last:
                        pTb = pT[:].bitcast(BF16)
                        for dr in range(2):
                            n = tile_of(step, dr)
                            ts_ = slice(n * P, (n + 1) * P)
                            S.op("pe", lambda e: e.transpose(out=pTb[:, dr * P:(dr + 1) * P], in_=kin[dr][:, ts_],
                                                             identity=ident[:]), [kin_r[dr], const_r], [pTr],
                                 signal=(dr == 1))
                        S.op("act", lambda e: e.activation(out=ktm2[sl][:],
                                                           in_=pTb[:, 0:2 * P].rearrange("p (d t) -> p d t", d=2),
                                                           func=AF.Copy), [pTr], [ktm_r2[sl]])

                def stageA2(step):
                    sl = step % 3
                    last = step == NT - 1
                    pM, pMr = PS()
                    if not last:
                        for dr in range(2):
                            n = tile_of(step, dr)
                            S.op("pe", lambda e: e.matmul(pM[:, dr * P:(dr + 1) * P], lhsT=ktm2[sl][:, dr, :], rhs=vtm[:, n, :],
                                                          start=True, stop=True), [ktm_r2[sl], vtm_r], [pMr],
                                 signal=(dr == 1))
                        pMs[step] = (pM, pMr)

                def stageB1(step):
                    sb_ = step % 2
                    first, last = step == 0, step == NT - 1
                    if not last:
                        pM, pMr = pMs.pop(step)
                    for dr in range(2):
                        n = tile_of(step, dr)
                        so, sn = (step - 1) % 2, step % 2
                        if not first:
                            S.op("act", lambda e: e.activation(out=Sbf[dr][sb_][:], in_=Sst[dr][so][:], func=AF.Copy,
                                                               scale=st[dr][:, 3, n:n + 1]),
                                 [Sst_r[dr][so], st_r[dr]], [Sbf_r[dr][sb_]])
                        if not last:
                            pMd = pM[:, dr * P:(dr + 1) * P]
                            if first:
                                S.op("dve", lambda e: e.tensor_scalar(
                                    out=Sst[dr][sn][:], in0=pMd, scalar1=st[dr][:, 4, n:n + 1], scalar2=None,
                                    op0=ALU.mult), [pMr, st_r[dr]], [Sst_r[dr][sn]])
                            else:
                                S.op("dve", lambda e: e.tensor_scalar(
                                    out=Sst[dr][sn][:], in0=Sst[dr][so][:], scalar1=st[dr][:, 5, n:n + 1], scalar2=None,
                                    op0=ALU.mult), [Sst_r[dr][so], st_r[dr]], [Sst_r[dr][sn]])
                                S.op("dve", lambda e: e.scalar_tensor_tensor(
                                    out=Sst[dr][sn][:], in0=pMd, scalar=st[dr][:, 4, n:n + 1], in1=Sst[dr][sn][:],
                                    op0=ALU.mult, op1=ALU.add), [pMr, st_r[dr], Sst_r[dr][sn]], [Sst_r[dr][sn]])

                def stageB2(step):
                    sl = step % 3
                    sb_ = step % 2
                    first = step == 0
                    early = step < NT // 2
                    pO, pOr = PS()
                    for dr in range(2):
                        n = tile_of(step, dr)
                        ts_ = slice(n * P, (n + 1) * P)
                        half = dr if early else 1 - dr
                        po = pO[:, half * P:(half + 1) * P]
                        S.op("pe", lambda e: e.matmul(po, lhsT=vtm[:, n, :], rhs=AT2[sl][:, dr, :], start=True, stop=first),
                             [vtm_r, AT_r2[sl]], [pOr], signal=(first and dr == 1))
                        if not first:
                            S.op("pe", lambda e: e.matmul(po, lhsT=Sbf[dr][sb_][:], rhs=qin[dr][:, ts_],
                                                          start=False, stop=True),
                                 [Sbf_r[dr][sb_], qin_r[dr]], [pOr], signal=(dr == 1))
                    lo, hi_ = (step, NT - 1 - step) if early else (NT - 1 - step, step)
                    ov = oacc3[:, lo:hi_ + 1:hi_ - lo, :]
                    pv2 = pO[:, 0:2 * P].rearrange("p (d t) -> p d t", d=2)
                    orr = [oacc_r[lo], oacc_r[hi_]]
                    if early:
                        S.op("act", lambda e: e.activation(out=ov, in_=pv2, func=AF.Copy), [pOr], orr)
                    else:
                        S.op("dve", lambda e: e.tensor_tensor(out=ov, in0=pv2, in1=ov, op=ALU.add), [pOr] + orr, orr)

                ps_i[0] = 0
                PS()
                stageA1(0)
                PS()
                PS()
                stageA1(1)
                stageA2(0)
                for step in range(NT):
                    stageB1(step)
                    if step >= 1:
                        stageB2(step - 1)
                    else:
                        PS()
                    if step + 2 < NT:
                        stageA1(step + 2)
                    else:
                        PS()
                        PS()
                    if step + 1 < NT:
                        stageA2(step + 1)
                    else:
                        PS()
                stageB2(NT - 1)
                plive = {}

                def postA(blk):
                    t0 = blk * 512
                    ob = blk % 2
                    orr = oacc_r[blk * 4:blk * 4 + 4]
                    S.op("act", lambda e: e.activation(out=sqbs[ob][:], in_=oacc[:, t0:t0 + 512], func=AF.Square),
                         orr, [sqbs_r[ob]])
                    p2, p2r = PS()
                    S.op("pe", lambda e: e.matmul(p2[:, :], lhsT=ones[:], rhs=sqbs[ob][:], start=True, stop=True),
                         [sqbs_r[ob], const_r], [p2r])
                    plive[blk] = (p2, p2r)

                def postB(blk):
                    t0 = blk * 512
                    ob = blk % 2
                    orr = oacc_r[blk * 4:blk * 4 + 4]
                    p2, p2r = plive.pop(blk)
                    rs, rs_r, tmpn, tmpn_r = rss[ob], rss_r[ob], tmpns[ob], tmpns_r[ob]
                    S.op("act", lambda e: e.activation(out=rs[:], in_=p2[:, :], func=AF.Ln, bias=float(EPS),
                                                       scale=1.0 / P), [p2r], [rs_r])
                    S.op("act", lambda e: e.activation(out=rs[:], in_=rs[:], func=AF.Exp, scale=-0.5), [rs_r], [rs_r])
                    S.op("dve", lambda e: e.tensor_tensor(out=tmpn[:], in0=oacc[:, t0:t0 + 512], in1=rs[:],
                                                          op=ALU.mult), orr + [rs_r], [tmpn_r])
                    S.op("dve", lambda e: e.scalar_tensor_tensor(
                        out=ost[ob][:], in0=tmpn[:], scalar=hgains[:, 0:1], in1=gT[:, t0:t0 + 512],
                        op0=ALU.mult, op1=ALU.mult), [tmpn_r, gT_r, const_r], [ost_r[ob]])
                    S.dma("pool", mixT_d[h * P:(h + 1) * P, t0:t0 + 512], ost[ob][:], [ost_r[ob]], [mix_r[h]],
                          f"ost_hg{ob}")

                postA(0)
                for blk in range(L // 512):
                    if blk + 1 < L // 512:
                        postA(blk + 1)
                    postB(blk)
            S.barrier()


    def phase_gates(extra_loads=None):
        with ExitStack() as ph:
            for x_ in range(1):
                wst.append(sb(ph, f"wstx_g{x_}", [P, WCAP], F32))
                wst_r.append(Res(f"wstx_g{x_}"))
            wg = sb(ph, "wg", [P, KC, 3072], BF16)
            wg_col_r = [Res(f"wgc{i}") for i in range(8)] + [Res("wg1")] * 8 + [Res("wg2")] * 8
            for g8 in range(8):
                wload(wg, wg_col_r[g8], w_in_d, OFF_GATE + g8 * P, P, 0, KC, gain_idx=0, dst_col0=g8 * P)
            for ci in range(1, 3):
                wload_big(wg, wg_col_r[ci * 8], w_in_d, OFF_GATE + ci * 1024, 1024, KC, gain_idx=0, dst_col0=ci * 1024)
            if extra_loads is not None:
                extra_loads()
            sg = [sb(ph, f"sg{i}", [P, 4, 512], BF16) for i in range(2)]
            sg_r = [Res("sg0"), Res("sg1")]
            it = 0
            for ci in range(3):
                for blk in range(L // 512):
                    t0 = blk * 512
                    for fc0 in range(ci * 8, ci * 8 + 8, 4):
                        b = it % 2
                        it += 1
                        for j in range(4):
                            pg, pgr = PS()
                            proj_fm(pg[:, :], pgr, wg, wg_col_r[fc0 + j], (fc0 + j) * P, P, t0, 512)
                            S.op("act", lambda e, j=j, pg=pg: e.activation(out=sg[b][:, j, :], in_=pg[:, :],
                                                                           func=AF.Sigmoid), [pgr], [sg_r[b]])
                        dst = gate_d[fc0 * P:(fc0 + 4) * P, t0:t0 + 512].rearrange("(c p) t -> p c t", p=P)
                        S.dma("pool", dst, sg[b][:], [sg_r[b]], [gate_dr], f"sg{b}")
            S.barrier()
            del wst[NWST:]
            del wst_r[NWST:]

    gate_dr = Res("gate_d")
    out_r = [Res(f"out{n}") for n in range(NT)]
    h2_r = [Res(f"h2_{n}") for n in range(NT)]

    def phase_m1(wp, wp_r, wo, wo_r, extra_loads=None):
        with ExitStack() as ph:
            mixb = [sb(ph, f"mixb{i}", [P, 16, 512], BF16) for i in range(2)]
            mixb_r = [Res("mixb0"), Res("mixb1")]
            gtb = [sb(ph, f"gtb{i}", [P, 24, 512], BF16) for i in range(2)]
            gtb_r = [Res("gtb0"), Res("gtb1")]
            mg = sb(ph, "mg", [P, KC, 512], BF16)
            mg_r = Res("mg")
            macc = [sb(ph, f"macc{i}", [P, 512], BF16) for i in range(2)]
            macc_r = [Res("macc0"), Res("macc1")]
            mt = [sb(ph, f"mt{i}", [P, 512], BF16) for i in range(2)]
            mt_r = [Res("mt0"), Res("mt1")]
            xin = [sb(ph, f"m1_xin{i}", [P, D], F32) for i in range(2)]
            xin_r = [Res("m1xin0"), Res("m1xin1")]
            x1 = [sb(ph, f"m1_x1{i}", [P, D], F32) for i in range(2)]
            x1_r = [Res("m1x10"), Res("m1x11")]
            junk = sb(ph, "m1_junk", [P, D], BF16)
            junk_r = Res("m1junk")
            hb = [sb(ph, f"m1_hb{i}", [P, D], BF16) for i in range(2)]
            hb_r = [Res("m1hb0"), Res("m1hb1")]
            ss = sb(ph, "m1_ss", [P, 2 * NT], F32)
            ss_r = [Res(f"m1ss{n}") for n in range(NT)]
            h2s = [sb(ph, f"m1_h2s{i}", [P, KC, P], BF16) for i in range(2)]
            h2s_r = [Res("h2s0"), Res("h2s1")]
            def load_blk(blk):
                t0 = blk * 512
                sl = blk % 2
                for c4 in range(0, 16, 4):
                    S.dma("sp", mixb[sl][:, c4:c4 + 4, :],
                          mixT_d[c4 * P:(c4 + 4) * P, t0:t0 + 512].rearrange("(c p) t -> p c t", p=P),
                          mix_r, [mixb_r[sl]], f"mixb{sl}")
                for c4 in range(0, 24, 4):
                    S.dma("sp", gtb[sl][:, c4:c4 + 4, :],
                          gate_d[c4 * P:(c4 + 4) * P, t0:t0 + 512].rearrange("(c p) t -> p c t", p=P),
                          [gate_dr], [gtb_r[sl]], f"gtb{sl}")

            def load_x(n):
                S.dma("sp", xin[n % 2][:], x_d[n * P:(n + 1) * P, :], [], [xin_r[n % 2]], f"m1xin{n % 2}")

            def tile_stage2(n):
                i = n % 2
                pt, ptr = PS()
                ptb = pt[:].bitcast(BF16)
                for c in range(KC):
                    S.op("pe", lambda e, c=c: e.transpose(out=ptb[:, c * P:(c + 1) * P],
                                                          in_=hb[i][:, c * P:(c + 1) * P], identity=ident[:]),
                         [hb_r[i], const_r], [ptr], signal=(c == KC - 1))
                S.op("act", lambda e: e.activation(out=h2s[i][:], in_=ptb.rearrange("p (c t) -> p c t", c=KC),
                                                   func=AF.Copy), [ptr], [h2s_r[i]])
                for c4 in range(0, KC, 4):
                    S.dma("pool", h2T_d[c4 * P:(c4 + 4) * P, n * P:(n + 1) * P].rearrange("(c p) t -> p c t", p=P),
                          h2s[i][:, c4:c4 + 4, :], [h2s_r[i]], [h2_r[n]], f"h2s{i}")

            it = 0
            pending = None
            load_blk(0)
            load_x(0)
            if extra_loads is not None:
                extra_loads()
            for blk in range(L // 512):
                t0 = blk * 512
                sl = blk % 2
                if blk + 1 < L // 512:
                    load_blk(blk + 1)
                for oc in range(KC):
                    b = it % 2
                    it += 1
                    for br, (kc0, nk) in enumerate(((0, 8), (8, 4), (12, 4))):
                        pp, ppr = PS()
                        for k in range(nk):
                            S.op("pe", lambda e, k=k, kc0=kc0, pp=pp: e.matmul(
                                pp[:, :], lhsT=wp[:, kc0 + k, oc * P:(oc + 1) * P], rhs=mixb[sl][:, kc0 + k, :],
                                start=(k == 0), stop=(k == nk - 1)), [wp_r, mixb_r[sl]], [ppr], signal=(k == nk - 1))
                        gate = gtb[sl][:, br * 8 + oc, :]
                        if br == 0:
                            S.op("dve", lambda e, pp=pp: e.tensor_tensor(out=macc[b][:], in0=pp[:, :], in1=gate, op=ALU.mult),
                                 [ppr, gtb_r[sl]], [macc_r[b]])
                        else:
                            S.op("dve", lambda e, pp=pp: e.tensor_tensor(out=mt[b][:], in0=pp[:, :], in1=gate, op=ALU.mult),
                                 [ppr, gtb_r[sl]], [mt_r[b]])
                            if br == 1:
                                S.op("dve", lambda e: e.tensor_tensor(out=macc[b][:], in0=macc[b][:], in1=mt[b][:],
                                                                      op=ALU.add), [macc_r[b], mt_r[b]], [macc_r[b]])
                            else:
                                S.op("dve", lambda e: e.tensor_tensor(out=mg[:, oc, :], in0=macc[b][:], in1=mt[b][:],
                                                                      op=ALU.add), [macc_r[b], mt_r[b]], [mg_r])
                    if oc == 3 and pending is not None:
                        tile_stage2(pending)
                        pending = None
                for tt in range(4):
                    n = blk * 4 + tt
                    i = n % 2
                    for half in range(2):
                        py, pyr = PS()
                        for k in range(KC):
                            S.op("pe", lambda e, k=k, py=py: e.matmul(
                                py[:, :], lhsT=mg[:, k, tt * P:(tt + 1) * P], rhs=wo[:, k, half * 512:(half + 1) * 512],
                                start=(k == 0), stop=(k == KC - 1)), [mg_r, wo_r], [pyr], signal=(k == KC - 1))
                        S.op("dve", lambda e, py=py: e.tensor_tensor(
                            out=x1[i][:, half * 512:(half + 1) * 512], in0=py[:, :], in1=xin[i][:, half * 512:(half + 1) * 512],
                            op=ALU.add), [pyr, xin_r[i]], [x1_r[i]])
                    if n + 1 < NT:
                        load_x(n + 1)
                    S.dma("pool", out_d[n * P:(n + 1) * P, :], x1[i][:], [x1_r[i]], [out_r[n]], f"m1x1{i}")
                    S.op("act", lambda e: e.activation(out=junk[:], in_=x1[i][:], func=AF.Square,
                                                       accum_out=ss[:, 2 * n:2 * n + 1]), [x1_r[i]], [junk_r, ss_r[n]])
                    S.op("act", lambda e: e.activation(out=ss[:, 2 * n + 1:2 * n + 2], in_=ss[:, 2 * n:2 * n + 1],
                                                       func=AF.Sqrt, bias=float(EPS), scale=1.0 / D), [ss_r[n]], [ss_r[n]])
                    S.op("dve", lambda e: e.reciprocal(out=ss[:, 2 * n + 1:2 * n + 2], in_=ss[:, 2 * n + 1:2 * n + 2]),
                         [ss_r[n]], [ss_r[n]])
                    S.op("dve", lambda e: e.tensor_scalar(out=hb[i][:], in0=x1[i][:], scalar1=ss[:, 2 * n + 1:2 * n + 2],
                                                          scalar2=None, op0=ALU.mult), [x1_r[i], ss_r[n]], [hb_r[i]])
                    if pending is not None:
                        tile_stage2(pending)
                    pending = n
            tile_stage2(pending)
            S.barrier()

    def phase_m2(wfi_pre, wfi_rs):
        NF = DFF // P
        with ExitStack() as ph:
            for x_ in range(2):
                wst.append(sb(ph, f"wstx_m2{x_}", [P, WCAP], F32))
                wst_r.append(Res(f"wstx_m2{x_}"))
            wfi_c = dict(wfi_pre)
            for ci in (2, 3, 1, 4, 5):
                wfi_c[ci] = sb(ph, f"wfi_c{ci}", [P, KC, min(1024, 2 * DFF - ci * 1024)], BF16)

            def wfcol(col):
                return wfi_c[col // 1024], col % 1024
            wfo = sb(ph, "wfo", [P, NF, D], BF16)
            wfo_rs = [Res(f"wfo{i}") for i in range(NF // 2)]
            h2b = [sb(ph, f"h2b{i}", [P, KC, 512], BF16) for i in range(2)]
            h2b_r = [Res("h2b0"), Res("h2b1")]
            uT = sb(ph, "uT", [P, NF, 512], BF16)
            uT_r = Res("uT")
            sa = [sb(ph, f"sa{i}", [P, 512], BF16) for i in range(2)]
            sa_r = [Res("sa0"), Res("sa1")]
            x1in = [sb(ph, f"m2_x1{i}", [P, D], F32) for i in range(2)]
            x1in_r = [Res("m2x10"), Res("m2x11")]
            def load_h2(blk):
                t0 = blk * 512
                sl = blk % 2
                for c4 in range(0, KC, 4):
                    S.dma("sp", h2b[sl][:, c4:c4 + 4, :],
                          h2T_d[c4 * P:(c4 + 4) * P, t0:t0 + 512].rearrange("(c p) t -> p c t", p=P),
                          h2_r[blk * 4:blk * 4 + 4], [h2b_r[sl]], f"h2b{sl}")

            def load_x1(n):
                S.dma("sp", x1in[n % 2][:], out_d[n * P:(n + 1) * P, :], [out_r[n]], [x1in_r[n % 2]], f"m2x1{n % 2}")

            it = 0
            load_h2(0)
            load_x1(0)
            for ci in (2, 3, 1, 4, 5):
                wload_big(wfi_c[ci], wfi_rs[ci], w_fin_d, ci * 1024, min(1024, 2 * DFF - ci * 1024), KC, gain_idx=2)
            for k2 in range(NF // 2):
                for k1_ in range(2):
                    wload(wfo, wfo_rs[k2], w_fout_d, 0, D, 2 * k2 + k1_, 1, None, dst_kc0=2 * k2 + k1_, eng="dve")
            for blk in range(L // 512):
                t0 = blk * 512
                sl = blk % 2
                if blk + 1 < L // 512:
                    load_h2(blk + 1)
                for fc in range(NF):
                    b = it % 2
                    it += 1
                    pa, par = PS()
                    pb, pbr = PS()
                    for c in range(KC):
                        wa_, ca_ = wfcol(fc * P)
                        S.op("pe", lambda e, c=c, pa=pa: e.matmul(pa[:, :], lhsT=wa_[:, c, ca_:ca_ + P],
                                                                  rhs=h2b[sl][:, c, :], start=(c == 0), stop=(c == KC - 1)),
                             [wfi_rs[(fc * P) // 1024], h2b_r[sl]], [par], signal=(c == KC - 1))
                    for c in range(KC):
                        wb_, cb_ = wfcol(DFF + fc * P)
                        S.op("pe", lambda e, c=c, pb=pb: e.matmul(pb[:, :], lhsT=wb_[:, c, cb_:cb_ + P],
                                                                  rhs=h2b[sl][:, c, :], start=(c == 0), stop=(c == KC - 1)),
                             [wfi_rs[(DFF + fc * P) // 1024], h2b_r[sl]], [pbr], signal=(c == KC - 1))
                    S.op("act", lambda e, pa=pa: e.activation(out=sa[b][:], in_=pa[:, :], func=AF.Silu), [par], [sa_r[b]])
                    S.op("dve", lambda e, pb=pb: e.tensor_tensor(out=uT[:, fc, :], in0=pb[:, :], in1=sa[b][:], op=ALU.mult),
                         [pbr, sa_r[b]], [uT_r])
                for tt in range(4):
                    n = blk * 4 + tt
                    i = n % 2
                    for half in range(2):
                        py, pyr = PS()
                        for k in range(NF):
                            S.op("pe", lambda e, k=k, py=py: e.matmul(
                                py[:, :], lhsT=uT[:, k, tt * P:(tt + 1) * P], rhs=wfo[:, k, half * 512:(half + 1) * 512],
                                start=(k == 0), stop=(k == NF - 1)), [uT_r, wfo_rs[k // 2]], [pyr], signal=(k == NF - 1))
                        S.op("dve", lambda e, py=py: e.tensor_tensor(
                            out=x1in[i][:, half * 512:(half + 1) * 512], in0=py[:, :],
                            in1=x1in[i][:, half * 512:(half + 1) * 512], op=ALU.add), [pyr, x1in_r[i]], [x1in_r[i]])
                    if n + 1 < NT:
                        load_x1(n + 1)
                    S.dma("pool", out_d[n * P:(n + 1) * P, :], x1in[i][:], [x1in_r[i]], [out_r[n]], f"m2o{i}")
            S.barrier()
            del wst[NWST:]
            del wst_r[NWST:]

    if "mem" in phases:
        phase_mem()
    memw.close()
    hgw = ExitStack()
    hg_heads = list(range(8) if not debug_heads else debug_heads)
    wts_g = [[sb(hgw, f"w_hg{k}_{i}", [P, KC, P], BF16) for k in range(5)] for i in range(2)]
    wts_g_r = [[Res(f"w_hg{k}_{i}") for k in range(5)] for i in range(2)]

    def hg_first_weights():
        for k in range(5):
            wload(wts_g[0][k], wts_g_r[0][k], w_in_d, OFF_HG + k * 1024 + hg_heads[0] * P, P, 0, KC, gain_idx=0)

    pre_hg = hg_first_weights if ("hg" in phases and "da" in phases) else None
    if "da" in phases:
        phase_da(range(4) if not debug_slots else debug_slots, pre_hg)
    if "hg" in phases:
        phase_hg(hg_heads, wts_g, wts_g_r, first_loaded=(pre_hg is not None))
    hgw.close()
    if "tail" in phases:
        m2pre = ExitStack()
        wfi_pre = {ci: sb(m2pre, f"wfi_c{ci}", [P, KC, 1024], BF16) for ci in (0,)}
        wfi_rs = [Res(f"wfi{i}") for i in range(6)]

        def m2_preloads():
            for ci in (0,):
                wload_big(wfi_pre[ci], wfi_rs[ci], w_fin_d, ci * 1024, 1024, KC, gain_idx=2)

        m1w = ExitStack()
        wp = sb(m1w, "wp", [P, 16, D], BF16)
        wp_r = Res("wp")
        wo = sb(m1w, "wo", [P, KC, D], BF16)
        wo_r = Res("wo")

        def m1_loads():
            wload_big(wp, wp_r, w_phg_d, 0, D, 8, eng="dve")
            wload_big(wp, wp_r, w_pda_d, 0, D, 4, dst_kc0=8, eng="dve")
            wload_big(wp, wp_r, w_pmem_d, 0, D, 4, dst_kc0=12, eng="dve")
            wload_big(wo, wo_r, w_out_d, 0, D, KC, eng="dve")

        phase_gates(m1_loads)
    S.barrier()
    hstack.close()
    if "tail" in phases:
        phase_m1(wp, wp_r, wo, wo_r, m2_preloads)
        m1w.close()
        phase_m2(wfi_pre, wfi_rs)
        m2pre.close()

    S.barrier()
    es.close()
    print("kernel build: ops", S.nops, "sems", S.nsem)
    return nc


_CACHE = {}


def _consts():
    ident = np.eye(P, dtype=np.float32)
    s = np.arange(P)[:, None]
    t = np.arange(P)[None, :]
    cmask = np.stack([(s <= t), (s >= t)], axis=1).astype(np.float32)
    damask = np.zeros((P, 12, 2 * P), np.float64)
    b = np.arange(P)[:, None].astype(np.float64)
    a = np.arange(P)[None, :].astype(np.float64)
    for h in range(12):
        d = DA_CFG[h // 4][1]
        relA = b - a - 64.0
        relB = b - a + 64.0
        damask[:, h, 0:P] = (b >= a) * np.exp(-SLOPES[h] * d * np.abs(relA))
        damask[:, h, P:2 * P] = (b <= a) * np.exp(-SLOPES[h] * d * np.abs(relB))
    rmask = np.ones((P, 1024), np.float32)
    rmask[:, ::P] = 0.0
    return ident, cmask, damask.astype(np.float32), rmask


def make_in_maps(inputs):
    ident, cmask, damask, rmask = _consts()
    f = lambda k: np.ascontiguousarray(np.asarray(inputs[k], dtype=np.float32))
    gains = np.stack([f("norm_mix_gain")[0], f("norm_mem_gain")[0], f("norm_ffn_gain")[0]], 0)
    gainsT = np.ascontiguousarray(gains.reshape(3, KC, P).transpose(2, 0, 1))
    hgains = np.ascontiguousarray(np.stack([f("hg_norm_gain")[0], f("da_q_gain")[0], f("da_k_gain")[0],
                                            f("mem_q_gain")[0], f("mem_k_gain")[0]], 1))
    lb = np.concatenate([f("lb_logits_fw"), f("lb_logits_bw")], 0)
    lbl = np.ascontiguousarray(lb.reshape(4, 8, P).transpose(2, 0, 1))
    shared = {
        "w_in": f("w_in")[0], "w_mem_kv": f("w_mem_kv")[0], "w_proj_hg": f("w_proj_hg")[0],
        "w_proj_da": f("w_proj_da")[0], "w_proj_mem": f("w_proj_mem")[0], "w_out": f("w_out")[0],
        "w_ffn_in": f("w_ffn_in")[0], "w_ffn_out": f("w_ffn_out")[0],
        "gainsT": gainsT, "hgains": hgains, "lbl": lbl, "cmask": cmask, "damask": damask,
        "rmask": rmask, "ident": ident,
    }
    x = f("x")
    mem = f("mem")
    return [dict(shared, x=x[b], mem=mem[b]) for b in range(8)]


def kernel(**inputs):
    if "nc" not in _CACHE:
        _CACHE["nc"] = build()
    nc = _CACHE["nc"]
    in_maps = make_in_maps(inputs)
    res = run_bass_kernel_spmd(nc, in_maps, core_ids=list(range(8)))
    return np.stack([np.asarray(r["out"], dtype=np.float32) for r in res.results], 0)
```

```python
import numpy as np
from contextlib import ExitStack
import concourse.bass as bass
import concourse.mybir as mybir
from concourse.bass_utils import run_bass_kernel_spmd

F32 = mybir.dt.float32
BF16 = mybir.dt.bfloat16
AF = mybir.ActivationFunctionType
ALU = mybir.AluOpType
AX = mybir.AxisListType

P = 128
L = 4096
D = 1024
KC = 8
NT = L // P
NMEM = 256
DFF = 2816
EPS = 1e-6
IN_COLS = 13312
OFF_HG = 0
OFF_DA = 5120
OFF_MEMQ = 9728
OFF_GATE = 10240
DA_CFG = ((128, 1), (512, 4), (2048, 16))
SLOPES = (2.0 ** (-8.0 * np.arange(1, 13) / 12)).astype(np.float64)


class Res:
    __slots__ = ("name", "w", "rd")

    def __init__(self, name):
        self.name = name
        self.w = None
        self.rd = {}


class Sched:
    EPOCH = 30000

    def __init__(self, nc, es):
        self.nc = nc
        self.es = es
        self.h = {"pe": nc.tensor, "act": nc.scalar, "dve": nc.vector, "pool": nc.gpsimd, "sp": nc.sync}
        self.E = {n: dict(sems=[], count=0, waited={}) for n in self.h}
        self.dsem = {}
        self.nsem = 0
        self.nops = 0

    def _newsem(self, name):
        self.nsem += 1
        return self.es.enter_context(self.nc.semaphore(name))

    def _next_tag(self, e):
        E = self.E[e]
        c = E["count"] + 1
        ep = (c - 1) // self.EPOCH
        while len(E["sems"]) <= ep:
            E["sems"].append(self._newsem(f"s_{e}_{len(E['sems'])}"))
        return (E["sems"][ep], c - ep * self.EPOCH, e, (e, ep))

    def _waits(self, e, reads, writes, is_dma):
        need = {}

        def add(tag, kind):
            sem, val, pe, key = tag
            if pe == e and not is_dma:
                if e == "pe":
                    return
                if kind != "raw":
                    return
            if key not in need or need[key][1] < val:
                need[key] = (sem, val)

        for r in reads:
            if r.w is not None:
                add(r.w, "raw")
        for w in writes:
            if w.w is not None:
                add(w.w, "waw")
            for t in w.rd.values():
                add(t, "war")
        E = self.E[e]
        for key, (sem, val) in need.items():
            if E["waited"].get(key, 0) >= val:
                continue
            E["waited"][key] = val
            self.h[e].wait_ge(sem, val)

    def _commit(self, tag, reads, writes):
        for w in writes:
            w.w = tag
            w.rd = {}
        for r in reads:
            k = tag[3]
            if k not in r.rd or r.rd[k][1] < tag[1]:
                r.rd[k] = tag

    def op(self, e, fn, R=(), W=(), signal=True):
        self._waits(e, R, W, False)
        tag = self._next_tag(e)
        ins = fn(self.h[e])
        if signal:
            ins.then_inc(tag[0], 1)
            self.E[e]["count"] += 1
        self._commit(tag, R, W)
        self.nops += 1
        return ins

    def dma(self, q, out, in_, R, W, key):
        self._waits(q, R, W, True)
        if key not in self.dsem:
            self.dsem[key] = [self._newsem(f"d_{len(self.dsem)}"), 0]
        ds = self.dsem[key]
        ds[1] += 16
        tag = (ds[0], ds[1], "dma", ("dma", key))
        self.h[q].dma_start(out=out, in_=in_).then_inc(ds[0], 16)
        self._commit(tag, R, W)
        self.nops += 1

    def barrier(self):
        tags = []
        for e, E in self.E.items():
            if E["count"] > 0:
                c = E["count"]
                ep = (c - 1) // self.EPOCH
                tags.append((E["sems"][ep], c - ep * self.EPOCH, (e, ep)))
        for key, ds in self.dsem.items():
            if ds[1] > 0:
                tags.append((ds[0], ds[1], ("dma", key)))
        for e, E in self.E.items():
            for sem, val, key in tags:
                if key[0] == e and e == "pe":
                    continue
                if E["waited"].get(key, 0) >= val:
                    continue
                E["waited"][key] = val
                self.h[e].wait_ge(sem, val)


def build(debug=False, phases=("mem", "da", "hg", "tail"), debug_slots=None, debug_heads=None):
    nc = bass.Bass("TRN2", target_bir_lowering=False)
    es = ExitStack()
    S = Sched(nc, es)

    def din(name, shape, dt=F32):
        return nc.dram_tensor(name, list(shape), dt, kind="ExternalInput").ap()

    x_d = din("x", [L, D])
    mem_d = din("mem", [NMEM, D])
    w_in_d = din("w_in", [D, IN_COLS])
    w_kv_d = din("w_mem_kv", [D, D])
    w_phg_d = din("w_proj_hg", [1024, D])
    w_pda_d = din("w_proj_da", [512, D])
    w_pmem_d = din("w_proj_mem", [512, D])
    w_out_d = din("w_out", [D, D])
    w_fin_d = din("w_ffn_in", [D, 2 * DFF])
    w_fout_d = din("w_ffn_out", [DFF, D])
    gains_d = din("gainsT", [P, 3, KC])
    hgains_d = din("hgains", [P, 5])
    lbl_d = din("lbl", [P, 4, 8])
    cmask_d = din("cmask", [P, 2, P])
    damask_d = din("damask", [P, 12, 2 * P])
    rmask_d = din("rmask", [P, 1024])
    ident_d = din("ident", [P, P])
    out_d = nc.dram_tensor("out", [L, D], F32, kind="ExternalOutput").ap()
    mix_kind = "ExternalOutput" if debug else "Internal"
    mixT_d = nc.dram_tensor("mixT", [2048, L], BF16, kind=mix_kind).ap()
    h2T_d = nc.dram_tensor("h2T", [D, L], BF16, kind="Internal").ap()
    gate_d = nc.dram_tensor("gateT", [3072, L], BF16, kind="Internal").ap()

    def sb(st, name, shape, dt, side=None):
        if side is None:
            return st.enter_context(nc.sbuf_tensor(name, list(shape), dt))
        return st.enter_context(nc.sbuf_tensor(name, list(shape), dt, side=side))

    hT_r = [Res(f"hT{n}") for n in range(NT)]
    ident = sb(es, "identb", [P, P], BF16)
    ones = sb(es, "onesb", [P, P], BF16)
    cmask = sb(es, "cmaskb", [P, 2, P], BF16)
    rmask = sb(es, "rmaskf", [P, 1024], F32)
    gainsT = sb(es, "gainsT_sb", [P, 3, KC], F32)
    hgains = sb(es, "hgains_sb", [P, 5], F32)
    hgs = sb(es, "hgs", [P, 5], F32)
    lbl = sb(es, "lbl_sb", [P, 4, 8], F32)
    lbv = sb(es, "lbv", [P, 2, 8], F32)
    oml = sb(es, "oml", [P, 2, 8], F32)
    noml = sb(es, "noml", [P, 2, 8], F32)
    const_r = Res("const")
    psum = [es.enter_context(nc.psum_tensor(f"ps{i}", [P, 512], F32)) for i in range(8)]
    ps_r = [Res(f"ps{i}") for i in range(8)]
    ps_i = [0]

    ps_c = [0] * 8

    def PS(chain=None, nch=2):
        if chain is None:
            i = ps_i[0] % 8
            ps_i[0] += 1
        else:
            w = 8 // nch
            i = w * chain + ps_c[chain] % w
            ps_c[chain] += 1
        return psum[i], ps_r[i]

    def run_chains(gens, stagger=0):
        gens = [(i_, g_) for i_, g_ in enumerate(gens)]
        rnd = 0
        while gens:
            for i_, g_ in list(gens):
                if rnd < i_ * stagger:
                    continue
                try:
                    next(g_)
                except StopIteration:
                    gens.remove((i_, g_))
            rnd += 1

    SQ128 = float(np.sqrt(128.0))
    HG_STAG, P0_STAG = 0, 1
    NWST, WCAP = 3, 1024
    wst = [sb(es, f"wst{i}", [P, WCAP], F32) for i in range(NWST)]
    hstack = ExitStack()
    hT = sb(hstack, "hT", [P, KC, L], BF16, side="right")

    with ExitStack() as ph:
        cst = sb(ph, "cstage", [P, 2 * P], F32)
        cst_r = Res("cstage")

        def load_const(dram, dst, n, cast):
            flat_d = dram if len(dram.shape) == 2 else (
                dram.rearrange("p a b -> p (a b)"))
            if cast:
                S.dma("sp", cst[:, 0:n], flat_d, [], [cst_r], "cstage")
                dflat = dst[:] if len(dst.shape) == 2 else dst[:].rearrange("p a b -> p (a b)")
                S.op("dve", lambda e: e.tensor_copy(out=dflat, in_=cst[:, 0:n]), [cst_r], [const_r])
            else:
                dflat = dst[:] if len(dst.shape) == 2 else dst[:].rearrange("p a b -> p (a b)")
                S.dma("sp", dflat, flat_d, [], [const_r], "const_" + dst.name)

        load_const(ident_d, ident, P, True)
        load_const(cmask_d, cmask, 2 * P, True)
        S.dma("sp", rmask[:], rmask_d[:, 0:1024], [], [const_r], "const_rmask")
        load_const(gains_d, gainsT, 3 * KC, False)
        load_const(hgains_d, hgains, 5, False)
        load_const(lbl_d, lbl, 32, False)
        S.op("dve", lambda e: e.memset(ones[:], 1.0), [], [const_r])
        S.op("dve", lambda e: e.tensor_scalar(out=hgs[:], in0=hgains[:], scalar1=1.0 / SQ128, scalar2=None,
                                              op0=ALU.mult), [const_r], [const_r])
        for d_ in range(2):
            S.op("dve", lambda e, d_=d_: e.tensor_sub(out=lbv[:, d_, :], in0=lbl[:, 2 * d_, :],
                                                      in1=lbl[:, 2 * d_ + 1, :]), [const_r], [const_r])
        S.op("act", lambda e: e.activation(out=lbv[:], in_=lbv[:], func=AF.Sigmoid), [const_r], [const_r])
        S.op("dve", lambda e: e.tensor_scalar(out=oml[:], in0=lbv[:], scalar1=-1.0, scalar2=1.0,
                                              op0=ALU.mult, op1=ALU.add), [const_r], [const_r])
        S.op("dve", lambda e: e.tensor_scalar(out=noml[:], in0=oml[:], scalar1=-1.0, scalar2=None,
                                              op0=ALU.mult), [const_r], [const_r])
        S.barrier()

    wst_r = [Res(f"wst{i}") for i in range(NWST)]
    wst_i = [0]

    def wload(dst, dst_r, w2d, col0, ncols, kc0, nkc, gain_idx=None, dst_kc0=0, dst_col0=0, eng="dve"):
        assert nkc * ncols <= WCAP
        i = wst_i[0] % len(wst)
        wst_i[0] += 1
        st = wst[i][:, 0:nkc * ncols].rearrange("p (c n) -> p c n", c=nkc)
        src = w2d[kc0 * P:(kc0 + nkc) * P, col0:col0 + ncols].rearrange("(c p) n -> p c n", p=P)
        S.dma("sp", st, src, [], [wst_r[i]], f"wst{i}")
        o = dst[:, dst_kc0:dst_kc0 + nkc, dst_col0:dst_col0 + ncols]
        if gain_idx is None:
            S.op(eng, lambda e: e.tensor_copy(out=o, in_=st), [wst_r[i]], [dst_r])
        else:
            g = gainsT[:, gain_idx, kc0:kc0 + nkc].unsqueeze(2).to_broadcast([P, nkc, ncols])
            S.op(eng, lambda e: e.tensor_tensor(out=o, in0=st, in1=g, op=ALU.mult),
                 [wst_r[i], const_r], [dst_r])

    def wload_big(dst, dst_r, w2d, col0, ncols, nkc_total, gain_idx=None, dst_kc0=0, dst_col0=0, eng="dve"):
        for cc in range(0, ncols, 1024):
            nc_ = min(1024, ncols - cc)
            step = max(1, WCAP // nc_)
            for k0 in range(0, nkc_total, step):
                wload(dst, dst_r, w2d, col0 + cc, nc_, k0, min(step, nkc_total - k0), gain_idx,
                      dst_kc0=dst_kc0 + k0, dst_col0=dst_col0 + cc, eng=eng)

    def norm_transpose(ph, tag, src_rows, ntiles, dstT, dst_res, dma_key, NCH=2):
        xin = [sb(ph, f"{tag}_xin{i}", [P, D], F32) for i in range(NCH)]
        xin_r = [Res(f"{tag}_xin{i}") for i in range(NCH)]
        junk = [sb(ph, f"{tag}_junk{i}", [P, D], BF16) for i in range(NCH)]
        junk_r = [Res(f"junk{i}") for i in range(NCH)]
        hb = [sb(ph, f"{tag}_hb{i}", [P, D], BF16) for i in range(NCH)]
        hb_r = [Res(f"{tag}_hb{i}") for i in range(NCH)]
        ss = sb(ph, f"{tag}_ss", [P, 2 * ntiles], F32)
        ss_r = [Res(f"{tag}_ss{i}") for i in range(ntiles)]

        def chain(i):
            for n in range(i, ntiles, NCH):
                S.dma("sp", xin[i][:], src_rows(n), [], [xin_r[i]], f"{dma_key}{i}")
                yield
                S.op("act", lambda e: e.activation(out=junk[i][:], in_=xin[i][:], func=AF.Square,
                                                   accum_out=ss[:, 2 * n:2 * n + 1]), [xin_r[i]], [junk_r[i], ss_r[n]])
                yield
                S.op("act", lambda e: e.activation(out=ss[:, 2 * n + 1:2 * n + 2], in_=ss[:, 2 * n:2 * n + 1],
                                                   func=AF.Sqrt, bias=float(EPS), scale=1.0 / D), [ss_r[n]], [ss_r[n]])
                yield
                S.op("dve", lambda e: e.reciprocal(out=ss[:, 2 * n + 1:2 * n + 2], in_=ss[:, 2 * n + 1:2 * n + 2]),
                     [ss_r[n]], [ss_r[n]])
                yield
                S.op("dve", lambda e: e.tensor_scalar(out=hb[i][:], in0=xin[i][:], scalar1=ss[:, 2 * n + 1:2 * n + 2],
                                                      scalar2=None, op0=ALU.mult), [xin_r[i], ss_r[n]], [hb_r[i]])
                yield
                pt, pr = PS(i, NCH)
                ptb = pt[:].bitcast(BF16)
                for c in range(KC):
                    S.op("pe", lambda e: e.transpose(out=ptb[:, c * P:(c + 1) * P], in_=hb[i][:, c * P:(c + 1) * P],
                                                     identity=ident[:]), [hb_r[i], const_r], [pr], signal=(c == KC - 1))
                yield
                S.op("dve" if i % 2 == 0 else "act", (lambda e: e.tensor_copy(
                    out=dstT[:, :, n * P:(n + 1) * P], in_=ptb.rearrange("p (c t) -> p c t", c=KC))) if i % 2 == 0 else (
                    lambda e: e.activation(out=dstT[:, :, n * P:(n + 1) * P],
                                           in_=ptb.rearrange("p (c t) -> p c t", c=KC), func=AF.Copy)),
                     [pr], [dst_res[n]])
                yield

        run_chains([chain(c_) for c_ in range(NCH)], stagger=P0_STAG)

    memw = ExitStack()
    mem_pre = {}
    if "mem" in phases:
        mem_pre["wkv"] = (sb(memw, "wkv", [P, KC, D], BF16), Res("wkv"))
        mem_pre["wq"] = (sb(memw, "wq_mem", [P, KC, 512], BF16), Res("wq_mem"))
    with ExitStack() as ph:
        norm_transpose(ph, "p0", lambda n: x_d[n * P:(n + 1) * P, :], NT, hT, hT_r, "p0x", NCH=4)
        if "mem" in phases:
            wload_big(mem_pre["wkv"][0], mem_pre["wkv"][1], w_kv_d, 0, D, KC, gain_idx=1)
            wload_big(mem_pre["wq"][0], mem_pre["wq"][1], w_in_d, OFF_MEMQ, 512, KC, gain_idx=0)
        S.barrier()

    def hT_res(t0, nt):
        return hT_r[t0 // P:(t0 + nt + P - 1) // P]

    def proj_fm(ps_ap, pr, wb, wb_r, j0, ncol, t0, nt, tstep=1):
        for c in range(KC):
            rhs = hT[:, c, t0:t0 + (nt - 1) * tstep + 1:tstep] if tstep > 1 else hT[:, c, t0:t0 + nt]
            S.op("pe", lambda e, c=c, rhs=rhs: e.matmul(ps_ap, lhsT=wb[:, c, j0:j0 + ncol], rhs=rhs,
                                                        start=(c == 0), stop=(c == KC - 1)),
                 [wb_r] + (hT_r if tstep > 1 else hT_res(t0, nt)), [pr], signal=(c == KC - 1))

    qk_cnt = [0]

    def qk_norm_g(bufs, src_ps, src_r, n, gain_col, extra_scale, dst_ap, dst_r, chain, nch=2):
        sqb, sqb_r, rs, rs_r = bufs
        S.op("act", lambda e: e.activation(out=sqb[:, 0:n], in_=src_ps, func=AF.Square), [src_r], [sqb_r])
        yield
        p2, p2r = PS(chain, nch)
        S.op("pe", lambda e: e.matmul(p2[:, 0:n], lhsT=ones[:], rhs=sqb[:, 0:n], start=True, stop=True),
             [sqb_r, const_r], [p2r])
        yield
        S.op("act", lambda e: e.activation(out=rs[:, 0:n], in_=p2[:, 0:n], func=AF.Ln, bias=float(EPS),
                                           scale=1.0 / P), [p2r], [rs_r])
        yield
        S.op("act", lambda e: e.activation(out=rs[:, 0:n], in_=rs[:, 0:n], func=AF.Exp, scale=-0.5), [rs_r], [rs_r])
        yield
        gcol = (hgs if extra_scale == "qscale" else hgains)[:, gain_col:gain_col + 1]
        S.op("dve", lambda e: e.scalar_tensor_tensor(out=dst_ap, in0=src_ps, scalar=gcol, in1=rs[:, 0:n],
                                                     op0=ALU.mult, op1=ALU.mult),
             [src_r, rs_r, const_r], [dst_r])
        yield

    def qk_norm(ph_bufs, src_ps, src_r, n, gain_col, extra_scale, dst_ap, dst_r):
        sqb, sqb_r, rs, rs_r = ph_bufs[qk_cnt[0] % len(ph_bufs)]
        qk_cnt[0] += 1
        S.op("act", lambda e: e.activation(out=sqb[:, 0:n], in_=src_ps, func=AF.Square), [src_r], [sqb_r])
        p2, p2r = PS()
        S.op("pe", lambda e: e.matmul(p2[:, 0:n], lhsT=ones[:], rhs=sqb[:, 0:n], start=True, stop=True),
             [sqb_r, const_r], [p2r])
        S.op("act", lambda e: e.activation(out=rs[:, 0:n], in_=p2[:, 0:n], func=AF.Ln, bias=float(EPS),
                                           scale=1.0 / P), [p2r], [rs_r])
        S.op("act", lambda e: e.activation(out=rs[:, 0:n], in_=rs[:, 0:n], func=AF.Exp, scale=-0.5), [rs_r], [rs_r])
        gcol = (hgs if extra_scale == "qscale" else hgains)[:, gain_col:gain_col + 1]
        S.op("dve", lambda e: e.scalar_tensor_tensor(out=dst_ap, in0=src_ps, scalar=gcol, in1=rs[:, 0:n],
                                                     op0=ALU.mult, op1=ALU.mult),
             [src_r, rs_r, const_r], [dst_r])

    def phase_mem():
        with ExitStack() as ph:
            mnT = sb(ph, "mnT", [P, KC, NMEM], BF16)
            mnT_r = [Res("mnT0"), Res("mnT1")]
            norm_transpose(ph, "pm", lambda n: mem_d[n * P:(n + 1) * P, :], 2, mnT, mnT_r, "pmx")
            wkv, wkv_r = mem_pre["wkv"]
            wq, wq_r = mem_pre["wq"]
            khT = sb(ph, "khT_mem", [P, 4, NMEM], BF16)
            khT_r = Res("khT_mem")
            vm = sb(ph, "v_mem", [P, 2, 512], BF16)
            vm_r = Res("v_mem")
            nb = [(sb(ph, f"sqb_mem{i}", [P, 512], BF16), Res(f"sqb{i}"), sb(ph, f"rs_mem{i}", [P, 512], F32),
                   Res(f"rs{i}")) for i in range(4)]
            for hd in range(4):
                pk, pkr = PS()
                for c in range(KC):
                    S.op("pe", lambda e, c=c, hd=hd: e.matmul(pk[:, 0:NMEM], lhsT=wkv[:, c, hd * P:(hd + 1) * P],
                                                               rhs=mnT[:, c, :], start=(c == 0), stop=(c == KC - 1)),
                         [wkv_r] + mnT_r, [pkr], signal=(c == KC - 1))
                qk_norm(nb, pk[:, 0:NMEM], pkr, NMEM, 4, None, khT[:, hd, :], khT_r)
            for mt in range(2):
                pv, pvr = PS()
                for c in range(KC):
                    S.op("pe", lambda e, c=c, mt=mt: e.matmul(pv[:, :], lhsT=mnT[:, c, mt * P:(mt + 1) * P],
                                                               rhs=wkv[:, c, 512:1024], start=(c == 0),
                                                               stop=(c == KC - 1)),
                         [wkv_r] + mnT_r, [pvr], signal=(c == KC - 1))
                S.op("act", lambda e, mt=mt: e.activation(out=vm[:, mt, :], in_=pv[:, :], func=AF.Copy), [pvr], [vm_r])
            qh = [[sb(ph, f"qh_mem{c}_{i}", [P, 512], BF16) for i in range(2)] for c in range(4)]
            qh_r = [[Res(f"qh{c}_{i}") for i in range(2)] for c in range(4)]
            pT = [[sb(ph, f"pT_mem{c}_{i}", [P, 2, 512], BF16) for i in range(2)] for c in range(4)]
            pT_r = [[Res(f"pT{c}_{i}") for i in range(2)] for c in range(4)]
            rz = [sb(ph, f"rz_mem{c}", [P, 512], F32) for c in range(4)]
            rz_r = [Res(f"rz_mem{c}") for c in range(4)]
            ost = [[sb(ph, f"ost_mem{c}_{i}", [P, 512], BF16) for i in range(2)] for c in range(4)]
            ost_r = [[Res(f"ost{c}_{i}") for i in range(2)] for c in range(4)]

            def mem_chain(c):
                it = 0
                for blk in range(L // 512):
                    t0 = blk * 512
                    ob = blk % 2
                    for h2 in range(1):
                        hd = c
                        b = it % 2
                        it += 1
                        pq, pqr = PS(c, 4)
                        proj_fm(pq[:, :], pqr, wq, wq_r, hd * P, P, t0, 512)
                        yield
                        yield from qk_norm_g(nb[c], pq[:, :], pqr, 512, 3, "qscale", qh[c][b][:], qh_r[c][b], c, 4)
                        for mt in range(2):
                            p_s, p_sr = PS(c, 4)
                            S.op("pe", lambda e: e.matmul(p_s[:, :], lhsT=khT[:, hd, mt * P:(mt + 1) * P],
                                                          rhs=qh[c][b][:], start=True, stop=True),
                                 [khT_r, qh_r[c][b]], [p_sr])
                            yield
                            S.op("act", lambda e: e.activation(out=pT[c][b][:, mt, :], in_=p_s[:, :], func=AF.Exp),
                                 [p_sr], [pT_r[c][b]])
                            yield
                        pu, pur = PS(c, 4)
                        pz, pzr = PS(c, 4)
                        for mt in range(2):
                            S.op("pe", lambda e: e.matmul(pu[:, :], lhsT=vm[:, mt, hd * P:(hd + 1) * P],
                                                          rhs=pT[c][b][:, mt, :], start=(mt == 0), stop=(mt == 1)),
                                 [vm_r, pT_r[c][b]], [pur], signal=(mt == 1))
                        for mt in range(2):
                            S.op("pe", lambda e: e.matmul(pz[:, :], lhsT=ones[:], rhs=pT[c][b][:, mt, :],
                                                          start=(mt == 0), stop=(mt == 1)),
                                 [const_r, pT_r[c][b]], [pzr], signal=(mt == 1))
                        yield
                        S.op("act", lambda e: e.activation(out=rz[c][:], in_=pz[:, :], func=AF.Ln), [pzr], [rz_r[c]])
                        yield
                        S.op("act", lambda e: e.activation(out=rz[c][:], in_=rz[c][:], func=AF.Exp, scale=-1.0),
                             [rz_r[c]], [rz_r[c]])
                        yield
                        S.op("dve", lambda e: e.tensor_tensor(out=ost[c][ob][:], in0=pu[:, :], in1=rz[c][:],
                                                              op=ALU.mult), [pur, rz_r[c]], [ost_r[c][ob]])
                        yield
                    dst = mixT_d[1536 + c * P:1536 + (c + 1) * P, t0:t0 + 512]
                    S.dma("pool", dst, ost[c][ob][:], [ost_r[c][ob]], [mix_r[12 + c]], f"ost_mem{c}_{ob}")
                    yield

            run_chains([mem_chain(c_) for c_ in range(4)], stagger=3)
            S.barrier()

    mix_r = [Res(f"mix{i}") for i in range(16)]

    def phase_da(slots=range(4), pre_barrier=None):
        with ExitStack() as ph:
            damask = sb(ph, "damaskb", [P, 12, 2 * P], BF16)
            dst_ = sb(ph, "dastage", [P, 24 * P], F32)
            dst_r = Res("dastage")
            damask_r = Res("damask")
            S.dma("sp", dst_[:], damask_d.rearrange("p a b -> p (a b)"), [], [dst_r], "dastage")
            S.op("dve", lambda e: e.tensor_copy(out=damask[:].rearrange("p a b -> p (a b)"), in_=dst_[:]),
                 [dst_r], [damask_r])
            wqkv = [[sb(ph, f"w_da{k}_{i}", [P, KC, P], BF16) for k in range(3)] for i in range(2)]
            wqkv_r = [[Res(f"w_da{k}_{i}") for k in range(3)] for i in range(2)]
            qhT = sb(ph, "qhT_da", [P, L], BF16)
            khT = sb(ph, "khT_da", [P, L], BF16)
            qk_r = [Res("qhT_da"), Res("khT_da")]
            vtm = sb(ph, "vtm_da", [P, NT, P], BF16)
            vtm_r = Res("vtm_da")
            uz = sb(ph, "uz_da", [P, 2, L], F32)
            uz_r = Res("uz_da")
            nb_ = [(sb(ph, f"sqb_da{i}", [P, 512], BF16), Res(f"sqb_da{i}"), sb(ph, f"rs_da{i}", [P, 512], F32),
                    Res(f"rs_da{i}")) for i in range(3)]
            pex = [sb(ph, f"pex_da{i}", [P, 2, 2, P], BF16) for i in range(2)]
            pex_r = [Res("pex0"), Res("pex1")]
            pm = [sb(ph, f"pm_da{i}", [P, 2, 2, P], BF16) for i in range(2)]
            pm_r = [Res("pm0"), Res("pm1")]
            rz = sb(ph, "rz_da", [P, 512], F32)
            rz_r = Res("rz_da")
            ost = [sb(ph, f"ost_da{i}", [P, 512], BF16) for i in range(2)]
            ost_r = [Res("ost_da0"), Res("ost_da1")]
            hcount = 0
            for slot in slots:
                for g in range(3):
                    head = g * 4 + slot
                    d = DA_CFG[g][1]
                    Ld = L // d
                    nb = Ld // P
                    wi = hcount % 2
                    hcount += 1
                    if hcount == 1:
                        for k in range(3):
                            wload(wqkv[wi][k], wqkv_r[wi][k], w_in_d, OFF_DA + k * 1536 + head * P, P, 0, KC, gain_idx=0)
                    nxt = hcount
                    slots_l = list(slots)
                    if nxt < 3 * len(slots_l):
                        nhead = (nxt % 3) * 4 + slots_l[nxt // 3]
                        for k in range(3):
                            wload(wqkv[nxt % 2][k], wqkv_r[nxt % 2][k], w_in_d, OFF_DA + k * 1536 + nhead * P, P, 0, KC,
                                  gain_idx=0)
                    wq, wk, wv = wqkv[wi]
                    wq_r, wk_r, wv_r = wqkv_r[wi]
                    items = []
                    for blk in range(L // 512):
                        items.append((blk * 512, wq, wq_r, 1, "qscale", qhT, qk_r[0]))
                        items.append((blk * 512, wk, wk_r, 2, None, khT, qk_r[1]))
                    live = {}

                    def qkA(j):
                        t0, w_, w_r_, gcol_i, esc, dstT_, dres = items[j]
                        sqb, sqb_r, rs, rs_r = nb_[j % len(nb_)]
                        pq, pqr = PS()
                        proj_fm(pq[:, :], pqr, w_, w_r_, 0, P, t0, 512)
                        S.op("act", lambda e: e.activation(out=sqb[:, :], in_=pq[:, :], func=AF.Square), [pqr], [sqb_r])
                        live[j] = (pq, pqr)

                    def qkB(j):
                        t0, w_, w_r_, gcol_i, esc, dstT_, dres = items[j]
                        sqb, sqb_r, rs, rs_r = nb_[j % len(nb_)]
                        pq, pqr = live.pop(j)
                        p2, p2r = PS()
                        S.op("pe", lambda e: e.matmul(p2[:, :], lhsT=ones[:], rhs=sqb[:, :], start=True, stop=True),
                             [sqb_r, const_r], [p2r])
                        S.op("act", lambda e: e.activation(out=rs[:, :], in_=p2[:, :], func=AF.Ln, bias=float(EPS),
                                                           scale=1.0 / P), [p2r], [rs_r])
                        S.op("act", lambda e: e.activation(out=rs[:, :], in_=rs[:, :], func=AF.Exp, scale=-0.5),
                             [rs_r], [rs_r])
                        gcol = (hgs if esc == "qscale" else hgains)[:, gcol_i:gcol_i + 1]
                        S.op("dve", lambda e: e.scalar_tensor_tensor(out=dstT_[:, t0:t0 + 512], in0=pq[:, :], scalar=gcol,
                                                                     in1=rs[:, :], op0=ALU.mult, op1=ALU.mult),
                             [pqr, rs_r, const_r], [dres])

                    qkA(0)
                    for j in range(len(items)):
                        if j + 1 < len(items):
                            qkA(j + 1)
                        qkB(j)
                    for bi0 in range(0, NT, 4):
                        pv, pvr = PS()
                        for j in range(4):
                            bi = bi0 + j
                            r, kb = bi // nb, bi % nb
                            tk0 = r + d * kb * P
                            for c in range(KC):
                                lhs = hT[:, c, tk0:tk0 + (P - 1) * d + 1:d]
                                S.op("pe", lambda e, c=c, j=j, lhs=lhs, pv=pv: e.matmul(
                                    pv[:, j * P:(j + 1) * P], lhsT=lhs, rhs=wv[:, c, :], start=(c == 0),
                                    stop=(c == KC - 1)), [wv_r] + hT_r, [pvr], signal=(c == KC - 1 and j == 3))
                        S.op("act", lambda e, bi0=bi0, pv=pv: e.activation(
                            out=vtm[:, bi0:bi0 + 4, :], in_=pv[:, :].rearrange("p (j v) -> p j v", j=4), func=AF.Copy),
                            [pvr], [vtm_r])
                    tiles = []
                    for r in range(d):
                        tiles.append((r, 0, 1))
                        i = 1
                        while i < nb:
                            if i + 1 < nb:
                                tiles.append((r, i, 2))
                                i += 2
                            else:
                                tiles.append((r, i, 1))
                                i += 1
                        tiles.append((r, nb, 1))
                    staged = {}

                    def stage1(idx):
                        r, i, nt_ = tiles[idx]
                        sl = idx % 2
                        p_s, p_sr = PS()
                        if nt_ == 2:
                            tq0 = r + d * (P * i - 64)
                            for tau in range(2):
                                qsl = qhT[:, tq0 + tau * P * d:tq0 + tau * P * d + (P - 1) * d + 1:d]
                                for bb in range(2):
                                    kb = i + tau - 1 + bb
                                    tk0 = r + d * kb * P
                                    ksl = khT[:, tk0:tk0 + (P - 1) * d + 1:d]
                                    S.op("pe", lambda e: e.matmul(
                                        p_s[:, tau * 2 * P + bb * P:tau * 2 * P + (bb + 1) * P], lhsT=ksl, rhs=qsl,
                                        start=True, stop=True), qk_r, [p_sr], signal=(tau == 1 and bb == 1))
                            S.op("act", lambda e: e.activation(out=pex[sl][:].rearrange("p t b a -> p (t b a)"),
                                                               in_=p_s[:, :], func=AF.Exp), [p_sr], [pex_r[sl]])
                            mk = damask[:, head, :].unsqueeze(1).to_broadcast([P, 2, 2 * P])
                            S.op("dve", lambda e: e.tensor_tensor(
                                out=pm[sl][:].rearrange("p t b a -> p t (b a)"),
                                in0=pex[sl][:].rearrange("p t b a -> p t (b a)"), in1=mk, op=ALU.mult),
                                [pex_r[sl], damask_r], [pm_r[sl]])
                            staged[idx] = (r, i, 2, None, tq0, 0, 2, sl)
                            return
                        a0 = 64 if i == 0 else 0
                        a1 = 64 if i == nb else P
                        nq = a1 - a0
                        tq0 = r + d * (P * i - 64 + a0)
                        qsl = qhT[:, tq0:tq0 + (nq - 1) * d + 1:d]
                        b0, b1 = (1 if i == 0 else 0), (1 if i == nb else 2)
                        for bb in range(b0, b1):
                            kb = i - 1 + bb
                            tk0 = r + d * kb * P
                            ksl = khT[:, tk0:tk0 + (P - 1) * d + 1:d]
                            S.op("pe", lambda e, bb=bb, ksl=ksl: e.matmul(
                                p_s[:, bb * P:bb * P + nq], lhsT=ksl, rhs=qsl, start=True, stop=True),
                                qk_r, [p_sr], signal=(bb == b1 - 1))
                        psv = p_s[:, 0:2 * P].rearrange("p (b a) -> p b a", b=2)[:, b0:b1, 0:nq]
                        S.op("act", lambda e: e.activation(out=pex[sl][:, 0, b0:b1, 0:nq], in_=psv, func=AF.Exp),
                             [p_sr], [pex_r[sl]])
                        mk = damask[:, head, :].rearrange("p (b a) -> p b a", b=2)[:, b0:b1, a0:a1]
                        S.op("dve", lambda e: e.tensor_tensor(out=pm[sl][:, 0, b0:b1, 0:nq],
                                                              in0=pex[sl][:, 0, b0:b1, 0:nq],
                                                              in1=mk, op=ALU.mult), [pex_r[sl], damask_r], [pm_r[sl]])
                        staged[idx] = (r, i, 1, nq, tq0, b0, b1, sl)

                    def stage2(idx):
                        r, i, nt_, nq, tq0, b0, b1, sl = staged.pop(idx)
                        pu, pur = PS()
                        if nt_ == 2:
                            for tau in range(2):
                                for bb in range(2):
                                    kb = i + tau - 1 + bb
                                    S.op("pe", lambda e: e.matmul(
                                        pu[:, tau * P:(tau + 1) * P], lhsT=vtm[:, r * nb + kb, :], rhs=pm[sl][:, tau, bb, :],
                                        start=(bb == 0), stop=(bb == 1)), [vtm_r, pm_r[sl]], [pur], signal=False)
                            for tau in range(2):
                                for bb in range(2):
                                    S.op("pe", lambda e: e.matmul(
                                        pu[:, 2 * P + tau * P:2 * P + (tau + 1) * P], lhsT=ones[:], rhs=pm[sl][:, tau, bb, :],
                                        start=(bb == 0), stop=(bb == 1)), [const_r, pm_r[sl]], [pur],
                                        signal=(tau == 1 and bb == 1))
                            puv = pu[:, :].rearrange("p (z a) -> p z a", z=2)
                            uzv = uz[:, :, tq0:tq0 + (2 * P - 1) * d + 1:d]
                        else:
                            for bb in range(b0, b1):
                                kb = i - 1 + bb
                                S.op("pe", lambda e, bb=bb, kb=kb: e.matmul(
                                    pu[:, 0:nq], lhsT=vtm[:, r * nb + kb, :], rhs=pm[sl][:, 0, bb, 0:nq],
                                    start=(bb == b0), stop=(bb == b1 - 1)), [vtm_r, pm_r[sl]], [pur], signal=False)
                            for bb in range(b0, b1):
                                S.op("pe", lambda e, bb=bb: e.matmul(
                                    pu[:, P:P + nq], lhsT=ones[:], rhs=pm[sl][:, 0, bb, 0:nq],
                                    start=(bb == b0), stop=(bb == b1 - 1)), [const_r, pm_r[sl]], [pur],
                                    signal=(bb == b1 - 1))
                            puv = pu[:, 0:2 * P].rearrange("p (b a) -> p b a", b=2)[:, :, 0:nq]
                            uzv = uz[:, :, tq0:tq0 + (nq - 1) * d + 1:d]
                        if g == 0:
                            S.op("act", lambda e: e.activation(out=uzv, in_=puv, func=AF.Copy), [pur], [uz_r])
                        else:
                            S.op("dve", lambda e: e.tensor_tensor(out=uzv, in0=puv, in1=uzv, op=ALU.add),
                                 [pur, uz_r], [uz_r])

                    for idx in range(len(tiles)):
                        stage1(idx)
                        if idx >= 1:
                            stage2(idx - 1)
                    stage2(len(tiles) - 1)
                for blk in range(L // 512):
                    t0 = blk * 512
                    ob = blk % 2
                    S.op("act", lambda e, t0=t0: e.activation(out=rz[:], in_=uz[:, 1, t0:t0 + 512], func=AF.Ln),
                         [uz_r], [rz_r])
                    S.op("act", lambda e: e.activation(out=rz[:], in_=rz[:], func=AF.Exp, scale=-1.0), [rz_r], [rz_r])
                    S.op("dve", lambda e, t0=t0, ob=ob: e.tensor_tensor(out=ost[ob][:], in0=uz[:, 0, t0:t0 + 512],
                                                                        in1=rz[:], op=ALU.mult),
                         [uz_r, rz_r], [ost_r[ob]])
                    S.dma("pool", mixT_d[1024 + slot * P:1024 + (slot + 1) * P, t0:t0 + 512], ost[ob][:],
                          [ost_r[ob]], [mix_r[8 + slot]], f"ost_da{ob}")
            if pre_barrier is not None:
                pre_barrier()
            S.barrier()

    HG_SCALE = float(128 ** -0.5)

    def phase_hg(heads=range(8), wts=None, wts_r=None, first_loaded=False):
        with ExitStack() as ph:
            sqT = sb(ph, "sqT_hg", [P, L], BF16)
            sqT_r = Res("sqT")
            gT = sb(ph, "gT_hg", [P, L], BF16)
            gT_r = Res("gT")
            vtm = sb(ph, "vtm_hg", [P, NT, P], BF16)
            vtm_r = Res("vtm_hg")
            qin = [sb(ph, f"qin_hg{i}", [P, L], BF16) for i in range(2)]
            kin = [sb(ph, f"kin_hg{i}", [P, L], BF16) for i in range(2)]
            qin_r = [Res("qin0"), Res("qin1")]
            kin_r = [Res("kin0"), Res("kin1")]
            SEG = 1024
            T1s = [sb(ph, f"T1_hg{i}", [P, 1 + SEG], F32) for i in range(2)]
            T1s_r = [Res("T1_0"), Res("T1_1")]
            T2s = [sb(ph, f"T2_hg{i}", [P, SEG], F32) for i in range(2)]
            T2s_r = [Res("T2_0"), Res("T2_1")]
            K1s = [sb(ph, f"K1_hg{i}", [P, SEG], BF16) for i in range(2)]
            K1s_r = [Res("K1_0"), Res("K1_1")]
            E1s = [sb(ph, f"E1_hg{i}", [P, SEG], BF16) for i in range(2)]
            E1s_r = [Res("E1_0"), Res("E1_1")]
            E2s, E2s_r = E1s, E1s_r
            st = [sb(ph, f"st_hg{i}", [P, 6, NT], F32) for i in range(2)]
            st_r = [Res("st0"), Res("st1")]
            oacc = sb(ph, "oacc_hg", [P, L], BF16)
            oacc_r = [Res(f"oacc{n}") for n in range(NT)]
            Sst = [[sb(ph, f"S_hg{i}_{j}", [P, P], F32) for j in range(2)] for i in range(2)]
            Sst_r = [[Res(f"S{i}_{j}") for j in range(2)] for i in range(2)]
            Sbf = [[sb(ph, f"Sbf_hg{i}_{j}", [P, P], BF16) for j in range(2)] for i in range(2)]
            Sbf_r = [[Res(f"Sbf{i}_{j}") for j in range(2)] for i in range(2)]
            AT2 = [sb(ph, f"AT2_hg{j}", [P, 2, P], BF16) for j in range(3)]
            AT_r2 = [Res(f"AT2_{j}") for j in range(3)]
            ktm2 = [sb(ph, f"ktm2_hg{j}", [P, 2, P], BF16) for j in range(3)]
            ktm_r2 = [Res(f"ktm2_{j}") for j in range(3)]
            sqbs = [sb(ph, f"sqb_hg{i}", [P, 512], BF16) for i in range(2)]
            sqbs_r = [Res("sqb_hg0"), Res("sqb_hg1")]
            sgs, sgs_r = sqbs, sqbs_r
            rss = [sb(ph, f"rs_hg{i}", [P, 512], F32) for i in range(2)]
            rss_r = [Res("rs_hg0"), Res("rs_hg1")]
            tmpns = [sb(ph, f"tmpn_hg{i}", [P, 512], BF16) for i in range(2)]
            tmpns_r = [Res("tmpn0"), Res("tmpn1")]
            ost = [sb(ph, f"ost_hg{i}", [P, 512], BF16) for i in range(2)]
            ost_r = [Res("ost_hg0"), Res("ost_hg1")]
            for i_ in range(2):
                S.op("dve", lambda e, i_=i_: e.memset(T1s[i_][:, 0:1], 0.0), [], [T1s_r[i_]])
            segc = [0]
            heads = list(heads)

            def loadw(hi_):
                for k in range(5):
                    wload(wts[hi_ % 2][k], wts_r[hi_ % 2][k], w_in_d, OFF_HG + k * 1024 + heads[hi_] * P, P, 0, KC,
                          gain_idx=0)

            if not first_loaded:
                loadw(0)
            for hi, h in enumerate(heads):
                wi = hi % 2
                if hi + 1 < len(heads):
                    loadw(hi + 1)
                W_, W_r = wts[wi], wts_r[wi]
                for blk in range(L // 512):
                    t0 = blk * 512
                    pq, pqr = PS()
                    proj_fm(pq[:, :], pqr, W_[0], W_r[0], 0, P, t0, 512)
                    sgi = blk % 2
                    S.op("act", lambda e, pq=pq, sgi=sgi: e.activation(out=sgs[sgi][:], in_=pq[:, :], func=AF.Sigmoid),
                         [pqr], [sgs_r[sgi]])
                    S.op("dve", lambda e, t0=t0, pq=pq, sgi=sgi: e.tensor_tensor(
                        out=sqT[:, t0:t0 + 512], in0=pq[:, :], in1=sgs[sgi][:], op=ALU.mult),
                        [pqr, sgs_r[sgi]], [sqT_r])

                def gv_chain():
                    for n0 in range(0, NT, 4):
                        pv, pvr = PS(2, 4)
                        for j in range(4):
                            for c in range(KC):
                                S.op("pe", lambda e: e.matmul(
                                    pv[:, j * P:(j + 1) * P], lhsT=hT[:, c, (n0 + j) * P:(n0 + j + 1) * P],
                                    rhs=W_[3][:, c, :], start=(c == 0), stop=(c == KC - 1)),
                                    [W_r[3]] + hT_r[n0:n0 + 4], [pvr], signal=(c == KC - 1 and j == 3))
                        yield
                        S.op("dve", lambda e: e.tensor_copy(
                            out=vtm[:, n0:n0 + 4, :], in_=pv[:, :].rearrange("p (j v) -> p j v", j=4)),
                            [pvr], [vtm_r])
                        yield
                        t0 = (n0 // 4) * 512
                        pg, pgr = PS(3, 4)
                        proj_fm(pg[:, :], pgr, W_[4], W_r[4], 0, P, t0, 512)
                        yield
                        sgi = (n0 // 4) % 2
                        S.op("act", lambda e: e.activation(out=sgs[sgi][:], in_=pg[:, :], func=AF.Sigmoid),
                             [pgr], [sgs_r[sgi]])
                        yield
                        S.op("dve", lambda e: e.tensor_tensor(out=gT[:, t0:t0 + 512], in0=pg[:, :], in1=sgs[sgi][:],
                                                              op=ALU.mult), [pgr, sgs_r[sgi]], [gT_r])
                        yield

                def pre_chain(dr):
                    sgn = 1.0 if dr == 0 else -1.0
                    lbc = lbv[:, dr, h:h + 1]
                    omc = oml[:, dr, h:h + 1]
                    nomc = noml[:, dr, h:h + 1]
                    T1, T1_r, T2, T2_r = T1s[dr], T1s_r[dr], T2s[dr], T2s_r[dr]
                    K1, K1_r, E1, E1_r, E2, E2_r = K1s[dr], K1s_r[dr], E1s[dr], E1s_r[dr], E2s[dr], E2s_r[dr]
                    T1d = T1[:, 1:1 + SEG]
                    NTS = SEG // P
                    for seg in range(L // SEG):
                        s0 = seg * SEG
                        for b2 in range(SEG // 512):
                            pf, pfr = PS(dr, 4)
                            proj_fm(pf[:, :], pfr, W_[1 + dr], W_r[1 + dr], 0, P, s0 + b2 * 512, 512)
                            yield
                            S.op("act", lambda e: e.activation(out=T1[:, 1 + b2 * 512:1 + (b2 + 1) * 512], in_=pf[:, :],
                                                               func=AF.Sigmoid), [pfr], [T1_r])
                            yield
                        S.op("dve", lambda e: e.tensor_scalar(out=K1[:], in0=T1d, scalar1=nomc, scalar2=omc,
                                                              op0=ALU.mult, op1=ALU.add), [T1_r, const_r], [K1_r])
                        S.op("dve", lambda e: e.tensor_scalar(out=T1d, in0=T1d, scalar1=omc, scalar2=lbc,
                                                              op0=ALU.mult, op1=ALU.add), [T1_r, const_r], [T1_r])
                        yield
                        S.op("act", lambda e: e.activation(out=T1d, in_=T1d, func=AF.Ln), [T1_r], [T1_r])
                        yield
                        if dr == 0:
                            S.op("dve", lambda e: e.tensor_tensor_scan(out=T2[:], data0=rmask[:, 0:SEG], data1=T1d,
                                                                       initial=0.0, op0=ALU.mult, op1=ALU.add),
                                 [T1_r, const_r], [T2_r])
                        else:
                            S.op("dve", lambda e: e.tensor_tensor_scan(out=T2[:], data0=T1[:, 0:SEG],
                                                                       data1=rmask[:, 0:SEG],
                                                                       initial=0.0, op0=ALU.add, op1=ALU.mult),
                                 [T1_r, const_r], [T2_r])
                        yield
                        T2v = T2[:].rearrange("p (n t) -> p n t", t=P)
                        T1v = T1d.rearrange("p (n t) -> p n t", t=P)
                        ns = slice(seg * NTS, (seg + 1) * NTS)
                        if dr == 0:
                            S.op("dve", lambda e: e.tensor_copy(out=st[dr][:, 1, ns], in_=T2v[:, :, P - 1]),
                                 [T2_r], [st_r[dr]])
                        else:
                            S.op("dve", lambda e: e.tensor_tensor(out=st[dr][:, 1, ns], in0=T2v[:, :, P - 1],
                                                                  in1=T1v[:, :, P - 1], op=ALU.add),
                                 [T2_r, T1_r], [st_r[dr]])
                        yield
                        if dr == 0:
                            S.op("dve", lambda e: e.tensor_tensor(
                                out=T2v, in0=T2v, in1=st[dr][:, 1, ns].unsqueeze(2).to_broadcast([P, NTS, P]),
                                op=ALU.subtract), [T2_r, st_r[dr]], [T2_r])
                        yield
                        S.op("act", lambda e: e.activation(out=E1[:], in_=T2[:], func=AF.Exp, scale=sgn), [T2_r], [E1_r])
                        yield
                        S.op("dve", lambda e: e.scalar_tensor_tensor(
                            out=qin[dr][:, s0:s0 + SEG], in0=sqT[:, s0:s0 + SEG], scalar=HG_SCALE, in1=E1[:],
                            op0=ALU.mult, op1=ALU.mult), [sqT_r, E1_r], [qin_r[dr]])
                        yield
                        S.op("act", lambda e: e.activation(out=E2[:], in_=T2[:], func=AF.Exp, scale=-sgn), [T2_r], [E2_r])
                        yield
                        S.op("dve", lambda e: e.tensor_tensor(out=kin[dr][:, s0:s0 + SEG], in0=K1[:], in1=E2[:],
                                                              op=ALU.mult), [K1_r, E2_r], [kin_r[dr]])
                        yield
                    S.op("act", lambda e: e.activation(out=st[dr][:, 5, :], in_=st[dr][:, 1, :], func=AF.Exp),
                         [st_r[dr]], [st_r[dr]])
                    yield

                run_chains([pre_chain(0), pre_chain(1), gv_chain()], stagger=HG_STAG)
                touched = set()

                pMs = {}
                oacc3 = oacc[:].rearrange("p (n t) -> p n t", t=P)

                def tile_of(step, dr):
                    return step if dr == 0 else NT - 1 - step

                def stageA1(step):
                    sl = step % 3
                    last = step == NT - 1
                    pA, pAr = PS()
                    for dr in range(2):
                        n = tile_of(step, dr)
                        ts_ = slice(n * P, (n + 1) * P)
                        S.op("pe", lambda e: e.matmul(pA[:, dr * P:(dr + 1) * P], lhsT=kin[dr][:, ts_], rhs=qin[dr][:, ts_],
                                                      start=True, stop=True), [kin_r[dr], qin_r[dr]], [pAr],
                             signal=(dr == 1))
                    S.op("dve", lambda e: e.tensor_tensor(out=AT2[sl][:], in0=pA[:, 0:2 * P].rearrange("p (d t) -> p d t", d=2),
                                                          in1=cmask[:], op=ALU.mult), [pAr, const_r], [AT_r2[sl]])
                    pT, pTr = PS()
                    if not last:
                        pTb = pT[:].bitcast(BF16)
                        for dr in range(2):
                            n = tile_of(step, dr)
                            ts_ = slice(n * P, (n + 1) * P)
                            S.op("pe", lambda e: e.transpose(out=pTb[:, dr * P:(dr + 1) * P], in_=kin[dr][:, ts_],
                                                             identity=ident[:]), [kin_r[dr], const_r], [pTr],
                                 signal=(dr == 1))
                        S.op("act", lambda e: e.activation(out=ktm2[sl][:],
                                                           in_=pTb[:, 0:2 * P].rearrange("p (d t) -> p d t", d=2),
                                                           func=AF.Copy), [pTr], [ktm_r2[sl]])

                def stageA2(step):
                    sl = step % 3
                    last = step == NT - 1
                    pM, pMr = PS()
                    if not last:
                        for dr in range(2):
                            n = tile_of(step, dr)
                            S.op("pe", lambda e: e.matmul(pM[:, dr * P:(dr + 1) * P], lhsT=ktm2[sl][:, dr, :], rhs=vtm[:, n, :],
                                                          start=True, stop=True), [ktm_r2[sl], vtm_r], [pMr],
                                 signal=(dr == 1))
                        pMs[step] = (pM, pMr)

                def stageB1(step):
                    sb_ = step % 2
                    first, last = step == 0, step == NT - 1
                    if not last:
                        pM, pMr = pMs.pop(step)
                    for dr in range(2):
                        n = tile_of(step, dr)
                        so, sn = (step - 1) % 2, step % 2
                        if not first:
                            S.op("act", lambda e: e.activation(out=Sbf[dr][sb_][:], in_=Sst[dr][so][:], func=AF.Copy,
                                                               scale=st[dr][:, 5, n:n + 1]),
                                 [Sst_r[dr][so], st_r[dr]], [Sbf_r[dr][sb_]])
                        if not last:
                            pMd = pM[:, dr * P:(dr + 1) * P]
                            if first:
                                S.op("dve", lambda e: e.tensor_copy(out=Sst[dr][sn][:], in_=pMd), [pMr], [Sst_r[dr][sn]])
                            else:
                                S.op("dve", lambda e: e.scalar_tensor_tensor(
                                    out=Sst[dr][sn][:], in0=Sst[dr][so][:], scalar=st[dr][:, 5, n:n + 1], in1=pMd,
                                    op0=ALU.mult, op1=ALU.add), [pMr, st_r[dr], Sst_r[dr][so]], [Sst_r[dr][sn]])

                def stageB2(step):
                    sl = step % 3
                    sb_ = step % 2
                    first = step == 0
                    early = step < NT // 2
                    pO, pOr = PS()
                    for dr in range(2):
                        n = tile_of(step, dr)
                        ts_ = slice(n * P, (n + 1) * P)
                        half = dr if early else 1 - dr
                        po = pO[:, half * P:(half + 1) * P]
                        S.op("pe", lambda e: e.matmul(po, lhsT=vtm[:, n, :], rhs=AT2[sl][:, dr, :], start=True, stop=first),
                             [vtm_r, AT_r2[sl]], [pOr], signal=(first and dr == 1))
                        if not first:
                            S.op("pe", lambda e: e.matmul(po, lhsT=Sbf[dr][sb_][:], rhs=qin[dr][:, ts_],
                                                          start=False, stop=True),
                                 [Sbf_r[dr][sb_], qin_r[dr]], [pOr], signal=(dr == 1))
                    lo, hi_ = (step, NT - 1 - step) if early else (NT - 1 - step, step)
                    ov = oacc3[:, lo:hi_ + 1:hi_ - lo, :]
                    pv2 = pO[:, 0:2 * P].rearrange("p (d t) -> p d t", d=2)
                    orr = [oacc_r[lo], oacc_r[hi_]]
                    if early:
                        S.op("act", lambda e: e.activation(out=ov, in_=pv2, func=AF.Copy), [pOr], orr)
                    else:
                        S.op("dve", lambda e: e.tensor_tensor(out=ov, in0=pv2, in1=ov, op=ALU.add), [pOr] + orr, orr)

                ps_i[0] = 0
                PS()
                stageA1(0)
                PS()
                PS()
                stageA1(1)
                stageA2(0)
                for step in range(NT):
                    stageB1(step)
                    if step >= 1:
                        stageB2(step - 1)
                    else:
                        PS()
                    if step + 2 < NT:
                        stageA1(step + 2)
                    else:
                        PS()
                        PS()
                    if step + 1 < NT:
                        stageA2(step + 1)
                    else:
                        PS()
                stageB2(NT - 1)
                plive = {}

                def postA(blk):
                    t0 = blk * 512
                    ob = blk % 2
                    orr = oacc_r[blk * 4:blk * 4 + 4]
                    S.op("act", lambda e: e.activation(out=sqbs[ob][:], in_=oacc[:, t0:t0 + 512], func=AF.Square),
                         orr, [sqbs_r[ob]])
                    p2, p2r = PS()
                    S.op("pe", lambda e: e.matmul(p2[:, :], lhsT=ones[:], rhs=sqbs[ob][:], start=True, stop=True),
                         [sqbs_r[ob], const_r], [p2r])
                    plive[blk] = (p2, p2r)

                def postB(blk):
                    t0 = blk * 512
                    ob = blk % 2
                    orr = oacc_r[blk * 4:blk * 4 + 4]
                    p2, p2r = plive.pop(blk)
                    rs, rs_r, tmpn, tmpn_r = rss[ob], rss_r[ob], tmpns[ob], tmpns_r[ob]
                    S.op("act", lambda e: e.activation(out=rs[:], in_=p2[:, :], func=AF.Ln, bias=float(EPS),
                                                       scale=1.0 / P), [p2r], [rs_r])
                    S.op("act", lambda e: e.activation(out=rs[:], in_=rs[:], func=AF.Exp, scale=-0.5), [rs_r], [rs_r])
                    S.op("dve", lambda e: e.tensor_tensor(out=tmpn[:], in0=oacc[:, t0:t0 + 512], in1=rs[:],
                                                          op=ALU.mult), orr + [rs_r], [tmpn_r])
                    S.op("dve", lambda e: e.scalar_tensor_tensor(
                        out=ost[ob][:], in0=tmpn[:], scalar=hgains[:, 0:1], in1=gT[:, t0:t0 + 512],
                        op0=ALU.mult, op1=ALU.mult), [tmpn_r, gT_r, const_r], [ost_r[ob]])
                    S.dma("pool", mixT_d[h * P:(h + 1) * P, t0:t0 + 512], ost[ob][:], [ost_r[ob]], [mix_r[h]],
                          f"ost_hg{ob}")

                postA(0)
                for blk in range(L // 512):
                    if blk + 1 < L // 512:
                        postA(blk + 1)
                    postB(blk)
            S.barrier()


    def phase_gates(extra_loads=None):
        with ExitStack() as ph:
            for x_ in range(1):
                wst.append(sb(ph, f"wstx_g{x_}", [P, WCAP], F32))
                wst_r.append(Res(f"wstx_g{x_}"))
            wg = sb(ph, "wg", [P, KC, 3072], BF16)
            wg_col_r = [Res(f"wgc{i}") for i in range(8)] + [Res("wg1")] * 8 + [Res("wg2")] * 8
            for g8 in range(8):
                wload(wg, wg_col_r[g8], w_in_d, OFF_GATE + g8 * P, P, 0, KC, gain_idx=0, dst_col0=g8 * P)
            for ci in range(1, 3):
                wload_big(wg, wg_col_r[ci * 8], w_in_d, OFF_GATE + ci * 1024, 1024, KC, gain_idx=0, dst_col0=ci * 1024)
            if extra_loads is not None:
                extra_loads()
            sg = [sb(ph, f"sg{i}", [P, 4, 512], BF16) for i in range(2)]
            sg_r = [Res("sg0"), Res("sg1")]
            it = 0
            for ci in range(3):
                for blk in range(L // 512):
                    t0 = blk * 512
                    for fc0 in range(ci * 8, ci * 8 + 8, 4):
                        b = it % 2
                        it += 1
                        for j in range(4):
                            pg, pgr = PS()
                            proj_fm(pg[:, :], pgr, wg, wg_col_r[fc0 + j], (fc0 + j) * P, P, t0, 512)
                            S.op("act", lambda e, j=j, pg=pg: e.activation(out=sg[b][:, j, :], in_=pg[:, :],
                                                                           func=AF.Sigmoid), [pgr], [sg_r[b]])
                        dst = gate_d[fc0 * P:(fc0 + 4) * P, t0:t0 + 512].rearrange("(c p) t -> p c t", p=P)
                        S.dma("pool", dst, sg[b][:], [sg_r[b]], [gate_dr], f"sg{b}")
            S.barrier()
            del wst[NWST:]
            del wst_r[NWST:]

    gate_dr = Res("gate_d")
    out_r = [Res(f"out{n}") for n in range(NT)]
    h2_r = [Res(f"h2_{n}") for n in range(NT)]

    def phase_m1(wp, wp_r, wo, wo_r, extra_loads=None):
        with ExitStack() as ph:
            mixb = [sb(ph, f"mixb{i}", [P, 16, 512], BF16) for i in range(2)]
            mixb_r = [Res("mixb0"), Res("mixb1")]
            gtb = [sb(ph, f"gtb{i}", [P, 24, 512], BF16) for i in range(2)]
            gtb_r = [Res("gtb0"), Res("gtb1")]
            mg = sb(ph, "mg", [P, KC, 512], BF16)
            mg_r = Res("mg")
            macc = [sb(ph, f"macc{i}", [P, 512], BF16) for i in range(2)]
            macc_r = [Res("macc0"), Res("macc1")]
            mt = [sb(ph, f"mt{i}", [P, 512], BF16) for i in range(2)]
            mt_r = [Res("mt0"), Res("mt1")]
            xin = [sb(ph, f"m1_xin{i}", [P, D], F32) for i in range(2)]
            xin_r = [Res("m1xin0"), Res("m1xin1")]
            x1 = [sb(ph, f"m1_x1{i}", [P, D], F32) for i in range(2)]
            x1_r = [Res("m1x10"), Res("m1x11")]
            junk = sb(ph, "m1_junk", [P, D], BF16)
            junk_r = Res("m1junk")
            hb = [sb(ph, f"m1_hb{i}", [P, D], BF16) for i in range(2)]
            hb_r = [Res("m1hb0"), Res("m1hb1")]
            ss = sb(ph, "m1_ss", [P, 2 * NT], F32)
            ss_r = [Res(f"m1ss{n}") for n in range(NT)]
            h2s = [sb(ph, f"m1_h2s{i}", [P, KC, P], BF16) for i in range(2)]
            h2s_r = [Res("h2s0"), Res("h2s1")]
            def load_blk(blk):
                t0 = blk * 512
                sl = blk % 2
                for c4 in range(0, 16, 4):
                    S.dma("sp", mixb[sl][:, c4:c4 + 4, :],
                          mixT_d[c4 * P:(c4 + 4) * P, t0:t0 + 512].rearrange("(c p) t -> p c t", p=P),
                          mix_r, [mixb_r[sl]], f"mixb{sl}")
                for c4 in range(0, 24, 4):
                    S.dma("sp", gtb[sl][:, c4:c4 + 4, :],
                          gate_d[c4 * P:(c4 + 4) * P, t0:t0 + 512].rearrange("(c p) t -> p c t", p=P),
                          [gate_dr], [gtb_r[sl]], f"gtb{sl}")

            def load_x(n):
                S.dma("sp", xin[n % 2][:], x_d[n * P:(n + 1) * P, :], [], [xin_r[n % 2]], f"m1xin{n % 2}")

            def tile_stage2(n):
                i = n % 2
                pt, ptr = PS()
                ptb = pt[:].bitcast(BF16)
                for c in range(KC):
                    S.op("pe", lambda e, c=c: e.transpose(out=ptb[:, c * P:(c + 1) * P],
                                                          in_=hb[i][:, c * P:(c + 1) * P], identity=ident[:]),
                         [hb_r[i], const_r], [ptr], signal=(c == KC - 1))
                S.op("act", lambda e: e.activation(out=h2s[i][:], in_=ptb.rearrange("p (c t) -> p c t", c=KC),
                                                   func=AF.Copy), [ptr], [h2s_r[i]])
                for c4 in range(0, KC, 4):
                    S.dma("pool", h2T_d[c4 * P:(c4 + 4) * P, n * P:(n + 1) * P].rearrange("(c p) t -> p c t", p=P),
                          h2s[i][:, c4:c4 + 4, :], [h2s_r[i]], [h2_r[n]], f"h2s{i}")

            it = 0
            pending = None
            load_blk(0)
            load_x(0)
            if extra_loads is not None:
                extra_loads()
            for blk in range(L // 512):
                t0 = blk * 512
                sl = blk % 2
                if blk + 1 < L // 512:
                    load_blk(blk + 1)
                for oc in range(KC):
                    b = it % 2
                    it += 1
                    for br, (kc0, nk) in enumerate(((0, 8), (8, 4), (12, 4))):
                        pp, ppr = PS()
                        for k in range(nk):
                            S.op("pe", lambda e, k=k, kc0=kc0, pp=pp: e.matmul(
                                pp[:, :], lhsT=wp[:, kc0 + k, oc * P:(oc + 1) * P], rhs=mixb[sl][:, kc0 + k, :],
                                start=(k == 0), stop=(k == nk - 1)), [wp_r, mixb_r[sl]], [ppr], signal=(k == nk - 1))
                        gate = gtb[sl][:, br * 8 + oc, :]
                        if br == 0:
                            S.op("dve", lambda e, pp=pp: e.tensor_tensor(out=macc[b][:], in0=pp[:, :], in1=gate, op=ALU.mult),
                                 [ppr, gtb_r[sl]], [macc_r[b]])
                        else:
                            S.op("dve", lambda e, pp=pp: e.tensor_tensor(out=mt[b][:], in0=pp[:, :], in1=gate, op=ALU.mult),
                                 [ppr, gtb_r[sl]], [mt_r[b]])
                            if br == 1:
                                S.op("dve", lambda e: e.tensor_tensor(out=macc[b][:], in0=macc[b][:], in1=mt[b][:],
                                                                      op=ALU.add), [macc_r[b], mt_r[b]], [macc_r[b]])
                            else:
                                S.op("dve", lambda e: e.tensor_tensor(out=mg[:, oc, :], in0=macc[b][:], in1=mt[b][:],
                                                                      op=ALU.add), [macc_r[b], mt_r[b]], [mg_r])
                    if oc == 3 and pending is not None:
                        tile_stage2(pending)
                        pending = None
                for tt in range(4):
                    n = blk * 4 + tt
                    i = n % 2
                    for half in range(2):
                        py, pyr = PS()
                        for k in range(KC):
                            S.op("pe", lambda e, k=k, py=py: e.matmul(
                                py[:, :], lhsT=mg[:, k, tt * P:(tt + 1) * P], rhs=wo[:, k, half * 512:(half + 1) * 512],
                                start=(k == 0), stop=(k == KC - 1)), [mg_r, wo_r], [pyr], signal=(k == KC - 1))
                        S.op("dve", lambda e, py=py: e.tensor_tensor(
                            out=x1[i][:, half * 512:(half + 1) * 512], in0=py[:, :], in1=xin[i][:, half * 512:(half + 1) * 512],
                            op=ALU.add), [pyr, xin_r[i]], [x1_r[i]])
                    if n + 1 < NT:
                        load_x(n + 1)
                    S.dma("pool", out_d[n * P:(n + 1) * P, :], x1[i][:], [x1_r[i]], [out_r[n]], f"m1x1{i}")
                    S.op("act", lambda e: e.activation(out=junk[:], in_=x1[i][:], func=AF.Square,
                                                       accum_out=ss[:, 2 * n:2 * n + 1]), [x1_r[i]], [junk_r, ss_r[n]])
                    S.op("act", lambda e: e.activation(out=ss[:, 2 * n + 1:2 * n + 2], in_=ss[:, 2 * n:2 * n + 1],
                                                       func=AF.Sqrt, bias=float(EPS), scale=1.0 / D), [ss_r[n]], [ss_r[n]])
                    S.op("dve", lambda e: e.reciprocal(out=ss[:, 2 * n + 1:2 * n + 2], in_=ss[:, 2 * n + 1:2 * n + 2]),
                         [ss_r[n]], [ss_r[n]])
                    S.op("dve", lambda e: e.tensor_scalar(out=hb[i][:], in0=x1[i][:], scalar1=ss[:, 2 * n + 1:2 * n + 2],
                                                          scalar2=None, op0=ALU.mult), [x1_r[i], ss_r[n]], [hb_r[i]])
                    if pending is not None:
                        tile_stage2(pending)
                    pending = n
            tile_stage2(pending)
            S.barrier()

    def phase_m2(wfi_pre, wfi_rs):
        NF = DFF // P
        with ExitStack() as ph:
            for x_ in range(2):
                wst.append(sb(ph, f"wstx_m2{x_}", [P, WCAP], F32))
                wst_r.append(Res(f"wstx_m2{x_}"))
            wfi_c = dict(wfi_pre)
            for ci in (2, 3, 1, 4, 5):
                wfi_c[ci] = sb(ph, f"wfi_c{ci}", [P, KC, min(1024, 2 * DFF - ci * 1024)], BF16)

            def wfcol(col):
                return wfi_c[col // 1024], col % 1024
            wfo = sb(ph, "wfo", [P, NF, D], BF16)
            wfo_rs = [Res(f"wfo{i}") for i in range(NF // 2)]
            h2b = [sb(ph, f"h2b{i}", [P, KC, 512], BF16) for i in range(2)]
            h2b_r = [Res("h2b0"), Res("h2b1")]
            uT = sb(ph, "uT", [P, NF, 512], BF16)
            uT_r = Res("uT")
            sa = [sb(ph, f"sa{i}", [P, 512], BF16) for i in range(2)]
            sa_r = [Res("sa0"), Res("sa1")]
            x1in = [sb(ph, f"m2_x1{i}", [P, D], F32) for i in range(2)]
            x1in_r = [Res("m2x10"), Res("m2x11")]
            def load_h2(blk):
                t0 = blk * 512
                sl = blk % 2
                for c4 in range(0, KC, 4):
                    S.dma("sp", h2b[sl][:, c4:c4 + 4, :],
                          h2T_d[c4 * P:(c4 + 4) * P, t0:t0 + 512].rearrange("(c p) t -> p c t", p=P),
                          h2_r[blk * 4:blk * 4 + 4], [h2b_r[sl]], f"h2b{sl}")

            def load_x1(n):
                S.dma("sp", x1in[n % 2][:], out_d[n * P:(n + 1) * P, :], [out_r[n]], [x1in_r[n % 2]], f"m2x1{n % 2}")

            it = 0
            load_h2(0)
            load_x1(0)
            for ci in (2, 3, 1, 4, 5):
                wload_big(wfi_c[ci], wfi_rs[ci], w_fin_d, ci * 1024, min(1024, 2 * DFF - ci * 1024), KC, gain_idx=2)
            for k2 in range(NF // 2):
                for k1_ in range(2):
                    wload(wfo, wfo_rs[k2], w_fout_d, 0, D, 2 * k2 + k1_, 1, None, dst_kc0=2 * k2 + k1_, eng="dve")
            for blk in range(L // 512):
                t0 = blk * 512
                sl = blk % 2
                if blk + 1 < L // 512:
                    load_h2(blk + 1)
                for fc in range(NF):
                    b = it % 2
                    it += 1
                    pa, par = PS()
                    pb, pbr = PS()
                    for c in range(KC):
                        wa_, ca_ = wfcol(fc * P)
                        S.op("pe", lambda e, c=c, pa=pa: e.matmul(pa[:, :], lhsT=wa_[:, c, ca_:ca_ + P],
                                                                  rhs=h2b[sl][:, c, :], start=(c == 0), stop=(c == KC - 1)),
                             [wfi_rs[(fc * P) // 1024], h2b_r[sl]], [par], signal=(c == KC - 1))
                    for c in range(KC):
                        wb_, cb_ = wfcol(DFF + fc * P)
                        S.op("pe", lambda e, c=c, pb=pb: e.matmul(pb[:, :], lhsT=wb_[:, c, cb_:cb_ + P],
                                                                  rhs=h2b[sl][:, c, :], start=(c == 0), stop=(c == KC - 1)),
                             [wfi_rs[(DFF + fc * P) // 1024], h2b_r[sl]], [pbr], signal=(c == KC - 1))
                    S.op("act", lambda e, pa=pa: e.activation(out=sa[b][:], in_=pa[:, :], func=AF.Silu), [par], [sa_r[b]])
                    S.op("dve", lambda e, pb=pb: e.tensor_tensor(out=uT[:, fc, :], in0=pb[:, :], in1=sa[b][:], op=ALU.mult),
                         [pbr, sa_r[b]], [uT_r])
                for tt in range(4):
                    n = blk * 4 + tt
                    i = n % 2
                    for half in range(2):
                        py, pyr = PS()
                        for k in range(NF):
                            S.op("pe", lambda e, k=k, py=py: e.matmul(
                                py[:, :], lhsT=uT[:, k, tt * P:(tt + 1) * P], rhs=wfo[:, k, half * 512:(half + 1) * 512],
                                start=(k == 0), stop=(k == NF - 1)), [uT_r, wfo_rs[k // 2]], [pyr], signal=(k == NF - 1))
                        S.op("dve", lambda e, py=py: e.tensor_tensor(
                            out=x1in[i][:, half * 512:(half + 1) * 512], in0=py[:, :],
                            in1=x1in[i][:, half * 512:(half + 1) * 512], op=ALU.add), [pyr, x1in_r[i]], [x1in_r[i]])
                    if n + 1 < NT:
                        load_x1(n + 1)
                    S.dma("pool", out_d[n * P:(n + 1) * P, :], x1in[i][:], [x1in_r[i]], [out_r[n]], f"m2o{i}")
            S.barrier()
            del wst[NWST:]
            del wst_r[NWST:]

    if "mem" in phases:
        phase_mem()
    memw.close()
    hgw = ExitStack()
    hg_heads = list(range(8) if not debug_heads else debug_heads)
    wts_g = [[sb(hgw, f"w_hg{k}_{i}", [P, KC, P], BF16) for k in range(5)] for i in range(2)]
    wts_g_r = [[Res(f"w_hg{k}_{i}") for k in range(5)] for i in range(2)]

    def hg_first_weights():
        for k in range(5):
            wload(wts_g[0][k], wts_g_r[0][k], w_in_d, OFF_HG + k * 1024 + hg_heads[0] * P, P, 0, KC, gain_idx=0)

    pre_hg = hg_first_weights if ("hg" in phases and "da" in phases) else None
    if "da" in phases:
        phase_da(range(4) if not debug_slots else debug_slots, pre_hg)
    if "hg" in phases:
        phase_hg(hg_heads, wts_g, wts_g_r, first_loaded=(pre_hg is not None))
    hgw.close()
    if "tail" in phases:
        m2pre = ExitStack()
        wfi_pre = {ci: sb(m2pre, f"wfi_c{ci}", [P, KC, 1024], BF16) for ci in (0,)}
        wfi_rs = [Res(f"wfi{i}") for i in range(6)]

        def m2_preloads():
            for ci in (0,):
                wload_big(wfi_pre[ci], wfi_rs[ci], w_fin_d, ci * 1024, 1024, KC, gain_idx=2)

        m1w = ExitStack()
        wp = sb(m1w, "wp", [P, 16, D], BF16)
        wp_r = Res("wp")
        wo = sb(m1w, "wo", [P, KC, D], BF16)
        wo_r = Res("wo")

        def m1_loads():
            wload_big(wp, wp_r, w_phg_d, 0, D, 8, eng="dve")
            wload_big(wp, wp_r, w_pda_d, 0, D, 4, dst_kc0=8, eng="dve")
            wload_big(wp, wp_r, w_pmem_d, 0, D, 4, dst_kc0=12, eng="dve")
            wload_big(wo, wo_r, w_out_d, 0, D, KC, eng="dve")

        phase_gates(m1_loads)
    S.barrier()
    hstack.close()
    if "tail" in phases:
        phase_m1(wp, wp_r, wo, wo_r, m2_preloads)
        m1w.close()
        phase_m2(wfi_pre, wfi_rs)
        m2pre.close()

    S.barrier()
    es.close()
    print("kernel build: ops", S.nops, "sems", S.nsem)
    return nc


_CACHE = {}


def _consts():
    ident = np.eye(P, dtype=np.float32)
    s = np.arange(P)[:, None]
    t = np.arange(P)[None, :]
    cmask = np.stack([(s <= t), (s >= t)], axis=1).astype(np.float32)
    damask = np.zeros((P, 12, 2 * P), np.float64)
    b = np.arange(P)[:, None].astype(np.float64)
    a = np.arange(P)[None, :].astype(np.float64)
    for h in range(12):
        d = DA_CFG[h // 4][1]
        relA = b - a - 64.0
        relB = b - a + 64.0
        damask[:, h, 0:P] = (b >= a) * np.exp(-SLOPES[h] * d * np.abs(relA))
        damask[:, h, P:2 * P] = (b <= a) * np.exp(-SLOPES[h] * d * np.abs(relB))
    rmask = np.ones((P, 1024), np.float32)
    rmask[:, ::P] = 0.0
    return ident, cmask, damask.astype(np.float32), rmask


def make_in_maps(inputs):
    ident, cmask, damask, rmask = _consts()
    f = lambda k: np.ascontiguousarray(np.asarray(inputs[k], dtype=np.float32))
    gains = np.stack([f("norm_mix_gain")[0], f("norm_mem_gain")[0], f("norm_ffn_gain")[0]], 0)
    gainsT = np.ascontiguousarray(gains.reshape(3, KC, P).transpose(2, 0, 1))
    hgains = np.ascontiguousarray(np.stack([f("hg_norm_gain")[0], f("da_q_gain")[0], f("da_k_gain")[0],
                                            f("mem_q_gain")[0], f("mem_k_gain")[0]], 1))
    lb = np.concatenate([f("lb_logits_fw"), f("lb_logits_bw")], 0)
    lbl = np.ascontiguousarray(lb.reshape(4, 8, P).transpose(2, 0, 1))
    shared = {
        "w_in": f("w_in")[0], "w_mem_kv": f("w_mem_kv")[0], "w_proj_hg": f("w_proj_hg")[0],
        "w_proj_da": f("w_proj_da")[0], "w_proj_mem": f("w_proj_mem")[0], "w_out": f("w_out")[0],
        "w_ffn_in": f("w_ffn_in")[0], "w_ffn_out": f("w_ffn_out")[0],
        "gainsT": gainsT, "hgains": hgains, "lbl": lbl, "cmask": cmask, "damask": damask,
        "rmask": rmask, "ident": ident,
    }
    x = f("x")
    mem = f("mem")
    return [dict(shared, x=x[b], mem=mem[b]) for b in range(8)]


def kernel(**inputs):
    if "nc" not in _CACHE:
        _CACHE["nc"] = build()
    nc = _CACHE["nc"]
    in_maps = make_in_maps(inputs)
    res = run_bass_kernel_spmd(nc, in_maps, core_ids=list(range(8)))
    return np.stack([np.asarray(r["out"], dtype=np.float32) for r in res.results], 0)
```

```python
import numpy as np
from contextlib import ExitStack
import concourse.bass as bass
import concourse.mybir as mybir
from concourse.bass_utils import run_bass_kernel_spmd

F32 = mybir.dt.float32
BF16 = mybir.dt.bfloat16
AF = mybir.ActivationFunctionType
ALU = mybir.AluOpType
AX = mybir.AxisListType

P = 128
L = 4096
D = 1024
KC = 8
NT = L // P
NMEM = 256
DFF = 2816
EPS = 1e-6
IN_COLS = 13312
OFF_HG = 0
OFF_DA = 5120
OFF_MEMQ = 9728
OFF_GATE = 10240
DA_CFG = ((128, 1), (512, 4), (2048, 16))
SLOPES = (2.0 ** (-8.0 * np.arange(1, 13) / 12)).astype(np.float64)


class Res:
    __slots__ = ("name", "w", "rd")

    def __init__(self, name):
        self.name = name
        self.w = None
        self.rd = {}


class Sched:
    EPOCH = 30000

    def __init__(self, nc, es):
        self.nc = nc
        self.es = es
        self.h = {"pe": nc.tensor, "act": nc.scalar, "dve": nc.vector, "pool": nc.gpsimd, "sp": nc.sync}
        self.E = {n: dict(sems=[], count=0, waited={}) for n in self.h}
        self.dsem = {}
        self.nsem = 0
        self.nops = 0

    def _newsem(self, name):
        self.nsem += 1
        return self.es.enter_context(self.nc.semaphore(name))

    def _next_tag(self, e):
        E = self.E[e]
        c = E["count"] + 1
        ep = (c - 1) // self.EPOCH
        while len(E["sems"]) <= ep:
            E["sems"].append(self._newsem(f"s_{e}_{len(E['sems'])}"))
        return (E["sems"][ep], c - ep * self.EPOCH, e, (e, ep))

    def _waits(self, e, reads, writes, is_dma):
        need = {}

        def add(tag, kind):
            sem, val, pe, key = tag
            if pe == e and not is_dma:
                if e == "pe":
                    return
                if kind != "raw":
                    return
            if key not in need or need[key][1] < val:
                need[key] = (sem, val)

        for r in reads:
            if r.w is not None:
                add(r.w, "raw")
        for w in writes:
            if w.w is not None:
                add(w.w, "waw")
            for t in w.rd.values():
                add(t, "war")
        E = self.E[e]
        for key, (sem, val) in need.items():
            if E["waited"].get(key, 0) >= val:
                continue
            E["waited"][key] = val
            self.h[e].wait_ge(sem, val)

    def _commit(self, tag, reads, writes):
        for w in writes:
            w.w = tag
            w.rd = {}
        for r in reads:
            k = tag[3]
            if k not in r.rd or r.rd[k][1] < tag[1]:
                r.rd[k] = tag

    def op(self, e, fn, R=(), W=(), signal=True):
        self._waits(e, R, W, False)
        tag = self._next_tag(e)
        ins = fn(self.h[e])
        if signal:
            ins.then_inc(tag[0], 1)
            self.E[e]["count"] += 1
        self._commit(tag, R, W)
        self.nops += 1
        return ins

    def dma(self, q, out, in_, R, W, key):
        self._waits(q, R, W, True)
        if key not in self.dsem:
            self.dsem[key] = [self._newsem(f"d_{len(self.dsem)}"), 0]
        ds = self.dsem[key]
        ds[1] += 16
        tag = (ds[0], ds[1], "dma", ("dma", key))
        self.h[q].dma_start(out=out, in_=in_).then_inc(ds[0], 16)
        self._commit(tag, R, W)
        self.nops += 1

    def barrier(self):
        tags = []
        for e, E in self.E.items():
            if E["count"] > 0:
                c = E["count"]
                ep = (c - 1) // self.EPOCH
                tags.append((E["sems"][ep], c - ep * self.EPOCH, (e, ep)))
        for key, ds in self.dsem.items():
            if ds[1] > 0:
                tags.append((ds[0], ds[1], ("dma", key)))
        for e, E in self.E.items():
            for sem, val, key in tags:
                if key[0] == e and e == "pe":
                    continue
                if E["waited"].get(key, 0) >= val:
                    continue
                E["waited"][key] = val
                self.h[e].wait_ge(sem, val)


def build(debug=False, phases=("mem", "da", "hg", "tail"), debug_slots=None, debug_heads=None):
    nc = bass.Bass("TRN2", target_bir_lowering=False)
    es = ExitStack()
    S = Sched(nc, es)

    def din(name, shape, dt=F32):
        return nc.dram_tensor(name, list(shape), dt, kind="ExternalInput").ap()

    x_d = din("x", [L, D])
    mem_d = din("mem", [NMEM, D])
    w_in_d = din("w_in", [D, IN_COLS])
    w_kv_d = din("w_mem_kv", [D, D])
    w_phg_d = din("w_proj_hg", [1024, D])
    w_pda_d = din("w_proj_da", [512, D])
    w_pmem_d = din("w_proj_mem", [512, D])
    w_out_d = din("w_out", [D, D])
    w_fin_d = din("w_ffn_in", [D, 2 * DFF])
    w_fout_d = din("w_ffn_out", [DFF, D])
    gains_d = din("gainsT", [P, 3, KC])
    hgains_d = din("hgains", [P, 5])
    lbl_d = din("lbl", [P, 4, 8])
    cmask_d = din("cmask", [P, 2, P])
    damask_d = din("damask", [P, 12, 2 * P])
    rmask_d = din("rmask", [P, 1024])
    ident_d = din("ident", [P, P])
    out_d = nc.dram_tensor("out", [L, D], F32, kind="ExternalOutput").ap()
    mix_kind = "ExternalOutput" if debug else "Internal"
    mixT_d = nc.dram_tensor("mixT", [2048, L], BF16, kind=mix_kind).ap()
    h2T_d = nc.dram_tensor("h2T", [D, L], BF16, kind="Internal").ap()
    gate_d = nc.dram_tensor("gateT", [3072, L], BF16, kind="Internal").ap()

    def sb(st, name, shape, dt, side=None):
        if side is None:
            return st.enter_context(nc.sbuf_tensor(name, list(shape), dt))
        return st.enter_context(nc.sbuf_tensor(name, list(shape), dt, side=side))

    hT_r = [Res(f"hT{n}") for n in range(NT)]
    ident = sb(es, "identb", [P, P], BF16)
    ones = sb(es, "onesb", [P, P], BF16)
    cmask = sb(es, "cmaskb", [P, 2, P], BF16)
    rmask = sb(es, "rmaskf", [P, 1024], F32)
    gainsT = sb(es, "gainsT_sb", [P, 3, KC], F32)
    hgains = sb(es, "hgains_sb", [P, 5], F32)
    hgs = sb(es, "hgs", [P, 5], F32)
    lbl = sb(es, "lbl_sb", [P, 4, 8], F32)
    lbv = sb(es, "lbv", [P, 2, 8], F32)
    oml = sb(es, "oml", [P, 2, 8], F32)
    noml = sb(es, "noml", [P, 2, 8], F32)
    const_r = Res("const")
    psum = [es.enter_context(nc.psum_tensor(f"ps{i}", [P, 512], F32)) for i in range(8)]
    ps_r = [Res(f"ps{i}") for i in range(8)]
    ps_i = [0]

    ps_c = [0] * 8

    def PS(chain=None, nch=2):
        if chain is None:
            i = ps_i[0] % 8
            ps_i[0] += 1
        else:
            w = 8 // nch
            i = w * chain + ps_c[chain] % w
            ps_c[chain] += 1
        return psum[i], ps_r[i]

    def run_chains(gens, stagger=0):
        gens = [(i_, g_) for i_, g_ in enumerate(gens)]
        rnd = 0
        while gens:
            for i_, g_ in list(gens):
                if rnd < i_ * stagger:
                    continue
                try:
                    next(g_)
                except StopIteration:
                    gens.remove((i_, g_))
            rnd += 1

    SQ128 = float(np.sqrt(128.0))
    HG_STAG, P0_STAG = 0, 1
    NWST, WCAP = 3, 1024
    wst = [sb(es, f"wst{i}", [P, WCAP], F32) for i in range(NWST)]
    hstack = ExitStack()
    hT = sb(hstack, "hT", [P, KC, L], BF16, side="right")

    with ExitStack() as ph:
        cst = sb(ph, "cstage", [P, 2 * P], F32)
        cst_r = Res("cstage")

        def load_const(dram, dst, n, cast):
            flat_d = dram if len(dram.shape) == 2 else (
                dram.rearrange("p a b -> p (a b)"))
            if cast:
                S.dma("sp", cst[:, 0:n], flat_d, [], [cst_r], "cstage")
                dflat = dst[:] if len(dst.shape) == 2 else dst[:].rearrange("p a b -> p (a b)")
                S.op("dve", lambda e: e.tensor_copy(out=dflat, in_=cst[:, 0:n]), [cst_r], [const_r])
            else:
                dflat = dst[:] if len(dst.shape) == 2 else dst[:].rearrange("p a b -> p (a b)")
                S.dma("sp", dflat, flat_d, [], [const_r], "const_" + dst.name)

        load_const(ident_d, ident, P, True)
        load_const(cmask_d, cmask, 2 * P, True)
        S.dma("sp", rmask[:], rmask_d[:, 0:1024], [], [const_r], "const_rmask")
        load_const(gains_d, gainsT, 3 * KC, False)
        load_const(hgains_d, hgains, 5, False)
        load_const(lbl_d, lbl, 32, False)
        S.op("dve", lambda e: e.memset(ones[:], 1.0), [], [const_r])
        S.op("dve", lambda e: e.tensor_scalar(out=hgs[:], in0=hgains[:], scalar1=1.0 / SQ128, scalar2=None,
                                              op0=ALU.mult), [const_r], [const_r])
        for d_ in range(2):
            S.op("dve", lambda e, d_=d_: e.tensor_sub(out=lbv[:, d_, :], in0=lbl[:, 2 * d_, :],
                                                      in1=lbl[:, 2 * d_ + 1, :]), [const_r], [const_r])
        S.op("act", lambda e: e.activation(out=lbv[:], in_=lbv[:], func=AF.Sigmoid), [const_r], [const_r])
        S.op("dve", lambda e: e.tensor_scalar(out=oml[:], in0=lbv[:], scalar1=-1.0, scalar2=1.0,
                                              op0=ALU.mult, op1=ALU.add), [const_r], [const_r])
        S.op("dve", lambda e: e.tensor_scalar(out=noml[:], in0=oml[:], scalar1=-1.0, scalar2=None,
                                              op0=ALU.mult), [const_r], [const_r])
        S.barrier()

    wst_r = [Res(f"wst{i}") for i in range(NWST)]
    wst_i = [0]

    def wload(dst, dst_r, w2d, col0, ncols, kc0, nkc, gain_idx=None, dst_kc0=0, dst_col0=0, eng="dve"):
        assert nkc * ncols <= WCAP
        i = wst_i[0] % len(wst)
        wst_i[0] += 1
        st = wst[i][:, 0:nkc * ncols].rearrange("p (c n) -> p c n", c=nkc)
        src = w2d[kc0 * P:(kc0 + nkc) * P, col0:col0 + ncols].rearrange("(c p) n -> p c n", p=P)
        S.dma("sp", st, src, [], [wst_r[i]], f"wst{i}")
        o = dst[:, dst_kc0:dst_kc0 + nkc, dst_col0:dst_col0 + ncols]
        if gain_idx is None:
            S.op(eng, lambda e: e.tensor_copy(out=o, in_=st), [wst_r[i]], [dst_r])
        else:
            g = gainsT[:, gain_idx, kc0:kc0 + nkc].unsqueeze(2).to_broadcast([P, nkc, ncols])
            S.op(eng, lambda e: e.tensor_tensor(out=o, in0=st, in1=g, op=ALU.mult),
                 [wst_r[i], const_r], [dst_r])

    def wload_big(dst, dst_r, w2d, col0, ncols, nkc_total, gain_idx=None, dst_kc0=0, dst_col0=0, eng="dve"):
        for cc in range(0, ncols, 1024):
            nc_ = min(1024, ncols - cc)
            step = max(1, WCAP // nc_)
            for k0 in range(0, nkc_total, step):
                wload(dst, dst_r, w2d, col0 + cc, nc_, k0, min(step, nkc_total - k0), gain_idx,
                      dst_kc0=dst_kc0 + k0, dst_col0=dst_col0 + cc, eng=eng)

    def norm_transpose(ph, tag, src_rows, ntiles, dstT, dst_res, dma_key, NCH=2):
        xin = [sb(ph, f"{tag}_xin{i}", [P, D], F32) for i in range(NCH)]
        xin_r = [Res(f"{tag}_xin{i}") for i in range(NCH)]
        junk = [sb(ph, f"{tag}_junk{i}", [P, D], BF16) for i in range(NCH)]
        junk_r = [Res(f"junk{i}") for i in range(NCH)]
        hb = [sb(ph, f"{tag}_hb{i}", [P, D], BF16) for i in range(NCH)]
        hb_r = [Res(f"{tag}_hb{i}") for i in range(NCH)]
        ss = sb(ph, f"{tag}_ss", [P, 2 * ntiles], F32)
        ss_r = [Res(f"{tag}_ss{i}") for i in range(ntiles)]

        def chain(i):
            for n in range(i, ntiles, NCH):
                S.dma("sp", xin[i][:], src_rows(n), [], [xin_r[i]], f"{dma_key}{i}")
                yield
                S.op("act", lambda e: e.activation(out=junk[i][:], in_=xin[i][:], func=AF.Square,
                                                   accum_out=ss[:, 2 * n:2 * n + 1]), [xin_r[i]], [junk_r[i], ss_r[n]])
                yield
                S.op("act", lambda e: e.activation(out=ss[:, 2 * n + 1:2 * n + 2], in_=ss[:, 2 * n:2 * n + 1],
                                                   func=AF.Sqrt, bias=float(EPS), scale=1.0 / D), [ss_r[n]], [ss_r[n]])
                yield
                S.op("dve", lambda e: e.reciprocal(out=ss[:, 2 * n + 1:2 * n + 2], in_=ss[:, 2 * n + 1:2 * n + 2]),
                     [ss_r[n]], [ss_r[n]])
                yield
                S.op("dve", lambda e: e.tensor_scalar(out=hb[i][:], in0=xin[i][:], scalar1=ss[:, 2 * n + 1:2 * n + 2],
                                                      scalar2=None, op0=ALU.mult), [xin_r[i], ss_r[n]], [hb_r[i]])
                yield
                pt, pr = PS(i, NCH)
                ptb = pt[:].bitcast(BF16)
                for c in range(KC):
                    S.op("pe", lambda e: e.transpose(out=ptb[:, c * P:(c + 1) * P], in_=hb[i][:, c * P:(c + 1) * P],
                                                     identity=ident[:]), [hb_r[i], const_r], [pr], signal=(c == KC - 1))
                yield
                S.op("dve" if i % 2 == 0 else "act", (lambda e: e.tensor_copy(
                    out=dstT[:, :, n * P:(n + 1) * P], in_=ptb.rearrange("p (c t) -> p c t", c=KC))) if i % 2 == 0 else (
                    lambda e: e.activation(out=dstT[:, :, n * P:(n + 1) * P],
                                           in_=ptb.rearrange("p (c t) -> p c t", c=KC), func=AF.Copy)),
                     [pr], [dst_res[n]])
                yield

        run_chains([chain(c_) for c_ in range(NCH)], stagger=P0_STAG)

    memw = ExitStack()
    mem_pre = {}
    if "mem" in phases:
        mem_pre["wkv"] = (sb(memw, "wkv", [P, KC, D], BF16), Res("wkv"))
        mem_pre["wq"] = (sb(memw, "wq_mem", [P, KC, 512], BF16), Res("wq_mem"))
    with ExitStack() as ph:
        norm_transpose(ph, "p0", lambda n: x_d[n * P:(n + 1) * P, :], NT, hT, hT_r, "p0x", NCH=4)
        if "mem" in phases:
            wload_big(mem_pre["wkv"][0], mem_pre["wkv"][1], w_kv_d, 0, D, KC, gain_idx=1)
            wload_big(mem_pre["wq"][0], mem_pre["wq"][1], w_in_d, OFF_MEMQ, 512, KC, gain_idx=0)
        S.barrier()

    def hT_res(t0, nt):
        return hT_r[t0 // P:(t0 + nt + P - 1) // P]

    def proj_fm(ps_ap, pr, wb, wb_r, j0, ncol, t0, nt, tstep=1):
        for c in range(KC):
            rhs = hT[:, c, t0:t0 + (nt - 1) * tstep + 1:tstep] if tstep > 1 else hT[:, c, t0:t0 + nt]
            S.op("pe", lambda e, c=c, rhs=rhs: e.matmul(ps_ap, lhsT=wb[:, c, j0:j0 + ncol], rhs=rhs,
                                                        start=(c == 0), stop=(c == KC - 1)),
                 [wb_r] + (hT_r if tstep > 1 else hT_res(t0, nt)), [pr], signal=(c == KC - 1))

    qk_cnt = [0]

    def qk_norm_g(bufs, src_ps, src_r, n, gain_col, extra_scale, dst_ap, dst_r, chain, nch=2):
        sqb, sqb_r, rs, rs_r = bufs
        S.op("act", lambda e: e.activation(out=sqb[:, 0:n], in_=src_ps, func=AF.Square), [src_r], [sqb_r])
        yield
        p2, p2r = PS(chain, nch)
        S.op("pe", lambda e: e.matmul(p2[:, 0:n], lhsT=ones[:], rhs=sqb[:, 0:n], start=True, stop=True),
             [sqb_r, const_r], [p2r])
        yield
        S.op("act", lambda e: e.activation(out=rs[:, 0:n], in_=p2[:, 0:n], func=AF.Ln, bias=float(EPS),
                                           scale=1.0 / P), [p2r], [rs_r])
        yield
        S.op("act", lambda e: e.activation(out=rs[:, 0:n], in_=rs[:, 0:n], func=AF.Exp, scale=-0.5), [rs_r], [rs_r])
        yield
        gcol = (hgs if extra_scale == "qscale" else hgains)[:, gain_col:gain_col + 1]
        S.op("dve", lambda e: e.scalar_tensor_tensor(out=dst_ap, in0=src_ps, scalar=gcol, in1=rs[:, 0:n],
                                                     op0=ALU.mult, op1=ALU.mult),
             [src_r, rs_r, const_r], [dst_r])
        yield

    def qk_norm(ph_bufs, src_ps, src_r, n, gain_col, extra_scale, dst_ap, dst_r):
        sqb, sqb_r, rs, rs_r = ph_bufs[qk_cnt[0] % len(ph_bufs)]
        qk_cnt[0] += 1
        S.op("act", lambda e: e.activation(out=sqb[:, 0:n], in_=src_ps, func=AF.Square), [src_r], [sqb_r])
        p2, p2r = PS()
        S.op("pe", lambda e: e.matmul(p2[:, 0:n], lhsT=ones[:], rhs=sqb[:, 0:n], start=True, stop=True),
             [sqb_r, const_r], [p2r])
        S.op("act", lambda e: e.activation(out=rs[:, 0:n], in_=p2[:, 0:n], func=AF.Ln, bias=float(EPS),
                                           scale=1.0 / P), [p2r], [rs_r])
        S.op("act", lambda e: e.activation(out=rs[:, 0:n], in_=rs[:, 0:n], func=AF.Exp, scale=-0.5), [rs_r], [rs_r])
        gcol = (hgs if extra_scale == "qscale" else hgains)[:, gain_col:gain_col + 1]
        S.op("dve", lambda e: e.scalar_tensor_tensor(out=dst_ap, in0=src_ps, scalar=gcol, in1=rs[:, 0:n],
                                                     op0=ALU.mult, op1=ALU.mult),
             [src_r, rs_r, const_r], [dst_r])

    def phase_mem():
        with ExitStack() as ph:
            mnT = sb(ph, "mnT", [P, KC, NMEM], BF16)
            mnT_r = [Res("mnT0"), Res("mnT1")]
            norm_transpose(ph, "pm", lambda n: mem_d[n * P:(n + 1) * P, :], 2, mnT, mnT_r, "pmx")
            wkv, wkv_r = mem_pre["wkv"]
            wq, wq_r = mem_pre["wq"]
            khT = sb(ph, "khT_mem", [P, 4, NMEM], BF16)
            khT_r = Res("khT_mem")
            vm = sb(ph, "v_mem", [P, 2, 512], BF16)
            vm_r = Res("v_mem")
            nb = [(sb(ph, f"sqb_mem{i}", [P, 512], BF16), Res(f"sqb{i}"), sb(ph, f"rs_mem{i}", [P, 512], F32),
                   Res(f"rs{i}")) for i in range(4)]
            for hd in range(4):
                pk, pkr = PS()
                for c in range(KC):
                    S.op("pe", lambda e, c=c, hd=hd: e.matmul(pk[:, 0:NMEM], lhsT=wkv[:, c, hd * P:(hd + 1) * P],
                                                               rhs=mnT[:, c, :], start=(c == 0), stop=(c == KC - 1)),
                         [wkv_r] + mnT_r, [pkr], signal=(c == KC - 1))
                qk_norm(nb, pk[:, 0:NMEM], pkr, NMEM, 4, None, khT[:, hd, :], khT_r)
            for mt in range(2):
                pv, pvr = PS()
                for c in range(KC):
                    S.op("pe", lambda e, c=c, mt=mt: e.matmul(pv[:, :], lhsT=mnT[:, c, mt * P:(mt + 1) * P],
                                                               rhs=wkv[:, c, 512:1024], start=(c == 0),
                                                               stop=(c == KC - 1)),
                         [wkv_r] + mnT_r, [pvr], signal=(c == KC - 1))
                S.op("act", lambda e, mt=mt: e.activation(out=vm[:, mt, :], in_=pv[:, :], func=AF.Copy), [pvr], [vm_r])
            qh = [[sb(ph, f"qh_mem{c}_{i}", [P, 512], BF16) for i in range(2)] for c in range(4)]
            qh_r = [[Res(f"qh{c}_{i}") for i in range(2)] for c in range(4)]
            pT = [[sb(ph, f"pT_mem{c}_{i}", [P, 2, 512], BF16) for i in range(2)] for c in range(4)]
            pT_r = [[Res(f"pT{c}_{i}") for i in range(2)] for c in range(4)]
            rz = [sb(ph, f"rz_mem{c}", [P, 512], F32) for c in range(4)]
            rz_r = [Res(f"rz_mem{c}") for c in range(4)]
            ost = [[sb(ph, f"ost_mem{c}_{i}", [P, 512], BF16) for i in range(2)] for c in range(4)]
            ost_r = [[Res(f"ost{c}_{i}") for i in range(2)] for c in range(4)]

            def mem_chain(c):
                it = 0
                for blk in range(L // 512):
                    t0 = blk * 512
                    ob = blk % 2
                    for h2 in range(1):
                        hd = c
                        b = it % 2
                        it += 1
                        pq, pqr = PS(c, 4)
                        proj_fm(pq[:, :], pqr, wq, wq_r, hd * P, P, t0, 512)
                        yield
                        yield from qk_norm_g(nb[c], pq[:, :], pqr, 512, 3, "qscale", qh[c][b][:], qh_r[c][b], c, 4)
                        for mt in range(2):
                            p_s, p_sr = PS(c, 4)
                            S.op("pe", lambda e: e.matmul(p_s[:, :], lhsT=khT[:, hd, mt * P:(mt + 1) * P],
                                                          rhs=qh[c][b][:], start=True, stop=True),
                                 [khT_r, qh_r[c][b]], [p_sr])
                            yield
                            S.op("act", lambda e: e.activation(out=pT[c][b][:, mt, :], in_=p_s[:, :], func=AF.Exp),
                                 [p_sr], [pT_r[c][b]])
                            yield
                        pu, pur = PS(c, 4)
                        pz, pzr = PS(c, 4)
                        for mt in range(2):
                            S.op("pe", lambda e: e.matmul(pu[:, :], lhsT=vm[:, mt, hd * P:(hd + 1) * P],
                                                          rhs=pT[c][b][:, mt, :], start=(mt == 0), stop=(mt == 1)),
                                 [vm_r, pT_r[c][b]], [pur], signal=(mt == 1))
                        for mt in range(2):
                            S.op("pe", lambda e: e.matmul(pz[:, :], lhsT=ones[:], rhs=pT[c][b][:, mt, :],
                                                          start=(mt == 0), stop=(mt == 1)),
                                 [const_r, pT_r[c][b]], [pzr], signal=(mt == 1))
                        yield
                        S.op("act", lambda e: e.activation(out=rz[c][:], in_=pz[:, :], func=AF.Ln), [pzr], [rz_r[c]])
                        yield
                        S.op("act", lambda e: e.activation(out=rz[c][:], in_=rz[c][:], func=AF.Exp, scale=-1.0),
                             [rz_r[c]], [rz_r[c]])
                        yield
                        S.op("dve", lambda e: e.tensor_tensor(out=ost[c][ob][:], in0=pu[:, :], in1=rz[c][:],
                                                              op=ALU.mult), [pur, rz_r[c]], [ost_r[c][ob]])
                        yield
                    dst = mixT_d[1536 + c * P:1536 + (c + 1) * P, t0:t0 + 512]
                    S.dma("pool", dst, ost[c][ob][:], [ost_r[c][ob]], [mix_r[12 + c]], f"ost_mem{c}_{ob}")
                    yield

            run_chains([mem_chain(c_) for c_ in range(4)], stagger=3)
            S.barrier()

    mix_r = [Res(f"mix{i}") for i in range(16)]

    def phase_da(slots=range(4), pre_barrier=None):
        with ExitStack() as ph:
            damask = sb(ph, "damaskb", [P, 12, 2 * P], BF16)
            dst_ = sb(ph, "dastage", [P, 24 * P], F32)
            dst_r = Res("dastage")
            damask_r = Res("damask")
            S.dma("sp", dst_[:], damask_d.rearrange("p a b -> p (a b)"), [], [dst_r], "dastage")
            S.op("dve", lambda e: e.tensor_copy(out=damask[:].rearrange("p a b -> p (a b)"), in_=dst_[:]),
                 [dst_r], [damask_r])
            wqkv = [[sb(ph, f"w_da{k}_{i}", [P, KC, P], BF16) for k in range(3)] for i in range(2)]
            wqkv_r = [[Res(f"w_da{k}_{i}") for k in range(3)] for i in range(2)]
            qhT = sb(ph, "qhT_da", [P, L], BF16)
            khT = sb(ph, "khT_da", [P, L], BF16)
            qk_r = [Res("qhT_da"), Res("khT_da")]
            vtm = sb(ph, "vtm_da", [P, NT, P], BF16)
            vtm_r = Res("vtm_da")
            uz = sb(ph, "uz_da", [P, 2, L], F32)
            uz_r = Res("uz_da")
            nb_ = [(sb(ph, f"sqb_da{i}", [P, 512], BF16), Res(f"sqb_da{i}"), sb(ph, f"rs_da{i}", [P, 512], F32),
                    Res(f"rs_da{i}")) for i in range(3)]
            pex = [sb(ph, f"pex_da{i}", [P, 2, 2, P], BF16) for i in range(2)]
            pex_r = [Res("pex0"), Res("pex1")]
            pm = [sb(ph, f"pm_da{i}", [P, 2, 2, P], BF16) for i in range(2)]
            pm_r = [Res("pm0"), Res("pm1")]
            rz = sb(ph, "rz_da", [P, 512], F32)
            rz_r = Res("rz_da")
            ost = [sb(ph, f"ost_da{i}", [P, 512], BF16) for i in range(2)]
            ost_r = [Res("ost_da0"), Res("ost_da1")]
            hcount = 0
            for slot in slots:
                for g in range(3):
                    head = g * 4 + slot
                    d = DA_CFG[g][1]
                    Ld = L // d
                    nb = Ld // P
                    wi = hcount % 2
                    hcount += 1
                    if hcount == 1:
                        for k in range(3):
                            wload(wqkv[wi][k], wqkv_r[wi][k], w_in_d, OFF_DA + k * 1536 + head * P, P, 0, KC, gain_idx=0)
                    nxt = hcount
                    slots_l = list(slots)
                    if nxt < 3 * len(slots_l):
                        nhead = (nxt % 3) * 4 + slots_l[nxt // 3]
                        for k in range(3):
                            wload(wqkv[nxt % 2][k], wqkv_r[nxt % 2][k], w_in_d, OFF_DA + k * 1536 + nhead * P, P, 0, KC,
                                  gain_idx=0)
                    wq, wk, wv = wqkv[wi]
                    wq_r, wk_r, wv_r = wqkv_r[wi]
                    items = []
                    for blk in range(L // 512):
                        items.append((blk * 512, wq, wq_r, 1, "qscale", qhT, qk_r[0]))
                        items.append((blk * 512, wk, wk_r, 2, None, khT, qk_r[1]))
                    live = {}

                    def qkA(j):
                        t0, w_, w_r_, gcol_i, esc, dstT_, dres = items[j]
                        sqb, sqb_r, rs, rs_r = nb_[j % len(nb_)]
                        pq, pqr = PS()
                        proj_fm(pq[:, :], pqr, w_, w_r_, 0, P, t0, 512)
                        S.op("act", lambda e: e.activation(out=sqb[:, :], in_=pq[:, :], func=AF.Square), [pqr], [sqb_r])
                        live[j] = (pq, pqr)

                    def qkB(j):
                        t0, w_, w_r_, gcol_i, esc, dstT_, dres = items[j]
                        sqb, sqb_r, rs, rs_r = nb_[j % len(nb_)]
                        pq, pqr = live.pop(j)
                        p2, p2r = PS()
                        S.op("pe", lambda e: e.matmul(p2[:, :], lhsT=ones[:], rhs=sqb[:, :], start=True, stop=True),
                             [sqb_r, const_r], [p2r])
                        S.op("act", lambda e: e.activation(out=rs[:, :], in_=p2[:, :], func=AF.Ln, bias=float(EPS),
                                                           scale=1.0 / P), [p2r], [rs_r])
                        S.op("act", lambda e: e.activation(out=rs[:, :], in_=rs[:, :], func=AF.Exp, scale=-0.5),
                             [rs_r], [rs_r])
                        gcol = (hgs if esc == "qscale" else hgains)[:, gcol_i:gcol_i + 1]
                        S.op("dve", lambda e: e.scalar_tensor_tensor(out=dstT_[:, t0:t0 + 512], in0=pq[:, :], scalar=gcol,
                                                                     in1=rs[:, :], op0=ALU.mult, op1=ALU.mult),
                             [pqr, rs_r, const_r], [dres])

                    qkA(0)
                    for j in range(len(items)):
                        if j + 1 < len(items):
                            qkA(j + 1)
                        qkB(j)
                    for bi0 in range(0, NT, 4):
                        pv, pvr = PS()
                        for j in range(4):
                            bi = bi0 + j
                            r, kb = bi // nb, bi % nb
                            tk0 = r + d * kb * P
                            for c in range(KC):
                                lhs = hT[:, c, tk0:tk0 + (P - 1) * d + 1:d]
                                S.op("pe", lambda e, c=c, j=j, lhs=lhs, pv=pv: e.matmul(
                                    pv[:, j * P:(j + 1) * P], lhsT=lhs, rhs=wv[:, c, :], start=(c == 0),
                                    stop=(c == KC - 1)), [wv_r] + hT_r, [pvr], signal=(c == KC - 1 and j == 3))
                        S.op("act", lambda e, bi0=bi0, pv=pv: e.activation(
                            out=vtm[:, bi0:bi0 + 4, :], in_=pv[:, :].rearrange("p (j v) -> p j v", j=4), func=AF.Copy),
                            [pvr], [vtm_r])
                    tiles = []
                    for r in range(d):
                        tiles.append((r, 0, 1))
                        i = 1
                        while i < nb:
                            if i + 1 < nb:
                                tiles.append((r, i, 2))
                                i += 2
                            else:
                                tiles.append((r, i, 1))
                                i += 1
                        tiles.append((r, nb, 1))
                    staged = {}

                    def stage1(idx):
                        r, i, nt_ = tiles[idx]
                        sl = idx % 2
                        p_s, p_sr = PS()
                        if nt_ == 2:
                            tq0 = r + d * (P * i - 64)
                            for tau in range(2):
                                qsl = qhT[:, tq0 + tau * P * d:tq0 + tau * P * d + (P - 1) * d + 1:d]
                                for bb in range(2):
                                    kb = i + tau - 1 + bb
                                    tk0 = r + d * kb * P
                                    ksl = khT[:, tk0:tk0 + (P - 1) * d + 1:d]
                                    S.op("pe", lambda e: e.matmul(
                                        p_s[:, tau * 2 * P + bb * P:tau * 2 * P + (bb + 1) * P], lhsT=ksl, rhs=qsl,
                                        start=True, stop=True), qk_r, [p_sr], signal=(tau == 1 and bb == 1))
                            S.op("act", lambda e: e.activation(out=pex[sl][:].rearrange("p t b a -> p (t b a)"),
                                                               in_=p_s[:, :], func=AF.Exp), [p_sr], [pex_r[sl]])
                            mk = damask[:, head, :].unsqueeze(1).to_broadcast([P, 2, 2 * P])
                            S.op("dve", lambda e: e.tensor_tensor(
                                out=pm[sl][:].rearrange("p t b a -> p t (b a)"),
                                in0=pex[sl][:].rearrange("p t b a -> p t (b a)"), in1=mk, op=ALU.mult),
                                [pex_r[sl], damask_r], [pm_r[sl]])
                            staged[idx] = (r, i, 2, None, tq0, 0, 2, sl)
                            return
                        a0 = 64 if i == 0 else 0
                        a1 = 64 if i == nb else P
                        nq = a1 - a0
                        tq0 = r + d * (P * i - 64 + a0)
                        qsl = qhT[:, tq0:tq0 + (nq - 1) * d + 1:d]
                        b0, b1 = (1 if i == 0 else 0), (1 if i == nb else 2)
                        for bb in range(b0, b1):
                            kb = i - 1 + bb
                            tk0 = r + d * kb * P
                            ksl = khT[:, tk0:tk0 + (P - 1) * d + 1:d]
                            S.op("pe", lambda e, bb=bb, ksl=ksl: e.matmul(
                                p_s[:, bb * P:bb * P + nq], lhsT=ksl, rhs=qsl, start=True, stop=True),
                                qk_r, [p_sr], signal=(bb == b1 - 1))
                        psv = p_s[:, 0:2 * P].rearrange("p (b a) -> p b a", b=2)[:, b0:b1, 0:nq]
                        S.op("act", lambda e: e.activation(out=pex[sl][:, 0, b0:b1, 0:nq], in_=psv, func=AF.Exp),
                             [p_sr], [pex_r[sl]])
                        mk = damask[:, head, :].rearrange("p (b a) -> p b a", b=2)[:, b0:b1, a0:a1]
                        S.op("dve", lambda e: e.tensor_tensor(out=pm[sl][:, 0, b0:b1, 0:nq],
                                                              in0=pex[sl][:, 0, b0:b1, 0:nq],
                                                              in1=mk, op=ALU.mult), [pex_r[sl], damask_r], [pm_r[sl]])
                        staged[idx] = (r, i, 1, nq, tq0, b0, b1, sl)

                    def stage2(idx):
                        r, i, nt_, nq, tq0, b0, b1, sl = staged.pop(idx)
                        pu, pur = PS()
                        if nt_ == 2:
                            for tau in range(2):
                                for bb in range(2):
                                    kb = i + tau - 1 + bb
                                    S.op("pe", lambda e: e.matmul(
                                        pu[:, tau * P:(tau + 1) * P], lhsT=vtm[:, r * nb + kb, :], rhs=pm[sl][:, tau, bb, :],
                                        start=(bb == 0), stop=(bb == 1)), [vtm_r, pm_r[sl]], [pur], signal=False)
                            for tau in range(2):
                                for bb in range(2):
                                    S.op("pe", lambda e: e.matmul(
                                        pu[:, 2 * P + tau * P:2 * P + (tau + 1) * P], lhsT=ones[:], rhs=pm[sl][:, tau, bb, :],
                                        start=(bb == 0), stop=(bb == 1)), [const_r, pm_r[sl]], [pur],
                                        signal=(tau == 1 and bb == 1))
                            puv = pu[:, :].rearrange("p (z a) -> p z a", z=2)
                            uzv = uz[:, :, tq0:tq0 + (2 * P - 1) * d + 1:d]
                        else:
                            for bb in range(b0, b1):
                                kb = i - 1 + bb
                                S.op("pe", lambda e, bb=bb, kb=kb: e.matmul(
                                    pu[:, 0:nq], lhsT=vtm[:, r * nb + kb, :], rhs=pm[sl][:, 0, bb, 0:nq],
                                    start=(bb == b0), stop=(bb == b1 - 1)), [vtm_r, pm_r[sl]], [pur], signal=False)
                            for bb in range(b0, b1):
                                S.op("pe", lambda e, bb=bb: e.matmul(
                                    pu[:, P:P + nq], lhsT=ones[:], rhs=pm[sl][:, 0, bb, 0:nq],
                                    start=(bb == b0), stop=(bb == b1 - 1)), [const_r, pm_r[sl]], [pur],
                                    signal=(bb == b1 - 1))
                            puv = pu[:, 0:2 * P].rearrange("p (b a) -> p b a", b=2)[:, :, 0:nq]
                            uzv = uz[:, :, tq0:tq0 + (nq - 1) * d + 1:d]
                        if g == 0:
                            S.op("act", lambda e: e.activation(out=uzv, in_=puv, func=AF.Copy), [pur], [uz_r])
                        else:
                            S.op("dve", lambda e: e.tensor_tensor(out=uzv, in0=puv, in1=uzv, op=ALU.add),
                                 [pur, uz_r], [uz_r])

                    for idx in range(len(tiles)):
                        stage1(idx)
                        if idx >= 1:
                            stage2(idx - 1)
                    stage2(len(tiles) - 1)
                for blk in range(L // 512):
                    t0 = blk * 512
                    ob = blk % 2
                    S.op("act", lambda e, t0=t0: e.activation(out=rz[:], in_=uz[:, 1, t0:t0 + 512], func=AF.Ln),
                         [uz_r], [rz_r])
                    S.op("act", lambda e: e.activation(out=rz[:], in_=rz[:], func=AF.Exp, scale=-1.0), [rz_r], [rz_r])
                    S.op("dve", lambda e, t0=t0, ob=ob: e.tensor_tensor(out=ost[ob][:], in0=uz[:, 0, t0:t0 + 512],
                                                                        in1=rz[:], op=ALU.mult),
                         [uz_r, rz_r], [ost_r[ob]])
                    S.dma("pool", mixT_d[1024 + slot * P:1024 + (slot + 1) * P, t0:t0 + 512], ost[ob][:],
                          [ost_r[ob]], [mix_r[8 + slot]], f"ost_da{ob}")
            if pre_barrier is not None:
                pre_barrier()
            S.barrier()

    HG_SCALE = float(128 ** -0.5)

    def phase_hg(heads=range(8), wts=None, wts_r=None, first_loaded=False):
        with ExitStack() as ph:
            sqT = sb(ph, "sqT_hg", [P, L], BF16)
            sqT_r = Res("sqT")
            gT = sb(ph, "gT_hg", [P, L], BF16)
            gT_r = Res("gT")
            vtm = sb(ph, "vtm_hg", [P, NT, P], BF16)
            vtm_r = Res("vtm_hg")
            qin = [sb(ph, f"qin_hg{i}", [P, L], BF16) for i in range(2)]
            kin = [sb(ph, f"kin_hg{i}", [P, L], BF16) for i in range(2)]
            qin_r = [Res("qin0"), Res("qin1")]
            kin_r = [Res("kin0"), Res("kin1")]
            SEG = 1024
            T1s = [sb(ph, f"T1_hg{i}", [P, 1 + SEG], F32) for i in range(2)]
            T1s_r = [Res("T1_0"), Res("T1_1")]
            T2s = [sb(ph, f"T2_hg{i}", [P, SEG], F32) for i in range(2)]
            T2s_r = [Res("T2_0"), Res("T2_1")]
            K1s = [sb(ph, f"K1_hg{i}", [P, SEG], BF16) for i in range(2)]
            K1s_r = [Res("K1_0"), Res("K1_1")]
            E1s = [sb(ph, f"E1_hg{i}", [P, SEG], BF16) for i in range(2)]
            E1s_r = [Res("E1_0"), Res("E1_1")]
            E2s, E2s_r = E1s, E1s_r
            st = [sb(ph, f"st_hg{i}", [P, 6, NT], F32) for i in range(2)]
            st_r = [Res("st0"), Res("st1")]
            oacc = sb(ph, "oacc_hg", [P, L], BF16)
            oacc_r = [Res(f"oacc{n}") for n in range(NT)]
            Sst = [[sb(ph, f"S_hg{i}_{j}", [P, P], F32) for j in range(2)] for i in range(2)]
            Sst_r = [[Res(f"S{i}_{j}") for j in range(2)] for i in range(2)]
            Sbf = [[sb(ph, f"Sbf_hg{i}_{j}", [P, P], BF16) for j in range(2)] for i in range(2)]
            Sbf_r = [[Res(f"Sbf{i}_{j}") for j in range(2)] for i in range(2)]
            AT2 = [sb(ph, f"AT2_hg{j}", [P, 2, P], BF16) for j in range(3)]
            AT_r2 = [Res(f"AT2_{j}") for j in range(3)]
            ktm2 = [sb(ph, f"ktm2_hg{j}", [P, 2, P], BF16) for j in range(3)]
            ktm_r2 = [Res(f"ktm2_{j}") for j in range(3)]
            sqbs = [sb(ph, f"sqb_hg{i}", [P, 512], BF16) for i in range(2)]
            sqbs_r = [Res("sqb_hg0"), Res("sqb_hg1")]
            sgs, sgs_r = sqbs, sqbs_r
            rss = [sb(ph, f"rs_hg{i}", [P, 512], F32) for i in range(2)]
            rss_r = [Res("rs_hg0"), Res("rs_hg1")]
            tmpns = [sb(ph, f"tmpn_hg{i}", [P, 512], BF16) for i in range(2)]
            tmpns_r = [Res("tmpn0"), Res("tmpn1")]
            ost = [sb(ph, f"ost_hg{i}", [P, 512], BF16) for i in range(2)]
            ost_r = [Res("ost_hg0"), Res("ost_hg1")]
            for i_ in range(2):
                S.op("dve", lambda e, i_=i_: e.memset(T1s[i_][:, 0:1], 0.0), [], [T1s_r[i_]])
            segc = [0]
            heads = list(heads)

            def loadw(hi_):
                for k in range(5):
                    wload(wts[hi_ % 2][k], wts_r[hi_ % 2][k], w_in_d, OFF_HG + k * 1024 + heads[hi_] * P, P, 0, KC,
                          gain_idx=0)

            if not first_loaded:
                loadw(0)
            for hi, h in enumerate(heads):
                wi = hi % 2
                if hi + 1 < len(heads):
                    loadw(hi + 1)
                W_, W_r = wts[wi], wts_r[wi]
                for blk in range(L // 512):
                    t0 = blk * 512
                    pq, pqr = PS()
                    proj_fm(pq[:, :], pqr, W_[0], W_r[0], 0, P, t0, 512)
                    sgi = blk % 2
                    S.op("act", lambda e, pq=pq, sgi=sgi: e.activation(out=sgs[sgi][:], in_=pq[:, :], func=AF.Sigmoid),
                         [pqr], [sgs_r[sgi]])
                    S.op("dve", lambda e, t0=t0, pq=pq, sgi=sgi: e.tensor_tensor(
                        out=sqT[:, t0:t0 + 512], in0=pq[:, :], in1=sgs[sgi][:], op=ALU.mult),
                        [pqr, sgs_r[sgi]], [sqT_r])

                def gv_chain():
                    for n0 in range(0, NT, 4):
                        pv, pvr = PS(2, 4)
                        for j in range(4):
                            for c in range(KC):
                                S.op("pe", lambda e: e.matmul(
                                    pv[:, j * P:(j + 1) * P], lhsT=hT[:, c, (n0 + j) * P:(n0 + j + 1) * P],
                                    rhs=W_[3][:, c, :], start=(c == 0), stop=(c == KC - 1)),
                                    [W_r[3]] + hT_r[n0:n0 + 4], [pvr], signal=(c == KC - 1 and j == 3))
                        yield
                        S.op("dve", lambda e: e.tensor_copy(
                            out=vtm[:, n0:n0 + 4, :], in_=pv[:, :].rearrange("p (j v) -> p j v", j=4)),
                            [pvr], [vtm_r])
                        yield
                        t0 = (n0 // 4) * 512
                        pg, pgr = PS(3, 4)
                        proj_fm(pg[:, :], pgr, W_[4], W_r[4], 0, P, t0, 512)
                        yield
                        sgi = (n0 // 4) % 2
                        S.op("act", lambda e: e.activation(out=sgs[sgi][:], in_=pg[:, :], func=AF.Sigmoid),
                             [pgr], [sgs_r[sgi]])
                        yield
                        S.op("dve", lambda e: e.tensor_tensor(out=gT[:, t0:t0 + 512], in0=pg[:, :], in1=sgs[sgi][:],
                                                              op=ALU.mult), [pgr, sgs_r[sgi]], [gT_r])
                        yield

                def pre_chain(dr):
                    sgn = 1.0 if dr == 0 else -1.0
                    lbc = lbv[:, dr, h:h + 1]
                    omc = oml[:, dr, h:h + 1]
                    nomc = noml[:, dr, h:h + 1]
                    T1, T1_r, T2, T2_r = T1s[dr], T1s_r[dr], T2s[dr], T2s_r[dr]
                    K1, K1_r, E1, E1_r, E2, E2_r = K1s[dr], K1s_r[dr], E1s[dr], E1s_r[dr], E2s[dr], E2s_r[dr]
                    T1d = T1[:, 1:1 + SEG]
                    NTS = SEG // P
                    for seg in range(L // SEG):
                        s0 = seg * SEG
                        for b2 in range(SEG // 512):
                            pf, pfr = PS(dr, 4)
                            proj_fm(pf[:, :], pfr, W_[1 + dr], W_r[1 + dr], 0, P, s0 + b2 * 512, 512)
                            yield
                            S.op("act", lambda e: e.activation(out=T1[:, 1 + b2 * 512:1 + (b2 + 1) * 512], in_=pf[:, :],
                                                               func=AF.Sigmoid), [pfr], [T1_r])
                            yield
                        S.op("dve", lambda e: e.tensor_scalar(out=K1[:], in0=T1d, scalar1=nomc, scalar2=omc,
                                                              op0=ALU.mult, op1=ALU.add), [T1_r, const_r], [K1_r])
                        S.op("dve", lambda e: e.tensor_scalar(out=T1d, in0=T1d, scalar1=omc, scalar2=lbc,
                                                              op0=ALU.mult, op1=ALU.add), [T1_r, const_r], [T1_r])
                        yield
                        S.op("act", lambda e: e.activation(out=T1d, in_=T1d, func=AF.Ln), [T1_r], [T1_r])
                        yield
                        if dr == 0:
                            S.op("dve", lambda e: e.tensor_tensor_scan(out=T2[:], data0=rmask[:, 0:SEG], data1=T1d,
                                                                       initial=0.0, op0=ALU.mult, op1=ALU.add),
                                 [T1_r, const_r], [T2_r])
                        else:
                            S.op("dve", lambda e: e.tensor_tensor_scan(out=T2[:], data0=T1[:, 0:SEG],
                                                                       data1=rmask[:, 0:SEG],
                                                                       initial=0.0, op0=ALU.add, op1=ALU.mult),
                                 [T1_r, const_r], [T2_r])
                        yield
                        T2v = T2[:].rearrange("p (n t) -> p n t", t=P)
                        T1v = T1d.rearrange("p (n t) -> p n t", t=P)
                        ns = slice(seg * NTS, (seg + 1) * NTS)
                        if dr == 0:
                            S.op("dve", lambda e: e.tensor_copy(out=st[dr][:, 1, ns], in_=T2v[:, :, P - 1]),
                                 [T2_r], [st_r[dr]])
                        else:
                            S.op("dve", lambda e: e.tensor_tensor(out=st[dr][:, 1, ns], in0=T2v[:, :, P - 1],
                                                                  in1=T1v[:, :, P - 1], op=ALU.add),
                                 [T2_r, T1_r], [st_r[dr]])
                        yield
                        if dr == 0:
                            S.op("dve", lambda e: e.tensor_tensor(
                                out=T2v, in0=T2v, in1=st[dr][:, 1, ns].unsqueeze(2).to_broadcast([P, NTS, P]),
                                op=ALU.subtract), [T2_r, st_r[dr]], [T2_r])
                        yield
                        S.op("act", lambda e: e.activation(out=E1[:], in_=T2[:], func=AF.Exp, scale=sgn), [T2_r], [E1_r])
                        yield
                        S.op("dve", lambda e: e.scalar_tensor_tensor(
                            out=qin[dr][:, s0:s0 + SEG], in0=sqT[:, s0:s0 + SEG], scalar=HG_SCALE, in1=E1[:],
                            op0=ALU.mult, op1=ALU.mult), [sqT_r, E1_r], [qin_r[dr]])
                        yield
                        S.op("act", lambda e: e.activation(out=E2[:], in_=T2[:], func=AF.Exp, scale=-sgn), [T2_r], [E2_r])
                        yield
                        S.op("dve", lambda e: e.tensor_tensor(out=kin[dr][:, s0:s0 + SEG], in0=K1[:], in1=E2[:],
                                                              op=ALU.mult), [K1_r, E2_r], [kin_r[dr]])
                        yield
                    S.op("act", lambda e: e.activation(out=st[dr][:, 5, :], in_=st[dr][:, 1, :], func=AF.Exp),
                         [st_r[dr]], [st_r[dr]])
                    yield

                run_chains([pre_chain(0), pre_chain(1), gv_chain()], stagger=HG_STAG)
                touched = set()

                pMs = {}
                oacc3 = oacc[:].rearrange("p (n t) -> p n t", t=P)

                def tile_of(step, dr):
                    return step if dr == 0 else NT - 1 - step

                def stageA1(step):
                    sl = step % 3
                    last = step == NT - 1
                    pA, pAr = PS()
                    for dr in range(2):
                        n = tile_of(step, dr)
                        ts_ = slice(n * P, (n + 1) * P)
                        S.op("pe", lambda e: e.matmul(pA[:, dr * P:(dr + 1) * P], lhsT=kin[dr][:, ts_], rhs=qin[dr][:, ts_],
                                                      start=True, stop=True), [kin_r[dr], qin_r[dr]], [pAr],
                             signal=(dr == 1))
                    S.op("dve", lambda e: e.tensor_tensor(out=AT2[sl][:], in0=pA[:, 0:2 * P].rearrange("p (d t) -> p d t", d=2),
                                                          in1=cmask[:], op=ALU.mult), [pAr, const_r], [AT_r2[sl]])
                    pT, pTr = PS()
                    if not last:
                        pTb = pT[:].bitcast(BF16)
                        for dr in range(2):
                            n = tile_of(step, dr)
                            ts_ = slice(n * P, (n + 1) * P)
                            S.op("pe", lambda e: e.transpose(out=pTb[:, dr * P:(dr + 1) * P], in_=kin[dr][:, ts_],
                                                             identity=ident[:]), [kin_r[dr], const_r], [pTr],
                                 signal=(dr == 1))
                        S.op("act", lambda e: e.activation(out=ktm2[sl][:],
                                                           in_=pTb[:, 0:2 * P].rearrange("p (d t) -> p d t", d=2),
                                                           func=AF.Copy), [pTr], [ktm_r2[sl]])

                def stageA2(step):
                    sl = step % 3
                    last = step == NT - 1
                    pM, pMr = PS()
                    if not last:
                        for dr in range(2):
                            n = tile_of(step, dr)
                            S.op("pe", lambda e: e.matmul(pM[:, dr * P:(dr + 1) * P], lhsT=ktm2[sl][:, dr, :], rhs=vtm[:, n, :],
                                                          start=True, stop=True), [ktm_r2[sl], vtm_r], [pMr],
                                 signal=(dr == 1))
                        pMs[step] = (pM, pMr)

                def stageB1(step):
                    sb_ = step % 2
                    first, last = step == 0, step == NT - 1
                    if not last:
                        pM, pMr = pMs.pop(step)
                    for dr in range(2):
                        n = tile_of(step, dr)
                        so, sn = (step - 1) % 2, step % 2
                        if not first:
                            if dr == 0:
                                S.op("dve", lambda e: e.tensor_scalar(out=Sbf[dr][sb_][:], in0=Sst[dr][so][:],
                                                                      scalar1=st[dr][:, 5, n:n + 1], scalar2=None,
                                                                      op0=ALU.mult),
                                     [Sst_r[dr][so], st_r[dr]], [Sbf_r[dr][sb_]])
                            else:
                                S.op("act", lambda e: e.activation(out=Sbf[dr][sb_][:], in_=Sst[dr][so][:], func=AF.Copy,
                                                                   scale=st[dr][:, 5, n:n + 1]),
                                     [Sst_r[dr][so], st_r[dr]], [Sbf_r[dr][sb_]])
                        if not last:
                            pMd = pM[:, dr * P:(dr + 1) * P]
                            if first:
                                S.op("dve", lambda e: e.tensor_copy(out=Sst[dr][sn][:], in_=pMd), [pMr], [Sst_r[dr][sn]])
                            else:
                                S.op("dve", lambda e: e.scalar_tensor_tensor(
                                    out=Sst[dr][sn][:], in0=Sst[dr][so][:], scalar=st[dr][:, 5, n:n + 1], in1=pMd,
                                    op0=ALU.mult, op1=ALU.add), [pMr, st_r[dr], Sst_r[dr][so]], [Sst_r[dr][sn]])

                def stageB2(step):
                    sl = step % 3
                    sb_ = step % 2
                    first = step == 0
                    early = step < NT // 2
                    pO, pOr = PS()
                    for dr in range(2):
                        n = tile_of(step, dr)
                        ts_ = slice(n * P, (n + 1) * P)
                        half = dr if early else 1 - dr
                        po = pO[:, half * P:(half + 1) * P]
                        S.op("pe", lambda e: e.matmul(po, lhsT=vtm[:, n, :], rhs=AT2[sl][:, dr, :], start=True, stop=first),
                             [vtm_r, AT_r2[sl]], [pOr], signal=(first and dr == 1))
                        if not first:
                            S.op("pe", lambda e: e.matmul(po, lhsT=Sbf[dr][sb_][:], rhs=qin[dr][:, ts_],
                                                          start=False, stop=True),
                                 [Sbf_r[dr][sb_], qin_r[dr]], [pOr], signal=(dr == 1))
                    lo, hi_ = (step, NT - 1 - step) if early else (NT - 1 - step, step)
                    ov = oacc3[:, lo:hi_ + 1:hi_ - lo, :]
                    pv2 = pO[:, 0:2 * P].rearrange("p (d t) -> p d t", d=2)
                    orr = [oacc_r[lo], oacc_r[hi_]]
                    if early:
                        S.op("act", lambda e: e.activation(out=ov, in_=pv2, func=AF.Copy), [pOr], orr)
                    else:
                        S.op("dve", lambda e: e.tensor_tensor(out=ov, in0=pv2, in1=ov, op=ALU.add), [pOr] + orr, orr)

                ps_i[0] = 0
                PS()
                stageA1(0)
                PS()
                PS()
                stageA1(1)
                stageA2(0)
                for step in range(NT):
                    stageB1(step)
                    if step >= 1:
                        stageB2(step - 1)
                    else:
                        PS()
                    if step + 2 < NT:
                        stageA1(step + 2)
                    else:
                        PS()
                        PS()
                    if step + 1 < NT:
                        stageA2(step + 1)
                    else:
                        PS()
                stageB2(NT - 1)
                plive = {}

                def postA(blk):
                    t0 = blk * 512
                    ob = blk % 2
                    orr = oacc_r[blk * 4:blk * 4 + 4]
                    S.op("act", lambda e: e.activation(out=sqbs[ob][:], in_=oacc[:, t0:t0 + 512], func=AF.Square),
                         orr, [sqbs_r[ob]])
                    p2, p2r = PS()
                    S.op("pe", lambda e: e.matmul(p2[:, :], lhsT=ones[:], rhs=sqbs[ob][:], start=True, stop=True),
                         [sqbs_r[ob], const_r], [p2r])
                    plive[blk] = (p2, p2r)

                def postB(blk):
                    t0 = blk * 512
                    ob = blk % 2
                    orr = oacc_r[blk * 4:blk * 4 + 4]
                    p2, p2r = plive.pop(blk)
                    rs, rs_r, tmpn, tmpn_r = rss[ob], rss_r[ob], tmpns[ob], tmpns_r[ob]
                    S.op("act", lambda e: e.activation(out=rs[:], in_=p2[:, :], func=AF.Ln, bias=float(EPS),
                                                       scale=1.0 / P), [p2r], [rs_r])
                    S.op("act", lambda e: e.activation(out=rs[:], in_=rs[:], func=AF.Exp, scale=-0.5), [rs_r], [rs_r])
                    S.op("dve", lambda e: e.tensor_tensor(out=tmpn[:], in0=oacc[:, t0:t0 + 512], in1=rs[:],
                                                          op=ALU.mult), orr + [rs_r], [tmpn_r])
                    S.op("dve", lambda e: e.scalar_tensor_tensor(
                        out=ost[ob][:], in0=tmpn[:], scalar=hgains[:, 0:1], in1=gT[:, t0:t0 + 512],
                        op0=ALU.mult, op1=ALU.mult), [tmpn_r, gT_r, const_r], [ost_r[ob]])
                    S.dma("pool", mixT_d[h * P:(h + 1) * P, t0:t0 + 512], ost[ob][:], [ost_r[ob]], [mix_r[h]],
                          f"ost_hg{ob}")

                postA(0)
                for blk in range(L // 512):
                    if blk + 1 < L // 512:
                        postA(blk + 1)
                    postB(blk)
            S.barrier()


    def phase_gates(extra_loads=None):
        with ExitStack() as ph:
            for x_ in range(1):
                wst.append(sb(ph, f"wstx_g{x_}", [P, WCAP], F32))
                wst_r.append(Res(f"wstx_g{x_}"))
            wg = sb(ph, "wg", [P, KC, 3072], BF16)
            wg_col_r = [Res(f"wgc{i}") for i in range(8)] + [Res("wg1")] * 8 + [Res("wg2")] * 8
            for g8 in range(8):
                wload(wg, wg_col_r[g8], w_in_d, OFF_GATE + g8 * P, P, 0, KC, gain_idx=0, dst_col0=g8 * P)
            for ci in range(1, 3):
                wload_big(wg, wg_col_r[ci * 8], w_in_d, OFF_GATE + ci * 1024, 1024, KC, gain_idx=0, dst_col0=ci * 1024)
            if extra_loads is not None:
                extra_loads()
            sg = [sb(ph, f"sg{i}", [P, 4, 512], BF16) for i in range(2)]
            sg_r = [Res("sg0"), Res("sg1")]
            it = 0
            for ci in range(3):
                for blk in range(L // 512):
                    t0 = blk * 512
                    for fc0 in range(ci * 8, ci * 8 + 8, 4):
                        b = it % 2
                        it += 1
                        for j in range(4):
                            pg, pgr = PS()
                            proj_fm(pg[:, :], pgr, wg, wg_col_r[fc0 + j], (fc0 + j) * P, P, t0, 512)
                            S.op("act", lambda e, j=j, pg=pg: e.activation(out=sg[b][:, j, :], in_=pg[:, :],
                                                                           func=AF.Sigmoid), [pgr], [sg_r[b]])
                        dst = gate_d[fc0 * P:(fc0 + 4) * P, t0:t0 + 512].rearrange("(c p) t -> p c t", p=P)
                        S.dma("pool", dst, sg[b][:], [sg_r[b]], [gate_dr], f"sg{b}")
            S.barrier()
            del wst[NWST:]
            del wst_r[NWST:]

    gate_dr = Res("gate_d")
    out_r = [Res(f"out{n}") for n in range(NT)]
    h2_r = [Res(f"h2_{n}") for n in range(NT)]

    def phase_m1(wp, wp_r, wo, wo_r, extra_loads=None):
        with ExitStack() as ph:
            mixb = [sb(ph, f"mixb{i}", [P, 16, 512], BF16) for i in range(2)]
            mixb_r = [Res("mixb0"), Res("mixb1")]
            gtb = [sb(ph, f"gtb{i}", [P, 24, 512], BF16) for i in range(2)]
            gtb_r = [Res("gtb0"), Res("gtb1")]
            mg = sb(ph, "mg", [P, KC, 512], BF16)
            mg_r = Res("mg")
            macc = [sb(ph, f"macc{i}", [P, 512], BF16) for i in range(2)]
            macc_r = [Res("macc0"), Res("macc1")]
            mt = [sb(ph, f"mt{i}", [P, 512], BF16) for i in range(2)]
            mt_r = [Res("mt0"), Res("mt1")]
            xin = [sb(ph, f"m1_xin{i}", [P, D], F32) for i in range(2)]
            xin_r = [Res("m1xin0"), Res("m1xin1")]
            x1 = [sb(ph, f"m1_x1{i}", [P, D], F32) for i in range(2)]
            x1_r = [Res("m1x10"), Res("m1x11")]
            junk = sb(ph, "m1_junk", [P, D], BF16)
            junk_r = Res("m1junk")
            hb = [sb(ph, f"m1_hb{i}", [P, D], BF16) for i in range(2)]
            hb_r = [Res("m1hb0"), Res("m1hb1")]
            ss = sb(ph, "m1_ss", [P, 2 * NT], F32)
            ss_r = [Res(f"m1ss{n}") for n in range(NT)]
            h2s = [sb(ph, f"m1_h2s{i}", [P, KC, P], BF16) for i in range(2)]
            h2s_r = [Res("h2s0"), Res("h2s1")]
            def load_blk(blk):
                t0 = blk * 512
                sl = blk % 2
                for c4 in range(0, 16, 4):
                    S.dma("sp", mixb[sl][:, c4:c4 + 4, :],
                          mixT_d[c4 * P:(c4 + 4) * P, t0:t0 + 512].rearrange("(c p) t -> p c t", p=P),
                          mix_r, [mixb_r[sl]], f"mixb{sl}")
                for c4 in range(0, 24, 4):
                    S.dma("sp", gtb[sl][:, c4:c4 + 4, :],
                          gate_d[c4 * P:(c4 + 4) * P, t0:t0 + 512].rearrange("(c p) t -> p c t", p=P),
                          [gate_dr], [gtb_r[sl]], f"gtb{sl}")

            def load_x(n):
                S.dma("sp", xin[n % 2][:], x_d[n * P:(n + 1) * P, :], [], [xin_r[n % 2]], f"m1xin{n % 2}")

            def tile_stage2(n):
                i = n % 2
                pt, ptr = PS()
                ptb = pt[:].bitcast(BF16)
                for c in range(KC):
                    S.op("pe", lambda e, c=c: e.transpose(out=ptb[:, c * P:(c + 1) * P],
                                                          in_=hb[i][:, c * P:(c + 1) * P], identity=ident[:]),
                         [hb_r[i], const_r], [ptr], signal=(c == KC - 1))
                S.op("act", lambda e: e.activation(out=h2s[i][:], in_=ptb.rearrange("p (c t) -> p c t", c=KC),
                                                   func=AF.Copy), [ptr], [h2s_r[i]])
                for c4 in range(0, KC, 4):
                    S.dma("pool", h2T_d[c4 * P:(c4 + 4) * P, n * P:(n + 1) * P].rearrange("(c p) t -> p c t", p=P),
                          h2s[i][:, c4:c4 + 4, :], [h2s_r[i]], [h2_r[n]], f"h2s{i}")

            it = 0
            pending = None
            load_blk(0)
            load_x(0)
            if extra_loads is not None:
                extra_loads()
            for blk in range(L // 512):
                t0 = blk * 512
                sl = blk % 2
                if blk + 1 < L // 512:
                    load_blk(blk + 1)
                for oc in range(KC):
                    b = it % 2
                    it += 1
                    for br, (kc0, nk) in enumerate(((0, 8), (8, 4), (12, 4))):
                        pp, ppr = PS()
                        for k in range(nk):
                            S.op("pe", lambda e, k=k, kc0=kc0, pp=pp: e.matmul(
                                pp[:, :], lhsT=wp[:, kc0 + k, oc * P:(oc + 1) * P], rhs=mixb[sl][:, kc0 + k, :],
                                start=(k == 0), stop=(k == nk - 1)), [wp_r, mixb_r[sl]], [ppr], signal=(k == nk - 1))
                        gate = gtb[sl][:, br * 8 + oc, :]
                        if br == 0:
                            S.op("dve", lambda e, pp=pp: e.tensor_tensor(out=macc[b][:], in0=pp[:, :], in1=gate, op=ALU.mult),
                                 [ppr, gtb_r[sl]], [macc_r[b]])
                        else:
                            S.op("dve", lambda e, pp=pp: e.tensor_tensor(out=mt[b][:], in0=pp[:, :], in1=gate, op=ALU.mult),
                                 [ppr, gtb_r[sl]], [mt_r[b]])
                            if br == 1:
                                S.op("dve", lambda e: e.tensor_tensor(out=macc[b][:], in0=macc[b][:], in1=mt[b][:],
                                                                      op=ALU.add), [macc_r[b], mt_r[b]], [macc_r[b]])
                            else:
                                S.op("dve", lambda e: e.tensor_tensor(out=mg[:, oc, :], in0=macc[b][:], in1=mt[b][:],
                                                                      op=ALU.add), [macc_r[b], mt_r[b]], [mg_r])
                    if oc == 3 and pending is not None:
                        tile_stage2(pending)
                        pending = None
                for tt in range(4):
                    n = blk * 4 + tt
                    i = n % 2
                    for half in range(2):
                        py, pyr = PS()
                        for k in range(KC):
                            S.op("pe", lambda e, k=k, py=py: e.matmul(
                                py[:, :], lhsT=mg[:, k, tt * P:(tt + 1) * P], rhs=wo[:, k, half * 512:(half + 1) * 512],
                                start=(k == 0), stop=(k == KC - 1)), [mg_r, wo_r], [pyr], signal=(k == KC - 1))
                        S.op("dve", lambda e, py=py: e.tensor_tensor(
                            out=x1[i][:, half * 512:(half + 1) * 512], in0=py[:, :], in1=xin[i][:, half * 512:(half + 1) * 512],
                            op=ALU.add), [pyr, xin_r[i]], [x1_r[i]])
                    if n + 1 < NT:
                        load_x(n + 1)
                    S.dma("pool", out_d[n * P:(n + 1) * P, :], x1[i][:], [x1_r[i]], [out_r[n]], f"m1x1{i}")
                    S.op("act", lambda e: e.activation(out=junk[:], in_=x1[i][:], func=AF.Square,
                                                       accum_out=ss[:, 2 * n:2 * n + 1]), [x1_r[i]], [junk_r, ss_r[n]])
                    S.op("act", lambda e: e.activation(out=ss[:, 2 * n + 1:2 * n + 2], in_=ss[:, 2 * n:2 * n + 1],
                                                       func=AF.Sqrt, bias=float(EPS), scale=1.0 / D), [ss_r[n]], [ss_r[n]])
                    S.op("dve", lambda e: e.reciprocal(out=ss[:, 2 * n + 1:2 * n + 2], in_=ss[:, 2 * n + 1:2 * n + 2]),
                         [ss_r[n]], [ss_r[n]])
                    S.op("dve", lambda e: e.tensor_scalar(out=hb[i][:], in0=x1[i][:], scalar1=ss[:, 2 * n + 1:2 * n + 2],
                                                          scalar2=None, op0=ALU.mult), [x1_r[i], ss_r[n]], [hb_r[i]])
                    if pending is not None:
                        tile_stage2(pending)
                    pending = n
            tile_stage2(pending)
            S.barrier()

    def phase_m2(wfi_pre, wfi_rs):
        NF = DFF // P
        with ExitStack() as ph:
            for x_ in range(2):
                wst.append(sb(ph, f"wstx_m2{x_}", [P, WCAP], F32))
                wst_r.append(Res(f"wstx_m2{x_}"))
            wfi_c = dict(wfi_pre)
            for ci in (2, 3, 1, 4, 5):
                wfi_c[ci] = sb(ph, f"wfi_c{ci}", [P, KC, min(1024, 2 * DFF - ci * 1024)], BF16)

            def wfcol(col):
                return wfi_c[col // 1024], col % 1024
            wfo = sb(ph, "wfo", [P, NF, D], BF16)
            wfo_rs = [Res(f"wfo{i}") for i in range(NF // 2)]
            h2b = [sb(ph, f"h2b{i}", [P, KC, 512], BF16) for i in range(2)]
            h2b_r = [Res("h2b0"), Res("h2b1")]
            uT = sb(ph, "uT", [P, NF, 512], BF16)
            uT_r = Res("uT")
            sa = [sb(ph, f"sa{i}", [P, 512], BF16) for i in range(2)]
            sa_r = [Res("sa0"), Res("sa1")]
            x1in = [sb(ph, f"m2_x1{i}", [P, D], F32) for i in range(2)]
            x1in_r = [Res("m2x10"), Res("m2x11")]
            def load_h2(blk):
                t0 = blk * 512
                sl = blk % 2
                for c4 in range(0, KC, 4):
                    S.dma("sp", h2b[sl][:, c4:c4 + 4, :],
                          h2T_d[c4 * P:(c4 + 4) * P, t0:t0 + 512].rearrange("(c p) t -> p c t", p=P),
                          h2_r[blk * 4:blk * 4 + 4], [h2b_r[sl]], f"h2b{sl}")

            def load_x1(n):
                S.dma("sp", x1in[n % 2][:], out_d[n * P:(n + 1) * P, :], [out_r[n]], [x1in_r[n % 2]], f"m2x1{n % 2}")

            it = 0
            load_h2(0)
            load_x1(0)
            for ci in (2, 3, 1, 4, 5):
                wload_big(wfi_c[ci], wfi_rs[ci], w_fin_d, ci * 1024, min(1024, 2 * DFF - ci * 1024), KC, gain_idx=2)
            for k2 in range(NF // 2):
                for k1_ in range(2):
                    wload(wfo, wfo_rs[k2], w_fout_d, 0, D, 2 * k2 + k1_, 1, None, dst_kc0=2 * k2 + k1_, eng="dve")
            for blk in range(L // 512):
                t0 = blk * 512
                sl = blk % 2
                if blk + 1 < L // 512:
                    load_h2(blk + 1)
                for fc in range(NF):
                    b = it % 2
                    it += 1
                    pa, par = PS()
                    pb, pbr = PS()
                    for c in range(KC):
                        wa_, ca_ = wfcol(fc * P)
                        S.op("pe", lambda e, c=c, pa=pa: e.matmul(pa[:, :], lhsT=wa_[:, c, ca_:ca_ + P],
                                                                  rhs=h2b[sl][:, c, :], start=(c == 0), stop=(c == KC - 1)),
                             [wfi_rs[(fc * P) // 1024], h2b_r[sl]], [par], signal=(c == KC - 1))
                    for c in range(KC):
                        wb_, cb_ = wfcol(DFF + fc * P)
                        S.op("pe", lambda e, c=c, pb=pb: e.matmul(pb[:, :], lhsT=wb_[:, c, cb_:cb_ + P],
                                                                  rhs=h2b[sl][:, c, :], start=(c == 0), stop=(c == KC - 1)),
                             [wfi_rs[(DFF + fc * P) // 1024], h2b_r[sl]], [pbr], signal=(c == KC - 1))
                    S.op("act", lambda e, pa=pa: e.activation(out=sa[b][:], in_=pa[:, :], func=AF.Silu), [par], [sa_r[b]])
                    S.op("dve", lambda e, pb=pb: e.tensor_tensor(out=uT[:, fc, :], in0=pb[:, :], in1=sa[b][:], op=ALU.mult),
                         [pbr, sa_r[b]], [uT_r])
                for tt in range(4):
                    n = blk * 4 + tt
                    i = n % 2
                    for half in range(2):
                        py, pyr = PS()
                        for k in range(NF):
                            S.op("pe", lambda e, k=k, py=py: e.matmul(
                                py[:, :], lhsT=uT[:, k, tt * P:(tt + 1) * P], rhs=wfo[:, k, half * 512:(half + 1) * 512],
                                start=(k == 0), stop=(k == NF - 1)), [uT_r, wfo_rs[k // 2]], [pyr], signal=(k == NF - 1))
                        S.op("dve", lambda e, py=py: e.tensor_tensor(
                            out=x1in[i][:, half * 512:(half + 1) * 512], in0=py[:, :],
                            in1=x1in[i][:, half * 512:(half + 1) * 512], op=ALU.add), [pyr, x1in_r[i]], [x1in_r[i]])
                    if n + 1 < NT:
                        load_x1(n + 1)
                    S.dma("pool", out_d[n * P:(n + 1) * P, :], x1in[i][:], [x1in_r[i]], [out_r[n]], f"m2o{i}")
            S.barrier()
            del wst[NWST:]
            del wst_r[NWST:]

    if "mem" in phases:
        phase_mem()
    memw.close()
    hgw = ExitStack()
    hg_heads = list(range(8) if not debug_heads else debug_heads)
    wts_g = [[sb(hgw, f"w_hg{k}_{i}", [P, KC, P], BF16) for k in range(5)] for i in range(2)]
    wts_g_r = [[Res(f"w_hg{k}_{i}") for k in range(5)] for i in range(2)]

    def hg_first_weights():
        for k in range(5):
            wload(wts_g[0][k], wts_g_r[0][k], w_in_d, OFF_HG + k * 1024 + hg_heads[0] * P, P, 0, KC, gain_idx=0)

    pre_hg = hg_first_weights if ("hg" in phases and "da" in phases) else None
    if "da" in phases:
        phase_da(range(4) if not debug_slots else debug_slots, pre_hg)
    if "hg" in phases:
        phase_hg(hg_heads, wts_g, wts_g_r, first_loaded=(pre_hg is not None))
    hgw.close()
    if "tail" in phases:
        m2pre = ExitStack()
        wfi_pre = {ci: sb(m2pre, f"wfi_c{ci}", [P, KC, 1024], BF16) for ci in (0,)}
        wfi_rs = [Res(f"wfi{i}") for i in range(6)]

        def m2_preloads():
            for ci in (0,):
                wload_big(wfi_pre[ci], wfi_rs[ci], w_fin_d, ci * 1024, 1024, KC, gain_idx=2)

        m1w = ExitStack()
        wp = sb(m1w, "wp", [P, 16, D], BF16)
        wp_r = Res("wp")
        wo = sb(m1w, "wo", [P, KC, D], BF16)
        wo_r = Res("wo")

        def m1_loads():
            wload_big(wp, wp_r, w_phg_d, 0, D, 8, eng="dve")
            wload_big(wp, wp_r, w_pda_d, 0, D, 4, dst_kc0=8, eng="dve")
            wload_big(wp, wp_r, w_pmem_d, 0, D, 4, dst_kc0=12, eng="dve")
            wload_big(wo, wo_r, w_out_d, 0, D, KC, eng="dve")

        phase_gates(m1_loads)
    S.barrier()
    hstack.close()
    if "tail" in phases:
        phase_m1(wp, wp_r, wo, wo_r, m2_preloads)
        m1w.close()
        phase_m2(wfi_pre, wfi_rs)
        m2pre.close()

    S.barrier()
    es.close()
    print("kernel build: ops", S.nops, "sems", S.nsem)
    return nc


_CACHE = {}


def _consts():
    ident = np.eye(P, dtype=np.float32)
    s = np.arange(P)[:, None]
    t = np.arange(P)[None, :]
    cmask = np.stack([(s <= t), (s >= t)], axis=1).astype(np.float32)
    damask = np.zeros((P, 12, 2 * P), np.float64)
    b = np.arange(P)[:, None].astype(np.float64)
    a = np.arange(P)[None, :].astype(np.float64)
    for h in range(12):
        d = DA_CFG[h // 4][1]
        relA = b - a - 64.0
        relB = b - a + 64.0
        damask[:, h, 0:P] = (b >= a) * np.exp(-SLOPES[h] * d * np.abs(relA))
        damask[:, h, P:2 * P] = (b <= a) * np.exp(-SLOPES[h] * d * np.abs(relB))
    rmask = np.ones((P, 1024), np.float32)
    rmask[:, ::P] = 0.0
    return ident, cmask, damask.astype(np.float32), rmask


def make_in_maps(inputs):
    ident, cmask, damask, rmask = _consts()
    f = lambda k: np.ascontiguousarray(np.asarray(inputs[k], dtype=np.float32))
    gains = np.stack([f("norm_mix_gain")[0], f("norm_mem_gain")[0], f("norm_ffn_gain")[0]], 0)
    gainsT = np.ascontiguousarray(gains.reshape(3, KC, P).transpose(2, 0, 1))
    hgains = np.ascontiguousarray(np.stack([f("hg_norm_gain")[0], f("da_q_gain")[0], f("da_k_gain")[0],
                                            f("mem_q_gain")[0], f("mem_k_gain")[0]], 1))
    lb = np.concatenate([f("lb_logits_fw"), f("lb_logits_bw")], 0)
    lbl = np.ascontiguousarray(lb.reshape(4, 8, P).transpose(2, 0, 1))
    shared = {
        "w_in": f("w_in")[0], "w_mem_kv": f("w_mem_kv")[0], "w_proj_hg": f("w_proj_hg")[0],
        "w_proj_da": f("w_proj_da")[0], "w_proj_mem": f("w_proj_mem")[0], "w_out": f("w_out")[0],
        "w_ffn_in": f("w_ffn_in")[0], "w_ffn_out": f("w_ffn_out")[0],
        "gainsT": gainsT, "hgains": hgains, "lbl": lbl, "cmask": cmask, "damask": damask,
        "rmask": rmask, "ident": ident,
    }
    x = f("x")
    mem = f("mem")
    return [dict(shared, x=x[b], mem=mem[b]) for b in range(8)]


def kernel(**inputs):
    if "nc" not in _CACHE:
        _CACHE["nc"] = build()
    nc = _CACHE["nc"]
    in_maps = make_in_maps(inputs)
    res = run_bass_kernel_spmd(nc, in_maps, core_ids=list(range(8)))
    return np.stack([np.asarray(r["out"], dtype=np.float32) for r in res.results], 0)
```
